# Optimizing a Trainium2 kernel written in Bass

```python
import jax, jax.numpy as jnp
from jax import lax
import numpy as np

D_MODEL = 2048
BATCH = 4
SEQ = 4096
DEPTH = 4
DEC_BATCH = 16
DEC_SEQ = 64
PAST_LEN = 1024

CHUNK = 64
N_A_LAYERS = DEPTH // 2
N_B_LAYERS = DEPTH - N_A_LAYERS
GLA_HEADS = 4
GLA_DK = D_MODEL // 2 // GLA_HEADS
GLA_DV = D_MODEL // GLA_HEADS
GLA_RANK = 16
GLA_TAU = 16.0
GLA_QK = GLA_HEADS * GLA_DK
GLA_VR = GLA_HEADS * GLA_DV
GLA_IN = 2 * GLA_QK + 2 * GLA_VR + GLA_RANK
ATT_HEADS = 16
ATT_KV_HEADS = 4
ATT_DH = D_MODEL // ATT_HEADS
ATT_GROUP = ATT_HEADS // ATT_KV_HEADS
LEFT_CHUNKS = 8
ATT_WINDOW = LEFT_CHUNKS * CHUNK
ATT_BAND = ATT_WINDOW + CHUNK
MAX_REL = 256
N_REL = 2 * MAX_REL + 1
D_FF = 5632
CONV_W = 3
EPS = 1e-6

kernel_name = "yoco_gla_chunked_relbias_convffn_step"


def _rms_norm(x, g):
    xf = x.astype(jnp.float32)
    y = xf * lax.rsqrt(jnp.mean(xf * xf, axis=-1, keepdims=True) + EPS)
    return (y * g.astype(jnp.float32)).astype(x.dtype)


def _gla_scan(q, k, v, gk, s0):
    B, T, H, DK = q.shape
    DV = v.shape[-1]
    n = -(-T // CHUNK)
    pad = n * CHUNK - T

    def blocks(a):
        a = jnp.pad(a.astype(jnp.float32), ((0, 0), (0, pad), (0, 0), (0, 0)))
        return jnp.moveaxis(a.reshape(B, n, CHUNK, H, a.shape[-1]), 1, 0)

    tri = jnp.tril(jnp.ones((CHUNK, CHUNK), bool))[None, :, :, None, None]

    def step(S, blk):
        qc, kc, vc, gc = blk
        b = jnp.cumsum(gc, axis=1)
        decay = jnp.exp(jnp.where(tri, b[:, :, None] - b[:, None, :], -jnp.inf))
        att = jnp.einsum('bthk,bshk,btshk->bhts', qc, kc, decay)
        o = (jnp.einsum('bhts,bshv->bthv', att, vc)
             + jnp.einsum('bthk,bhkv->bthv', qc * jnp.exp(b), S))
        b_end = b[:, -1]
        S = (jnp.exp(b_end)[..., None] * S
             + jnp.einsum('bshk,bshv->bhkv', kc * jnp.exp(b_end[:, None] - b), vc))
        return S, o

    S, o = lax.scan(step, s0.astype(jnp.float32), (blocks(q), blocks(k), blocks(v), blocks(gk)))
    o = jnp.moveaxis(o, 0, 1).reshape(B, n * CHUNK, H, DV)[:, :T]
    return o.astype(v.dtype), S.astype(s0.dtype)


def _gla_mixer(h, s0, w_in, w_gate, b_gate, head_norm, w_o):
    B, T, _ = h.shape
    proj = h @ w_in
    q, k, v, r, g_low = jnp.split(
        proj, [GLA_QK, 2 * GLA_QK, 2 * GLA_QK + GLA_VR, 2 * GLA_QK + 2 * GLA_VR], axis=-1)
    q = q.reshape(B, T, GLA_HEADS, GLA_DK) * (GLA_DK ** -0.5)
    k = k.reshape(B, T, GLA_HEADS, GLA_DK)
    v = v.reshape(B, T, GLA_HEADS, GLA_DV)
    gk = jax.nn.log_sigmoid((g_low @ w_gate + b_gate).astype(jnp.float32)) / GLA_TAU
    gk = gk.reshape(B, T, GLA_HEADS, GLA_DK)
    o, s_new = _gla_scan(q, k, v, gk, s0)
    o = _rms_norm(o, head_norm).reshape(B, T, GLA_VR) * jax.nn.silu(r)
    return o @ w_o, s_new


def _attend(q, k, v, q_pos, k_pos, k_valid, rel_bias):
    B, Tq = q.shape[:2]
    Tk = k.shape[1]
    qg = q.reshape(B, Tq, ATT_KV_HEADS, ATT_GROUP, ATT_DH)
    s = jnp.einsum('bqngd,bknd->bngqk', qg, k).astype(jnp.float32) * (ATT_DH ** -0.5)
    rel = jnp.clip(q_pos[:, None] - k_pos[None, :], -MAX_REL, MAX_REL) + MAX_REL
    bias = rel_bias[:, rel].astype(jnp.float32).reshape(ATT_KV_HEADS, ATT_GROUP, Tq, Tk)
    s = jnp.where(k_valid[None, None, None, None, :], s + bias[None], -jnp.inf)
    p = jax.nn.softmax(s, axis=-1)
    o = jnp.einsum('bngqk,bknd->bqngd', p.astype(v.dtype), v)
    return o.reshape(B, Tq, ATT_HEADS, ATT_DH)


def _band_attention_prompt(q, k, v, rel_bias):
    B, S = q.shape[:2]
    n = S // CHUNK
    kp = jnp.pad(k, ((0, 0), (ATT_WINDOW, 0), (0, 0), (0, 0)))
    vp = jnp.pad(v, ((0, 0), (ATT_WINDOW, 0), (0, 0), (0, 0)))
    qb = jnp.moveaxis(q.reshape(B, n, CHUNK, ATT_HEADS, ATT_DH), 1, 0)

    def one(args):
        c, qc = args
        start = c * CHUNK
        kb = lax.dynamic_slice_in_dim(kp, start, ATT_BAND, axis=1)
        vb = lax.dynamic_slice_in_dim(vp, start, ATT_BAND, axis=1)
        q_pos = start + jnp.arange(CHUNK)
        k_pos = start - ATT_WINDOW + jnp.arange(ATT_BAND)
        return _attend(qc, kb, vb, q_pos, k_pos, k_pos >= 0, rel_bias)

    o = lax.map(one, (jnp.arange(n), qb))
    return jnp.moveaxis(o, 0, 1).reshape(B, S, ATT_HEADS, ATT_DH)


def _band_attention_step(q, k_new, v_new, k_cache, v_cache, rel_bias):
    T = q.shape[1]
    W = k_cache.shape[1]
    k = jnp.concatenate([k_cache.astype(k_new.dtype), k_new], axis=1)
    v = jnp.concatenate([v_cache.astype(v_new.dtype), v_new], axis=1)
    q_pos = PAST_LEN + jnp.arange(T)
    k_pos = jnp.concatenate([PAST_LEN - W + jnp.arange(W), PAST_LEN + jnp.arange(T)])
    return _attend(q, k, v, q_pos, k_pos, k_pos >= 0, rel_bias)


def _conv_ffn(h, prev, w_up, conv_w, conv_b, w_down):
    T = h.shape[1]
    gate, val = jnp.split(h @ w_up, 2, axis=-1)
    ext = jnp.concatenate([prev.astype(gate.dtype), gate], axis=1)
    conv = conv_b
    for j in range(CONV_W):
        conv = conv + conv_w[j] * ext[:, j:j + T]
    return (jax.nn.silu(conv) * val) @ w_down, ext[:, T:]


def _trunk(x, gla_s0, conv_s0, k_cache, v_cache, norm_gains, gla_w_in, gla_w_gate,
           gla_b_gate, gla_head_norm, gla_w_o, kv_norm, att_w_kv, att_w_q, att_rel_bias,
           att_w_o, ffn_w_up, ffn_conv_w, ffn_conv_b, ffn_w_down):
    prompt = k_cache is None
    B, T, _ = x.shape
    gla_new, conv_new = [], []
    k_sh = v_sh = None
    for l in range(DEPTH):
        g = norm_gains[l]
        if l == N_A_LAYERS:
            kv = _rms_norm(x, kv_norm) @ att_w_kv
            k_sh, v_sh = jnp.split(kv, 2, axis=-1)
            k_sh = k_sh.reshape(B, T, ATT_KV_HEADS, ATT_DH)
            v_sh = v_sh.reshape(B, T, ATT_KV_HEADS, ATT_DH)
        h = _rms_norm(x, g[0])
        if l < N_A_LAYERS:
            m, s = _gla_mixer(h, gla_s0[l], gla_w_in[l], gla_w_gate[l], gla_b_gate[l],
                              gla_head_norm[l], gla_w_o[l])
            gla_new.append(s)
        else:
            j = l - N_A_LAYERS
            q = (h @ att_w_q[j]).reshape(B, T, ATT_HEADS, ATT_DH)
            if prompt:
                o = _band_attention_prompt(q, k_sh, v_sh, att_rel_bias[j])
            else:
                o = _band_attention_step(q, k_sh, v_sh, k_cache, v_cache, att_rel_bias[j])
            m = o.reshape(B, T, ATT_HEADS * ATT_DH) @ att_w_o[j]
        x = x + _rms_norm(m, g[1])
        f, c = _conv_ffn(_rms_norm(x, g[2]), conv_s0[l], ffn_w_up[l], ffn_conv_w[l],
                         ffn_conv_b[l], ffn_w_down[l])
        conv_new.append(c)
        x = x + _rms_norm(f, g[3])
    if prompt:
        keep = min(ATT_WINDOW, T)
        k_rows, v_rows = k_sh[:, T - keep:], v_sh[:, T - keep:]
    else:
        k_rows, v_rows = k_sh, v_sh
    return x, jnp.stack(gla_new), jnp.stack(conv_new), k_rows, v_rows


def setup_inputs(seed: int = 0) -> dict:
    key = jax.random.key(seed)
    ks = jax.random.split(key, 24)
    f32 = jnp.float32

    def nrm(k, shape, scale):
        return jax.random.normal(k, shape, f32) * scale

    kv_win = min(ATT_WINDOW, PAST_LEN)
    return {
        "x_prompt": nrm(ks[0], (BATCH, SEQ, D_MODEL), 1.0),
        "x_sample": nrm(ks[1], (DEC_BATCH, DEC_SEQ, D_MODEL), 1.0),
        "state_gla": nrm(ks[2], (N_A_LAYERS, DEC_BATCH, GLA_HEADS, GLA_DK, GLA_DV), 1.0),
        "state_ffn_conv": nrm(ks[3], (DEPTH, DEC_BATCH, CONV_W - 1, D_FF), 1.0),
        "cache_k": nrm(ks[4], (DEC_BATCH, kv_win, ATT_KV_HEADS, ATT_DH), 1.0),
        "cache_v": nrm(ks[5], (DEC_BATCH, kv_win, ATT_KV_HEADS, ATT_DH), 1.0),
        "norm_gains": 1.0 + nrm(ks[6], (DEPTH, 4, D_MODEL), 0.05),
        "gla_w_in": nrm(ks[7], (N_A_LAYERS, D_MODEL, GLA_IN), D_MODEL ** -0.5),
        "gla_w_gate": nrm(ks[8], (N_A_LAYERS, GLA_RANK, GLA_QK), GLA_RANK ** -0.5),
        "gla_b_gate": nrm(ks[9], (N_A_LAYERS, GLA_QK), 0.1),
        "gla_head_norm": 1.0 + nrm(ks[10], (N_A_LAYERS, GLA_DV), 0.05),
        "gla_w_o": nrm(ks[11], (N_A_LAYERS, GLA_VR, D_MODEL), GLA_VR ** -0.5),
        "kv_norm": 1.0 + nrm(ks[12], (D_MODEL,), 0.05),
        "att_w_kv": nrm(ks[13], (D_MODEL, 2 * ATT_KV_HEADS * ATT_DH), D_MODEL ** -0.5),
        "att_w_q": nrm(ks[14], (N_B_LAYERS, D_MODEL, ATT_HEADS * ATT_DH), D_MODEL ** -0.5),
        "att_rel_bias": nrm(ks[15], (N_B_LAYERS, ATT_HEADS, N_REL), 0.1),
        "att_w_o": nrm(ks[16], (N_B_LAYERS, ATT_HEADS * ATT_DH, D_MODEL), (ATT_HEADS * ATT_DH) ** -0.5),
        "ffn_w_up": nrm(ks[17], (DEPTH, D_MODEL, 2 * D_FF), D_MODEL ** -0.5),
        "ffn_conv_w": nrm(ks[18], (DEPTH, CONV_W, D_FF), CONV_W ** -0.5),
        "ffn_conv_b": nrm(ks[19], (DEPTH, D_FF), 0.02),
        "ffn_w_down": nrm(ks[20], (DEPTH, D_FF, D_MODEL), D_FF ** -0.5),
    }


def reference(x_prompt, x_sample, state_gla, state_ffn_conv, cache_k, cache_v, norm_gains,
              gla_w_in, gla_w_gate, gla_b_gate, gla_head_norm, gla_w_o, kv_norm, att_w_kv,
              att_w_q, att_rel_bias, att_w_o, ffn_w_up, ffn_conv_w, ffn_conv_b, ffn_w_down):
    B = x_prompt.shape[0]
    gla_zero = jnp.zeros((N_A_LAYERS, B, GLA_HEADS, GLA_DK, GLA_DV), x_prompt.dtype)
    conv_zero = jnp.zeros((DEPTH, B, CONV_W - 1, D_FF), x_prompt.dtype)
    y_prompt, gla_p, conv_p, k_p, v_p = _trunk(
        x_prompt, gla_zero, conv_zero, None, None, norm_gains, gla_w_in, gla_w_gate,
        gla_b_gate, gla_head_norm, gla_w_o, kv_norm, att_w_kv, att_w_q, att_rel_bias,
        att_w_o, ffn_w_up, ffn_conv_w, ffn_conv_b, ffn_w_down)
    y_sample, gla_s, conv_s, k_s, v_s = _trunk(
        x_sample, state_gla, state_ffn_conv, cache_k, cache_v, norm_gains, gla_w_in,
        gla_w_gate, gla_b_gate, gla_head_norm, gla_w_o, kv_norm, att_w_kv, att_w_q,
        att_rel_bias, att_w_o, ffn_w_up, ffn_conv_w, ffn_conv_b, ffn_w_down)
    return (y_prompt, y_sample, gla_p, gla_s, conv_p, conv_s, k_p, v_p, k_s, v_s)
```

```python
import numpy as np
from contextlib import ExitStack
import concourse.bass as bass
import concourse.mybir as mybir
from concourse.bass_utils import run_bass_kernel_spmd

F32 = mybir.dt.float32
BF16 = mybir.dt.bfloat16
AF = mybir.ActivationFunctionType
ALU = mybir.AluOpType
AX = mybir.AxisListType

D = 2048
KC = 16
DFF = 5632
FC = 44
GLA_IN = 6160
EPS = 1e-6
N_CORES = 8
SEQ = 4096
NPH = 2048
EPOCH = 16000


class Buf:
    __slots__ = ("w", "r", "name")

    def __init__(self, name=""):
        self.w = None
        self.r = {}
        self.name = name


class KB:
    def __init__(self, nc, es):
        self.nc = nc
        self.es = es
        self.eng = dict(pe=nc.tensor, act=nc.scalar, dve=nc.vector, pool=nc.gpsimd, sp=nc.sync)
        self.sems = {}
        self.cnt = {}
        self.epoch = {}
        for e in ("pe", "act", "dve", "pool"):
            self.epoch[e] = 0
            self._newsem(self._ekey(e))
        self.ndma = 24
        self.dkeys = []
        for j in range(self.ndma):
            k = "d%d_0" % j
            self._newsem(k)
            self.dkeys.append(k)
        self.dma_rr = 0
        self.waited = {e: {} for e in self.eng}
        self.out_events = []
        self.fence_evs = []

    def _ekey(self, e):
        return "%s_%d" % (e, self.epoch[e])

    def _newsem(self, key):
        self.sems[key] = self.es.enter_context(self.nc.semaphore("s_" + key))
        self.cnt[key] = 0

    def fence(self):
        self.fence_evs = [(k, c) for k, c in self.cnt.items() if c > 0]

    def _wait(self, e, evs):
        best = {}
        for (k, v) in list(evs) + self.fence_evs:
            if v > best.get(k, 0):
                best[k] = v
        for k, v in best.items():
            if e == "pe" and k.startswith("pe_"):
                continue
            if self.waited[e].get(k, 0) >= v:
                continue
            if e != "sp" and k == self._ekey(e) and v > self.cnt[k]:
                continue
            self.eng[e].wait_ge(self.sems[k], v)
            self.waited[e][k] = v

    @staticmethod
    def _deps(reads, writes):
        evs = []
        for b in reads:
            if b.w is not None:
                evs.append(b.w)
        for b in writes:
            if b.w is not None:
                evs.append(b.w)
            evs.extend(b.r.items())
        return evs

    @staticmethod
    def _upd(ev, reads, writes):
        for b in reads:
            if ev[1] > b.r.get(ev[0], 0):
                b.r[ev[0]] = ev[1]
        for b in writes:
            b.w = ev
            b.r = {}

    def op(self, e, fn, reads=(), writes=(), mark=True):
        self._wait(e, self._deps(reads, writes))
        ins = fn()
        key = self._ekey(e)
        ev = (key, self.cnt[key] + 1)
        if mark:
            ins.then_inc(self.sems[key], 1)
            self.cnt[key] += 1
            if self.cnt[key] >= EPOCH:
                self.epoch[e] += 1
                self._newsem(self._ekey(e))
        self._upd(ev, reads, writes)
        return ins

    def dma(self, out_ap, in_ap, reads=(), writes=(), q="sp", is_out=False, slow=False):
        j = self.dma_rr
        self.dma_rr = (j + 1) % self.ndma
        key = self.dkeys[j]
        if self.cnt[key] >= EPOCH:
            nk = "d%d_%d" % (j, int(key.split("_")[1]) + 1)
            self._wait(q, [(key, self.cnt[key])])
            self._newsem(nk)
            self.dkeys[j] = nk
            key = nk
        evs = self._deps(reads, writes)
        if self.cnt[key] > 0:
            evs.append((key, self.cnt[key]))
        self._wait(q, evs)
        if slow:
            self.eng[q].dma_start(out=out_ap, in_=in_ap, allow_slow_non_contiguous=True).then_inc(self.sems[key], 16)
        else:
            self.eng[q].dma_start(out=out_ap, in_=in_ap).then_inc(self.sems[key], 16)
        self.cnt[key] += 16
        ev = (key, self.cnt[key])
        self._upd(ev, reads, writes)
        if is_out:
            self.out_events.append(ev)
        return ev

    def finish(self):
        evs = list(self.out_events)
        for k in self.dkeys:
            if self.cnt[k] > 0:
                evs.append((k, self.cnt[k]))
        self._wait("sp", evs)


def build_program(n_layers=4, np_tok=NPH, n_st=2, dbg=None):
    nc = bass.Bass("TRN2", target_bir_lowering=False)
    NPT = np_tok * n_st
    XR = NPT + 128
    dt = nc.dram_tensor

    def din(name, shape, dtype=F32):
        return dt(name, list(shape), dtype, kind="ExternalInput").ap()

    def dout(name, shape, dtype=F32):
        return dt(name, list(shape), dtype, kind="ExternalOutput").ap()

    def dscr(name, shape, dtype):
        return dt(name, list(shape), dtype, kind="Internal").ap()

    xin = din("xin", [XR, D])
    gla_s0 = din("gla_s0", [2, 2, 4, 256, 512])
    conv_s0 = din("conv_s0", [4, 2, 2, DFF])
    ck = din("ck", [2, 512, 512])
    cv = din("cv", [2, 512, 512])
    norm_gains = din("norm_gains", [4, 4, D])
    gla_w_in = din("gla_w_in", [2, D, GLA_IN])
    gla_w_gate = din("gla_w_gate", [2, 16, 1024])
    gla_b_gate = din("gla_b_gate", [2, 1024])
    gla_head_norm = din("gla_head_norm", [2, 512])
    gla_w_o = din("gla_w_o", [2, D, D])
    kv_norm = din("kv_norm", [D])
    att_w_kv = din("att_w_kv", [D, 1024])
    att_w_q = din("att_w_q", [2, D, D])
    att_rel_bias = din("att_rel_bias", [2, 16, 513])
    att_w_o = din("att_w_o", [2, D, D])
    ffn_w_up = din("ffn_w_up", [4, D, 2 * DFF])
    ffn_conv_w = din("ffn_conv_w", [4, 3, DFF])
    ffn_conv_b = din("ffn_conv_b", [4, DFF])
    ffn_w_down = din("ffn_w_down", [4, DFF, D])
    yout = dout("yout", [XR, D])
    o_gla_p = dout("o_gla_p", [2, 4, 256, 512])
    o_gla_s = dout("o_gla_s", [2, 2, 4, 256, 512])
    o_conv_p = dout("o_conv_p", [4, 2, DFF])
    o_conv_s = dout("o_conv_s", [4, 2, 2, DFF])
    o_k_p = dout("o_k_p", [512, 512])
    o_v_p = dout("o_v_p", [512, 512])
    o_k_s = dout("o_k_s", [128, 512])
    o_v_s = dout("o_v_s", [128, 512])
    TMAX = np_tok + 128
    Xs = dscr("Xs", [TMAX, D], F32)
    Ms = dscr("Ms", [TMAX, D], F32)
    PROJ = dscr("PROJ", [TMAX, 6144], BF16)
    GLs = dscr("GLs", [16, TMAX], F32)
    ATs = dscr("ATs", [FC, 128, TMAX], BF16)
    CARRY_S = dscr("CARRY_S", [2, 128, 8, 512], F32)
    NPC = NPT // 64
    VSH = dscr("VSH", [(8 + NPC + 18) * 64, 512], BF16)
    KTs = dscr("KTs", [4, 128, 512 + NPT], BF16)
    KTss = dscr("KTss", [2, 4, 128, 576], BF16)
    QTs = dscr("QTs", [8, 128, 2 * TMAX], BF16)
    EREL = dscr("EREL", [16, 640], F32)
    ZREL = dscr("ZREL", [16, 41024], F32)
    dbg_out = None
    DL = 0
    if dbg is not None:
        if ':' in dbg[0]:
            DL = int(dbg[0].split(':')[1])
            dbg = (dbg[0].split(':')[0],) + tuple(dbg[1:])
        dbg_out = dout("dbg", dbg[1], dbg[2])

    es = ExitStack()
    with es:
        kb = KB(nc, es)
        op, dma = kb.op, kb.dma
        PE, ACT, DVE, POOL = nc.tensor, nc.scalar, nc.vector, nc.gpsimd

        def sb(name, shape, dtype=F32):
            return es.enter_context(nc.sbuf_tensor(name, list(shape), dtype))

        psb = [es.enter_context(nc.psum_tensor("ps%d" % i, [128, 512], F32)) for i in range(8)]
        PB = [Buf("ps%d" % i) for i in range(8)]

        def psbf(i):
            return psb[i][:].bitcast(BF16)

        identf = sb("identf", [128, 128], F32)
        ident = sb("ident", [128, 128], BF16)
        c_one = sb("c_one", [128, 1], F32)
        c_eps = sb("c_eps", [128, 1], F32)
        c_ln16 = sb("c_ln16", [128, 1], F32)
        triI = sb("triI", [64, 64], F32)
        triU = sb("triU", [64, 64], F32)
        maskT = sb("maskT", [64, 4, 64], F32)
        B_const = Buf("const")

        def cst(fn):
            op("pool", fn, writes=[B_const])

        cst(lambda: POOL.memset(identf[:], 0.0))
        cst(lambda: POOL.affine_select(out=identf[:], in_=identf[:], pattern=[[-1, 128]], compare_op=ALU.not_equal,
                                       fill=1.0, base=0, channel_multiplier=1))
        cst(lambda: POOL.tensor_copy(out=ident[:], in_=identf[:]))
        cst(lambda: POOL.memset(c_one[:], 1.0))
        cst(lambda: POOL.memset(c_eps[:], EPS))
        cst(lambda: POOL.memset(c_ln16[:], float(np.log(1.0 / 16.0))))
        cst(lambda: POOL.memset(triI[:], -1.0 / 16.0))
        cst(lambda: POOL.affine_select(out=triI[:], in_=triI[:], pattern=[[1, 64]], compare_op=ALU.is_ge, fill=0.0,
                                       base=0, channel_multiplier=-1))
        cst(lambda: POOL.memset(triU[:], -1.0 / 16.0))
        cst(lambda: POOL.affine_select(out=triU[:], in_=triU[:], pattern=[[-1, 64]], compare_op=ALU.is_gt, fill=0.0,
                                       base=0, channel_multiplier=1))
        cst(lambda: POOL.memset(maskT[:], 1.0))
        cst(lambda: POOL.affine_select(out=maskT[:], in_=maskT[:], pattern=[[0, 4], [1, 64]], compare_op=ALU.is_ge,
                                       fill=0.0, base=0, channel_multiplier=-1))

        NG = sb("NG", [128, 256], F32)
        KVN = sb("KVN", [128, 16], F32)
        HNx = sb("HNx", [128, 2, 4, 4], F32)
        CG = sb("CG", [128, 4, FC, 2], F32)
        prm = sb("prm", [128, 2, 128], F32)
        prm2 = sb("prm2", [24, 128], F32)
        B_prm = Buf("prm")
        B_par = Buf("par")
        B_CG = Buf("cg")
        dma(prm[:], norm_gains.rearrange("l i (k p) -> (l i k) p", p=128).rearrange("(a r) p -> r a p", r=128),
            writes=[B_prm])
        dma(prm2[0:16, :], kv_norm.rearrange("(k p) -> k p", p=128), writes=[B_prm])
        dma(prm2[16:24, :], gla_head_norm.rearrange("l (k p) -> (l k) p", p=128), writes=[B_prm])
        for a in range(2):
            op("pe", lambda a=a: PE.transpose(out=psb[0][:, a * 128:(a + 1) * 128], in_=prm[:, a, :], identity=identf[:]),
               reads=[B_prm, B_const], writes=[PB[0]])
        op("pe", lambda: PE.transpose(out=psb[0][:, 256:280], in_=prm2[0:24, :], identity=identf[0:24, 0:24]),
           reads=[B_prm, B_const], writes=[PB[0]])
        op("dve", lambda: DVE.tensor_copy(out=NG[:], in_=psb[0][:, 0:256]), reads=[PB[0]], writes=[B_par])
        op("dve", lambda: DVE.tensor_copy(out=KVN[:], in_=psb[0][:, 256:272]), reads=[PB[0]], writes=[B_par])
        for l_ in range(2):
            op("dve", lambda l_=l_: DVE.tensor_copy(
                out=HNx[:, l_, :, :],
                in_=psb[0][:, 272 + l_ * 4:276 + l_ * 4].unsqueeze(1).broadcast_to([128, 4, 4])),
                reads=[PB[0]], writes=[B_par])
        op("pool", lambda: POOL.memset(CG[:], 0.0), writes=[B_CG])

        ACT_F = max(KC * TMAX // 2, 17408)
        WORK_F = 19456
        STG_F = 3 * 4096
        big = sb("big", [128, ACT_F + WORK_F + STG_F], F32)
        WOFF = ACT_F
        SOFF = ACT_F + WORK_F

        def cvf(off, n):
            return big[:, off:off + n]

        def cvb(off, n):
            return big[:, off:off + n // 2].bitcast(BF16)

        actT = cvb(0, KC * TMAX).rearrange("p (k t) -> p k t", k=KC)
        B_actc = [Buf("act%d" % c) for c in range(TMAX // 64)]
        stg = [cvf(SOFF + i * 4096, 4096).rearrange("p (k n) -> p k n", k=8) for i in range(3)]
        B_stg = [Buf("stg%d" % i) for i in range(3)]
        stg_rr = [0]

        NTM = TMAX // 128
        B_Xs = [Buf() for _ in range(NTM)]
        B_Ms = [Buf() for _ in range(NTM)]
        B_PROJ = [Buf() for _ in range(NTM)]
        B_GLs = Buf()
        B_ATs = Buf()
        B_CS = [Buf(), Buf()]
        B_VSH = Buf()
        B_KTs = Buf()
        B_QTs = Buf()
        B_EREL = Buf()
        B_ZREL = Buf()

        def actbufs(tt):
            return [B_actc[2 * tt], B_actc[2 * tt + 1]]

        def load_wblock(wb_ap, B_wb, Wrows, kcs, c0, ncols, gain=None):
            for k0 in range(0, kcs, 8):
                nk = min(8, kcs - k0)
                i = stg_rr[0]
                stg_rr[0] = (i + 1) % 3
                src = Wrows[k0 * 128:(k0 + nk) * 128, c0:c0 + ncols].rearrange("(k p) n -> p k n", p=128)
                dma(stg[i][:, 0:nk, 0:ncols], src, writes=[B_stg[i]])
                if gain is None:
                    op("pool", lambda i=i, k0=k0, nk=nk: POOL.tensor_copy(out=wb_ap[:, k0:k0 + nk, 0:ncols],
                                                                           in_=stg[i][:, 0:nk, 0:ncols]),
                       reads=[B_stg[i]], writes=[B_wb])
                else:
                    for j in range(nk):
                        op("pool", lambda i=i, j=j, k0=k0: POOL.tensor_scalar(
                            out=wb_ap[:, k0 + j, 0:ncols], in0=stg[i][:, j, 0:ncols],
                            scalar1=gain[:, k0 + j:k0 + j + 1], scalar2=None, op0=ALU.mult),
                            reads=[B_stg[i], B_par], writes=[B_wb], mark=(j == nk - 1))

        def run_supertile(st):
            NS = 2 if st == 0 else 0
            T = np_tok + 64 * NS
            NT = T // 128
            NPCH = np_tok // 64
            first = (st == 0)
            last = (st == n_st - 1)
            tblocks = [(t0, min(512, np_tok - t0)) for t0 in range(0, np_tok, 512)]
            if NS:
                tblocks.append((np_tok, 128))
            segs = [(0, NPCH, "p", 0)] + [(NPCH + i, 1, "s", i) for i in range(NS)]

            def xrow(tt):
                if tt < np_tok // 128:
                    return st * np_tok + tt * 128
                return NPT

            def resnorm(x_src, Bx_src, m_src, gain_row, x_dst, Bx_dst, want_hT, is_out=False):
                kb.fence()
                W = WOFF
                xt = [cvf(W + 0, 2048), cvf(W + 2048, 2048)]
                mt = [cvf(W + 4096, 2048), cvf(W + 6144, 2048)]
                gB = cvf(W + 8192, 2048)
                xb = cvb(W + 10240, 2048)
                junk = cvb(W + 11264, 2048)
                small = cvf(W + 12288, 32)
                B_xt = [Buf(), Buf()]
                B_mt = [Buf(), Buf()]
                B_gb, B_xb, B_jk, B_sm = Buf(), Buf(), Buf(), [Buf(), Buf()]
                if m_src is not None:
                    dma(gB, gain_row.partition_broadcast(128), writes=[B_gb])
                for tt in range(NT):
                    p = tt % 2
                    sm = small[:, p * 8:(p + 1) * 8]
                    dma(xt[p], x_src(tt), reads=Bx_src(tt), writes=[B_xt[p]])
                    if m_src is not None:
                        dma(mt[p], m_src(tt), reads=[B_Ms[tt]], writes=[B_mt[p]])
                        op("act", lambda p=p, sm=sm: ACT.activation(out=junk, in_=mt[p], func=AF.Square,
                                                                     accum_out=sm[:, 0:1]),
                           reads=[B_mt[p]], writes=[B_jk, B_sm[p]])
                        op("act", lambda sm=sm: ACT.activation(out=sm[:, 1:2], in_=sm[:, 0:1], func=AF.Sqrt,
                                                               scale=1.0 / D, bias=c_eps[:]),
                           reads=[B_sm[p], B_const], writes=[B_sm[p]])
                        op("dve", lambda sm=sm: DVE.reciprocal(out=sm[:, 2:3], in_=sm[:, 1:2]),
                           reads=[B_sm[p]], writes=[B_sm[p]])
                        op("dve", lambda p=p, sm=sm: DVE.scalar_tensor_tensor(
                            out=mt[p], in0=mt[p], scalar=sm[:, 2:3], in1=gB, op0=ALU.mult, op1=ALU.mult),
                            reads=[B_sm[p], B_gb, B_mt[p]], writes=[B_mt[p]])
                        op("pool", lambda p=p: POOL.tensor_tensor(out=xt[p], in0=xt[p], in1=mt[p], op=ALU.add),
                           reads=[B_mt[p], B_xt[p]], writes=[B_xt[p]])
                    if x_dst is not None:
                        dma(x_dst(tt), xt[p], reads=[B_xt[p]], writes=Bx_dst(tt), is_out=is_out)
                    if want_hT:
                        op("act", lambda p=p, sm=sm: ACT.activation(out=junk, in_=xt[p], func=AF.Square,
                                                                     accum_out=sm[:, 3:4]),
                           reads=[B_xt[p]], writes=[B_jk, B_sm[p]])
                        op("act", lambda sm=sm: ACT.activation(out=sm[:, 4:5], in_=sm[:, 3:4], func=AF.Sqrt,
                                                               scale=1.0 / D, bias=c_eps[:]),
                           reads=[B_sm[p], B_const], writes=[B_sm[p]])
                        op("dve", lambda sm=sm: DVE.reciprocal(out=sm[:, 5:6], in_=sm[:, 4:5]),
                           reads=[B_sm[p]], writes=[B_sm[p]])
                        op("act", lambda p=p, sm=sm: ACT.activation(out=xb, in_=xt[p], func=AF.Copy, scale=sm[:, 5:6]),
                           reads=[B_sm[p], B_xt[p]], writes=[B_xb])
                        for g in range(2):
                            bank = 6 + g
                            pv = psbf(bank)
                            for j in range(8):
                                kc = g * 8 + j
                                op("pe", lambda kc=kc, j=j, pv=pv: PE.transpose(
                                    out=pv[:, j * 128:(j + 1) * 128], in_=xb[:, kc * 128:(kc + 1) * 128],
                                    identity=ident[:]),
                                    reads=[B_xb, B_const], writes=[PB[bank]], mark=(j == 7))
                            dst = actT[:, g * 8:(g + 1) * 8, tt * 128:(tt + 1) * 128]
                            srcv = pv.rearrange("p (k t) -> p k t", k=8)
                            if g == 0:
                                op("dve", lambda dst=dst, srcv=srcv: DVE.tensor_copy(out=dst, in_=srcv),
                                   reads=[PB[bank]], writes=actbufs(tt))
                            else:
                                op("act", lambda dst=dst, srcv=srcv: ACT.activation(out=dst, in_=srcv, func=AF.Copy),
                                   reads=[PB[bank]], writes=actbufs(tt))

            def gemm_setup():
                kb.fence()
                wbs = [cvb(WOFF + i * 4096, 8192).rearrange("p (k n) -> p k n", k=16) for i in range(2)]
                otf = [cvf(WOFF + 8192 + i * 512, 512) for i in range(4)]
                return wbs, [Buf(), Buf()], otf, [Buf() for _ in range(4)]

            def gemm_tok(Wrows, col_blocks, gain, evac, extra=None):
                wbs, B_wb, otf, B_ot = gemm_setup()
                rr = [0, 0]

                def next_ot():
                    i = rr[1]
                    rr[1] = (i + 1) % 4
                    return i, otf[i], B_ot[i]

                load_wblock(wbs[0], B_wb[0], Wrows, KC, col_blocks[0][0], col_blocks[0][1], gain)
                for bi, (c0, ncols) in enumerate(col_blocks):
                    cur = bi % 2
                    if bi + 1 < len(col_blocks):
                        load_wblock(wbs[1 - cur], B_wb[1 - cur], Wrows, KC, col_blocks[bi + 1][0],
                                    col_blocks[bi + 1][1], gain)
                    if extra is not None and extra(bi, c0, ncols, wbs[cur], B_wb[cur], next_ot):
                        continue
                    for tt in range(NT):
                        bank = rr[0] % 4
                        rr[0] += 1
                        for kc in range(KC):
                            op("pe", lambda kc=kc, tt=tt, bank=bank, cur=cur, ncols=ncols: PE.matmul(
                                psb[bank][:, 0:ncols], lhsT=actT[:, kc, tt * 128:(tt + 1) * 128],
                                rhs=wbs[cur][:, kc, 0:ncols], start=(kc == 0), stop=(kc == KC - 1)),
                                reads=actbufs(tt) + [B_wb[cur]], writes=[PB[bank]], mark=(kc == KC - 1))
                        evac(bi, c0, ncols, tt, bank, next_ot)

            def mm_feat(bank, wb, B_wb, f0, m, t0, n):
                c0_ = t0 // 64
                rb_ = [B_actc[c] for c in range(c0_, (t0 + n) // 64)]
                for kc in range(KC):
                    op("pe", lambda kc=kc: PE.matmul(psb[bank][0:m, 0:n], lhsT=wb[:, kc, f0:f0 + m],
                                                     rhs=actT[:, kc, t0:t0 + n], start=(kc == 0), stop=(kc == KC - 1)),
                       reads=rb_ + [B_wb], writes=[PB[bank]], mark=(kc == KC - 1))

            def evac_to_Ms(bi, c0, ncols, tt, bank, next_ot):
                i, o, Bo = next_ot()
                op("act", lambda o=o, bank=bank: ACT.activation(out=o, in_=psb[bank][:, 0:512], func=AF.Copy),
                   reads=[PB[bank]], writes=[Bo])
                dma(Ms[tt * 128:(tt + 1) * 128, c0:c0 + 512], o, reads=[Bo], writes=[B_Ms[tt]])

            def phaseA(l):
                gain = NG[:, (l * 4 + 0) * 16:(l * 4 + 0) * 16 + 16]
                import os
                blocks = [(c, 512) for c in range(0, 6144, 512)] + [(6144, 16)]
                if os.environ.get("DBG_NOG"): blocks = blocks[:int(os.environ["DBG_NOG"])]

                def evacA(bi, c0, ncols, tt, bank, next_ot):
                    i, o_, Bo = next_ot()
                    o = o_.bitcast(BF16)[:, 0:512]
                    fn = AF.Silu if c0 >= 4096 else AF.Copy
                    op("act", lambda o=o, bank=bank, fn=fn: ACT.activation(out=o, in_=psb[bank][:, 0:512], func=fn),
                       reads=[PB[bank]], writes=[Bo])
                    dma(PROJ[tt * 128:(tt + 1) * 128, c0:c0 + 512], o, reads=[Bo], writes=[B_PROJ[tt]])

                def extraA(bi, c0, ncols, wb, B_wb, next_ot):
                    if c0 != 6144:
                        return False
                    for (t0, n) in tblocks:
                        bank = 4
                        mm_feat(bank, wb, B_wb, 0, 16, t0, n)
                        i, o, Bo = next_ot()
                        op("act", lambda o=o, n=n: ACT.activation(out=o[0:16, 0:n], in_=psb[4][0:16, 0:n], func=AF.Copy),
                           reads=[PB[4]], writes=[Bo])
                        dma(GLs[:, t0:t0 + n], o[0:16, 0:n], reads=[Bo], writes=[B_GLs])
                    return True

                gemm_tok(gla_w_in[l], blocks, gain, evacA, extraA)

            def phaseB(l):
                kb.fence()
                W = WOFF
                Pb = [cvb(SOFF + i * 3072, 6144)[0:64, :] for i in range(3)]
                B_P = [Buf() for _ in range(3)]
                S = cvf(W + 0, 4096).rearrange("p (k v) -> p k v", k=8)
                Sb = cvb(W + 4096, 4096).rearrange("p (k v) -> p k v", k=8)
                sp = cvf(W + 6144, 1024)[0:64, :]
                XD = cvf(W + 7168, 1024)[0:64, :]
                EQ = cvf(W + 8192, 512).rearrange("p (k t) -> p k t", k=8)
                EK = cvf(W + 8704, 512).rearrange("p (k t) -> p k t", k=8)
                qt = cvb(W + 9216, 512).rearrange("p (k t) -> p k t", k=8)
                kt = cvb(W + 9472, 512).rearrange("p (k t) -> p k t", k=8)
                kp = cvb(W + 9728, 1024)[0:64, :]
                am = cvb(W + 10240, 256)[0:64, :].rearrange("p (h t) -> p h t", h=4)
                osb = [cvf(W + 10368 + i * 512, 512)[0:64, :] for i in range(2)]
                on = cvb(W + 11392, 2048)[0:64, :]
                small = cvf(W + 12416, 32)
                junk = cvb(W + 12448, 512)[0:64, :]
                WG = cvf(W + 12704, 1024)[0:17, :]
                GLc = [cvf(W + 13728 + i * 64, 64)[0:17, :] for i in range(2)]
                Eend = cvf(W + 13856, 8)
                B_S, B_Sb, B_sp, B_XD, B_EQ, B_EK, B_qt, B_kt, B_kp, B_am = [Buf() for _ in range(10)]
                B_osb = [Buf(), Buf()]
                B_on, B_small, B_junk, B_WG, B_Eend = Buf(), Buf(), Buf(), Buf(), Buf()
                B_GLc = [Buf(), Buf()]
                dma(WG[0:16, :], gla_w_gate[l], writes=[B_WG])
                dma(WG[16:17, :], gla_b_gate[l].unsqueeze(0), writes=[B_WG])
                for i in range(2):
                    op("pool", lambda i=i: POOL.memset(GLc[i], 1.0), writes=[B_GLc[i]])
                cidx = 0
                for (c0, ncs, kind, sidx) in segs:
                    if kind == "p":
                        if first:
                            op("pool", lambda: POOL.memset(S, 0.0), writes=[B_S])
                        else:
                            dma(S, CARRY_S[l], reads=[B_CS[l]], writes=[B_S])
                    else:
                        dma(S, gla_s0[l, sidx].rearrange("h (j p) v -> p (h j) v", p=128), writes=[B_S])
                    op("act", lambda: ACT.activation(out=Sb, in_=S, func=AF.Copy), reads=[B_S], writes=[B_Sb])
                    for c in range(c0, c0 + ncs):
                        pi = cidx % 3
                        gi = cidx % 2
                        cidx += 1
                        P = Pb[pi]
                        dma(P, PROJ[c * 64:(c + 1) * 64, :], reads=[B_PROJ[c // 2]], writes=[B_P[pi]])
                        dma(GLc[gi][0:16, :], GLs[:, c * 64:(c + 1) * 64], reads=[B_GLs], writes=[B_GLc[gi]])
                        for hf in range(2):
                            op("pe", lambda hf=hf, gi=gi: PE.matmul(psb[hf][0:64, :], lhsT=GLc[gi], rhs=WG[:, hf * 512:(hf + 1) * 512],
                                                                    start=True, stop=True),
                               reads=[B_GLc[gi], B_WG], writes=[PB[hf]])
                            op("act", lambda hf=hf: ACT.activation(out=sp[:, hf * 512:(hf + 1) * 512], in_=psb[hf][0:64, :],
                                                                   func=AF.Exp, scale=-1.0),
                               reads=[PB[hf]], writes=[B_sp])
                        op("act", lambda: ACT.activation(out=sp, in_=sp, func=AF.Ln, bias=c_one[0:64, :]),
                           reads=[B_sp, B_const], writes=[B_sp])
                        bps = psb[2][:].rearrange("p (k t) -> p k t", k=8)
                        for kc in range(8):
                            op("pe", lambda kc=kc: PE.matmul(bps[:, kc, :], lhsT=sp[:, kc * 128:(kc + 1) * 128], rhs=triI[:],
                                                             start=True, stop=True),
                               reads=[B_sp, B_const], writes=[PB[2]], mark=(kc == 7))
                        for hf in range(2):
                            op("pe", lambda hf=hf: PE.matmul(psb[hf][0:64, :], lhsT=triU[:], rhs=sp[:, hf * 512:(hf + 1) * 512],
                                                             start=True, stop=True),
                               reads=[B_sp, B_const], writes=[PB[hf]])
                            op("act", lambda hf=hf: ACT.activation(out=XD[:, hf * 512:(hf + 1) * 512], in_=psb[hf][0:64, :],
                                                                   func=AF.Exp),
                               reads=[PB[hf]], writes=[B_XD])
                        op("act", lambda: ACT.activation(out=EQ, in_=bps, func=AF.Exp, bias=c_ln16[:]),
                           reads=[PB[2], B_const], writes=[B_EQ])
                        op("act", lambda: ACT.activation(out=EK, in_=bps, func=AF.Exp, scale=-1.0),
                           reads=[PB[2]], writes=[B_EK])
                        op("act", lambda: ACT.activation(out=Eend, in_=bps[:, :, 63], func=AF.Exp),
                           reads=[PB[2]], writes=[B_Eend])
                        qv = psbf(3)[:, 0:512].rearrange("p (k t) -> p k t", k=8)
                        kv_ = psbf(3)[:, 512:1024].rearrange("p (k t) -> p k t", k=8)
                        for kc in range(8):
                            op("pe", lambda kc=kc: PE.transpose(out=qv[:, kc, :], in_=P[:, kc * 128:(kc + 1) * 128],
                                                                identity=ident[0:64, 0:64]),
                               reads=[B_P[pi], B_const], writes=[PB[3]], mark=False)
                        for kc in range(8):
                            op("pe", lambda kc=kc: PE.transpose(out=kv_[:, kc, :], in_=P[:, 1024 + kc * 128:1024 + (kc + 1) * 128],
                                                                identity=ident[0:64, 0:64]),
                               reads=[B_P[pi], B_const], writes=[PB[3]], mark=(kc == 7))
                        op("dve", lambda: DVE.tensor_tensor(out=qt, in0=qv, in1=EQ, op=ALU.mult),
                           reads=[PB[3], B_EQ], writes=[B_qt])
                        op("dve", lambda: DVE.tensor_tensor(out=kt, in0=kv_, in1=EK, op=ALU.mult),
                           reads=[PB[3], B_EK], writes=[B_kt])
                        op("dve", lambda: DVE.tensor_tensor(out=kp, in0=P[:, 1024:2048], in1=XD, op=ALU.mult),
                           reads=[B_P[pi], B_XD], writes=[B_kp])
                        aps = psb[4][0:64, 0:256].rearrange("p (h t) -> p h t", h=4)
                        for h in range(4):
                            for j in range(2):
                                op("pe", lambda h=h, j=j: PE.matmul(aps[:, h, :], lhsT=kt[:, 2 * h + j, :], rhs=qt[:, 2 * h + j, :],
                                                                    start=(j == 0), stop=(j == 1)),
                                   reads=[B_kt, B_qt], writes=[PB[4]], mark=(h == 3 and j == 1))
                        op("dve", lambda: DVE.tensor_tensor(out=am, in0=aps, in1=maskT[:], op=ALU.mult),
                           reads=[PB[4], B_const], writes=[B_am])
                        for h in range(4):
                            oi = h % 2
                            vh = P[:, 2048 + h * 512:2048 + (h + 1) * 512]
                            op("pe", lambda h=h, vh=vh: PE.matmul(psb[5][0:64, :], lhsT=am[:, h, :], rhs=vh, start=True, stop=False),
                               reads=[B_am, B_P[pi]], writes=[PB[5]], mark=False)
                            for j in range(2):
                                op("pe", lambda h=h, j=j: PE.matmul(psb[5][0:64, :], lhsT=qt[:, 2 * h + j, :], rhs=Sb[:, 2 * h + j, :],
                                                                    start=False, stop=(j == 1)),
                                   reads=[B_qt, B_Sb], writes=[PB[5]], mark=(j == 1))
                            op("act", lambda oi=oi: ACT.activation(out=osb[oi], in_=psb[5][0:64, :], func=AF.Copy),
                               reads=[PB[5]], writes=[B_osb[oi]])
                            op("act", lambda oi=oi, h=h: ACT.activation(out=junk, in_=osb[oi], func=AF.Square,
                                                                        accum_out=small[0:64, h:h + 1]),
                               reads=[B_osb[oi]], writes=[B_junk, B_small])
                            op("act", lambda h=h: ACT.activation(out=small[0:64, 4 + h:5 + h], in_=small[0:64, h:h + 1],
                                                                 func=AF.Sqrt, scale=1.0 / 512, bias=c_eps[0:64, :]),
                               reads=[B_small, B_const], writes=[B_small])
                            op("dve", lambda h=h: DVE.reciprocal(out=small[0:64, 8 + h:9 + h], in_=small[0:64, 4 + h:5 + h]),
                               reads=[B_small], writes=[B_small])
                            op("dve", lambda h=h, oi=oi: DVE.scalar_tensor_tensor(
                                out=on[:, h * 512:(h + 1) * 512], in0=osb[oi], scalar=small[0:64, 8 + h:9 + h],
                                in1=P[:, 4096 + h * 512:4096 + (h + 1) * 512], op0=ALU.mult, op1=ALU.mult),
                                reads=[B_small, B_osb[oi], B_P[pi]], writes=[B_on])
                            for j in range(2):
                                kc = 2 * h + j
                                op("pe", lambda kc=kc, vh=vh: PE.matmul(psb[6][:, :], lhsT=kp[:, kc * 128:(kc + 1) * 128], rhs=vh,
                                                                        start=True, stop=True),
                                   reads=[B_kp, B_P[pi]], writes=[PB[6]])
                                op("dve", lambda kc=kc: DVE.scalar_tensor_tensor(
                                    out=S[:, kc, :], in0=S[:, kc, :], scalar=Eend[:, kc:kc + 1], in1=psb[6][:, :],
                                    op0=ALU.mult, op1=ALU.add),
                                    reads=[B_Eend, PB[6], B_S, B_Sb], writes=[B_S])
                                op("act", lambda kc=kc: ACT.activation(out=Sb[:, kc, :], in_=S[:, kc, :], func=AF.Copy),
                                   reads=[B_S], writes=[B_Sb])
                        tv = psbf(7).rearrange("p (k t) -> p k t", k=16)
                        for kc in range(16):
                            op("pe", lambda kc=kc: PE.transpose(out=tv[:, kc, :], in_=on[:, kc * 128:(kc + 1) * 128],
                                                                identity=ident[0:64, 0:64]),
                               reads=[B_on, B_const], writes=[PB[7]], mark=(kc == 15))
                        op("dve", lambda c=c: DVE.tensor_copy(out=actT[:, :, c * 64:(c + 1) * 64], in_=tv),
                           reads=[PB[7]], writes=[B_actc[c]])
                    sview = lambda ap: ap.rearrange("h (j p) v -> p (h j) v", p=128)
                    if kind == "p":
                        if last:
                            dma(sview(o_gla_p[l]), S, reads=[B_S], is_out=True)
                        else:
                            dma(CARRY_S[l], S, reads=[B_S], writes=[B_CS[l]])
                    else:
                        dma(sview(o_gla_s[l, sidx]), S, reads=[B_S], is_out=True)

            def phaseC(Wrows, gain):
                gemm_tok(Wrows, [(c, 512) for c in range(0, D, 512)], gain, evac_to_Ms)

            def phaseE(l):
                kb.fence()
                W = WOFF
                prw = cvf(SOFF, DFF)[0:8, :]
                B_prw = Buf()
                cp = cvf(W + 17000, FC * 8).rearrange("p (f j) -> p f j", f=FC)
                GT = cvf(W + 17352, FC * 6).rearrange("p (f j) -> p f j", f=FC)
                B_cp, B_GT = Buf(), Buf()
                op("pool", lambda: POOL.memset(prw, 0.0), writes=[B_prw])
                dma(prw[0:3, :], ffn_conv_w[l], writes=[B_prw])
                dma(prw[3:4, :], ffn_conv_b[l].unsqueeze(0), writes=[B_prw])
                if NS:
                    dma(prw[4:8, :], conv_s0[l].rearrange("s j f -> (s j) f"), writes=[B_prw])
                for f in range(FC):
                    op("pe", lambda f=f: PE.transpose(out=psb[0][:, f * 8:(f + 1) * 8], in_=prw[:, f * 128:(f + 1) * 128],
                                                      identity=identf[0:8, 0:8]),
                       reads=[B_prw, B_const], writes=[PB[0]], mark=(f == FC - 1))
                op("dve", lambda: DVE.tensor_copy(out=cp, in_=psb[0][:, 0:FC * 8].rearrange("p (f j) -> p f j", f=FC)),
                   reads=[PB[0]], writes=[B_cp])
                kb.fence()
                wbs = [cvb(W + i * 4096, 8192).rearrange("p (g k n) -> p g k n", g=2, k=16) for i in range(2)]
                B_wb = [Buf(), Buf()]
                TG = T + 6
                Gx = cvf(W + 8192, TG)
                Vx = [cvb(W + 8192 + 2184 + i * 1088, 2176)[:, 0:T] for i in range(2)]
                Cx = cvf(W + 8192 + 2184 + 2176, 2176)[:, 0:T]
                At = [cvb(W + 8192 + 2184 + 2176 + 2176 + i * 1088, 2176)[:, 0:T] for i in range(2)]
                B_Gx, B_Cx = Buf(), Buf()
                B_Vx = [Buf(), Buf()]
                B_At = [Buf(), Buf()]
                goff = {}
                for (c0, ncs, kind, sidx) in segs:
                    goff[(kind, sidx)] = 0 if kind == "p" else np_tok + 2 + 66 * sidx
                gain = NG[:, (l * 4 + 2) * 16:(l * 4 + 2) * 16 + 16]
                Wup = ffn_w_up[l]
                NG2 = DFF // 256

                def load_grp(g, slot):
                    load_wblock(wbs[slot][:, 0, :, :], B_wb[slot], Wup, KC, g * 256, 256, gain)
                    load_wblock(wbs[slot][:, 1, :, :], B_wb[slot], Wup, KC, DFF + g * 256, 256, gain)

                load_grp(0, 0)
                fbi = 0
                for g in range(NG2):
                    cur = g % 2
                    if g + 1 < NG2:
                        load_grp(g + 1, 1 - cur)
                    for fl in range(2):
                        fb = g * 2 + fl
                        vi = fbi % 2
                        fbi += 1
                        if first:
                            op("pool", lambda: POOL.memset(Gx[:, 0:2], 0.0), writes=[B_Gx])
                        else:
                            op("pool", lambda fb=fb: POOL.tensor_copy(out=Gx[:, 0:2], in_=CG[:, l, fb, :]),
                               reads=[B_CG], writes=[B_Gx])
                        for i in range(NS):
                            o_ = goff[("s", i)]
                            op("pool", lambda fb=fb, i=i, o_=o_: POOL.tensor_copy(out=Gx[:, o_:o_ + 2], in_=cp[:, fb, 4 + 2 * i:6 + 2 * i]),
                               reads=[B_cp], writes=[B_Gx])
                        for bi, (t0, n) in enumerate(tblocks):
                            bg = bi % 3
                            bv = 3 + bi % 3
                            mm_feat(bg, wbs[cur][:, 0, :, :], B_wb[cur], fl * 128, 128, t0, n)
                            mm_feat(bv, wbs[cur][:, 1, :, :], B_wb[cur], fl * 128, 128, t0, n)
                            if t0 < np_tok:
                                op("act", lambda bg=bg, t0=t0, n=n: ACT.activation(out=Gx[:, 2 + t0:2 + t0 + n], in_=psb[bg][:, 0:n], func=AF.Copy),
                                   reads=[PB[bg]], writes=[B_Gx])
                            else:
                                for i in range(NS):
                                    o_ = goff[("s", i)] + 2
                                    op("act", lambda bg=bg, i=i, o_=o_: ACT.activation(out=Gx[:, o_:o_ + 64], in_=psb[bg][:, i * 64:(i + 1) * 64], func=AF.Copy),
                                       reads=[PB[bg]], writes=[B_Gx])
                            op("dve", lambda bv=bv, t0=t0, n=n, vi=vi: DVE.tensor_copy(out=Vx[vi][:, t0:t0 + n], in_=psb[bv][:, 0:n]),
                               reads=[PB[bv]], writes=[B_Vx[vi]])
                        for (c0, ncs, kind, sidx) in segs:
                            o_ = goff[(kind, sidx)]
                            n = ncs * 64
                            t0 = c0 * 64
                            op("dve", lambda fb=fb, o_=o_, n=n, t0=t0: DVE.tensor_scalar(
                                out=Cx[:, t0:t0 + n], in0=Gx[:, o_ + 2:o_ + 2 + n], scalar1=cp[:, fb, 2:3], scalar2=cp[:, fb, 3:4],
                                op0=ALU.mult, op1=ALU.add), reads=[B_Gx, B_cp], writes=[B_Cx])
                            op("dve", lambda fb=fb, o_=o_, n=n, t0=t0: DVE.scalar_tensor_tensor(
                                out=Cx[:, t0:t0 + n], in0=Gx[:, o_ + 1:o_ + 1 + n], scalar=cp[:, fb, 1:2], in1=Cx[:, t0:t0 + n],
                                op0=ALU.mult, op1=ALU.add), reads=[B_Gx, B_cp, B_Cx], writes=[B_Cx])
                            op("dve", lambda fb=fb, o_=o_, n=n, t0=t0: DVE.scalar_tensor_tensor(
                                out=Cx[:, t0:t0 + n], in0=Gx[:, o_:o_ + n], scalar=cp[:, fb, 0:1], in1=Cx[:, t0:t0 + n],
                                op0=ALU.mult, op1=ALU.add), reads=[B_Gx, B_cp, B_Cx], writes=[B_Cx])
                            j0 = 0 if kind == "p" else 2 + 2 * sidx
                            op("pool", lambda fb=fb, o_=o_, n=n, j0=j0: POOL.tensor_copy(out=GT[:, fb, j0:j0 + 2], in_=Gx[:, o_ + n:o_ + n + 2]),
                               reads=[B_Gx], writes=[B_GT])
                            if kind == "p":
                                op("pool", lambda fb=fb, o_=o_, n=n: POOL.tensor_copy(out=CG[:, l, fb, :], in_=Gx[:, o_ + n:o_ + n + 2]),
                                   reads=[B_Gx], writes=[B_CG])
                        op("act", lambda: ACT.activation(out=Cx, in_=Cx, func=AF.Silu), reads=[B_Cx], writes=[B_Cx])
                        op("dve", lambda vi=vi: DVE.tensor_tensor(out=At[vi], in0=Cx, in1=Vx[vi], op=ALU.mult),
                           reads=[B_Cx, B_Vx[vi]], writes=[B_At[vi]])
                        dma(ATs[fb][:, 0:T], At[vi], reads=[B_At[vi]], writes=[B_ATs])
                kb.fence()
                orow = cvf(SOFF, 2048)[0:6, :]
                B_orow = Buf()
                for r0 in range(0, FC, 16):
                    nf = min(16, FC - r0)
                    for f in range(nf):
                        bank = f // 4
                        op("pe", lambda f=f, r0=r0, bank=bank: PE.transpose(
                            out=psb[bank][0:6, (f % 4) * 128:(f % 4 + 1) * 128], in_=GT[:, r0 + f, :], identity=identf[:]),
                            reads=[B_GT, B_const], writes=[PB[bank]], mark=(f % 4 == 3 or f == nf - 1))
                    for bank in range((nf + 3) // 4):
                        op("dve", lambda bank=bank: DVE.tensor_copy(out=orow[:, bank * 512:(bank + 1) * 512], in_=psb[bank][0:6, :]),
                           reads=[PB[bank]], writes=[B_orow])
                    cs = slice(r0 * 128, (r0 + nf) * 128)
                    if last:
                        dma(o_conv_p[l][:, cs], orow[0:2, 0:nf * 128], reads=[B_orow], is_out=True)
                    for i in range(NS):
                        dma(o_conv_s[l, i][:, cs], orow[2 + 2 * i:4 + 2 * i, 0:nf * 128], reads=[B_orow], is_out=True)

            def phaseF(l):
                kb.fence()
                wbs = [cvb(i * 11264, 22528).rearrange("p (k n) -> p k n", k=FC) for i in range(2)]
                aT = [cvb(22528 + i * 2816, 5632).rearrange("p (k t) -> p k t", k=FC) for i in range(2)]
                otf = [cvf(28160 + i * 512, 512) for i in range(4)]
                B_wb = [Buf(), Buf()]
                B_aT = [Buf(), Buf()]
                B_ot = [Buf() for _ in range(4)]
                Wd = ffn_w_down[l]
                load_wblock(wbs[0], B_wb[0], Wd, FC, 0, 512, None)
                rr = 0
                ai = 0
                for nb in range(4):
                    cur = nb % 2
                    if nb + 1 < 4:
                        load_wblock(wbs[1 - cur], B_wb[1 - cur], Wd, FC, (nb + 1) * 512, 512, None)
                    for tt in range(NT):
                        a = ai % 2
                        ai += 1
                        dma(aT[a], ATs[:, :, tt * 128:(tt + 1) * 128].rearrange("k p t -> p k t"), reads=[B_ATs], writes=[B_aT[a]])
                        bank = rr % 4
                        oi = rr % 4
                        rr += 1
                        for kc in range(FC):
                            op("pe", lambda kc=kc, a=a, bank=bank, cur=cur: PE.matmul(
                                psb[bank][:, :], lhsT=aT[a][:, kc, :], rhs=wbs[cur][:, kc, :], start=(kc == 0), stop=(kc == FC - 1)),
                                reads=[B_aT[a], B_wb[cur]], writes=[PB[bank]], mark=(kc == FC - 1))
                        op("act", lambda oi=oi, bank=bank: ACT.activation(out=otf[oi], in_=psb[bank][:, :], func=AF.Copy),
                           reads=[PB[bank]], writes=[B_ot[oi]])
                        dma(Ms[tt * 128:(tt + 1) * 128, nb * 512:(nb + 1) * 512], otf[oi], reads=[B_ot[oi]], writes=[B_Ms[tt]])

            def phaseKV():
                wbs, B_wb, otf, B_ot = gemm_setup()
                rr = [0, 0]

                def next_ot():
                    i = rr[1]
                    rr[1] = (i + 1) % 4
                    return i, otf[i], B_ot[i]

                load_wblock(wbs[0], B_wb[0], att_w_kv, KC, 0, 512, KVN)
                load_wblock(wbs[1], B_wb[1], att_w_kv, KC, 512, 512, KVN)

                def out_rows(tt):
                    if tt >= np_tok // 128:
                        return ("s", 0)
                    g0 = st * np_tok + tt * 128
                    if g0 >= NPT - 512:
                        return ("p", g0 - (NPT - 512))
                    return None

                if first:
                    i, o_, Bo = next_ot()
                    zt = o_.bitcast(BF16)[:, 0:512]
                    op("pool", lambda zt=zt: POOL.memset(zt, 0.0), writes=[Bo])
                    for h in range(4):
                        dma(KTs[h][:, 0:512], zt, reads=[Bo], writes=[B_KTs])
                        dma(VSH[h * 128:(h + 1) * 128, :], zt, reads=[Bo], writes=[B_VSH])
                for fb in range(4):
                    for (t0, n) in tblocks:
                        bank = rr[0] % 4
                        rr[0] += 1
                        mm_feat(bank, wbs[0], B_wb[0], fb * 128, 128, t0, n)
                        i, o_, Bo = next_ot()
                        o = o_.bitcast(BF16)
                        op("act", lambda o=o, bank=bank, n=n: ACT.activation(out=o[:, 0:n], in_=psb[bank][:, 0:n], func=AF.Copy),
                           reads=[PB[bank]], writes=[Bo])
                        if t0 < np_tok:
                            g0 = 512 + st * np_tok + t0
                            dma(KTs[fb][:, g0:g0 + n], o[:, 0:n], reads=[Bo], writes=[B_KTs])
                        else:
                            for si in range(NS):
                                dma(KTss[si, fb][:, 512:576], o[:, si * 64:(si + 1) * 64], reads=[Bo], writes=[B_KTs])
                import os
                if os.environ.get('DBG_KV') == '1':
                    return
                for tt in range(NT):
                    orow = out_rows(tt)
                    if orow is not None:
                        bank = rr[0] % 4
                        rr[0] += 1
                        for kc in range(KC):
                            op("pe", lambda kc=kc, tt=tt, bank=bank: PE.matmul(
                                psb[bank][:, :], lhsT=actT[:, kc, tt * 128:(tt + 1) * 128], rhs=wbs[0][:, kc, :],
                                start=(kc == 0), stop=(kc == KC - 1)),
                                reads=actbufs(tt) + [B_wb[0]], writes=[PB[bank]], mark=(kc == KC - 1))
                        i, o, Bo = next_ot()
                        op("act", lambda o=o, bank=bank: ACT.activation(out=o, in_=psb[bank][:, :], func=AF.Copy),
                           reads=[PB[bank]], writes=[Bo])
                        dst = o_k_p[orow[1]:orow[1] + 128, :] if orow[0] == "p" else o_k_s
                        dma(dst, o, reads=[Bo], is_out=True)
                    bank = rr[0] % 4
                    rr[0] += 1
                    for kc in range(KC):
                        op("pe", lambda kc=kc, tt=tt, bank=bank: PE.matmul(
                            psb[bank][:, :], lhsT=actT[:, kc, tt * 128:(tt + 1) * 128], rhs=wbs[1][:, kc, :],
                            start=(kc == 0), stop=(kc == KC - 1)),
                            reads=actbufs(tt) + [B_wb[1]], writes=[PB[bank]], mark=(kc == KC - 1))
                    vsrc, vB = psb[bank][:, :], PB[bank]
                    if orow is not None:
                        i, of, Bof = next_ot()
                        op("act", lambda of=of, bank=bank: ACT.activation(out=of, in_=psb[bank][:, :], func=AF.Copy),
                           reads=[PB[bank]], writes=[Bof])
                        dst = o_v_p[orow[1]:orow[1] + 128, :] if orow[0] == "p" else o_v_s
                        dma(dst, of, reads=[Bof], is_out=True)
                        vsrc, vB = of, Bof
                    i, o_, Bo = next_ot()
                    o = o_.bitcast(BF16)[:, 0:512]
                    op("dve", lambda o=o, vsrc=vsrc: DVE.tensor_copy(out=o, in_=vsrc), reads=[vB], writes=[Bo])
                    if tt < np_tok // 128:
                        r0 = 512 + st * np_tok + tt * 128
                        dma(VSH[r0:r0 + 128, :], o, reads=[Bo], writes=[B_VSH])
                    else:
                        for si in range(NS):
                            r0 = (8 + NPC + 9 * si) * 64 + 512
                            dma(VSH[r0:r0 + 64, :], o[si * 64:(si + 1) * 64, :], reads=[Bo], writes=[B_VSH])
                if os.environ.get('DBG_KV') == '2':
                    return
                if NS:
                    kb.fence()
                    cf = [cvf(SOFF + i * 512, 512) for i in range(2)]
                    cb_ = [cvb(SOFF + 1024 + i * 256, 512) for i in range(2)]
                    kt_ = [cvb(SOFF + 1536 + i * 256, 512).rearrange("p (h t) -> p h t", h=4) for i in range(2)]
                    B_cf, B_cb, B_kt = [Buf(), Buf()], [Buf(), Buf()], [Buf(), Buf()]
                    n_ = 0
                    for si in range(NS):
                        for rt in range(4):
                            for which in range(2):
                                b = n_ % 2
                                n_ += 1
                                src = (ck if which == 0 else cv)[si][rt * 128:(rt + 1) * 128, :]
                                dma(cf[b], src, writes=[B_cf[b]])
                                op("dve", lambda b=b: DVE.tensor_copy(out=cb_[b], in_=cf[b]), reads=[B_cf[b]], writes=[B_cb[b]])
                                if which == 1:
                                    r0 = (8 + NPC + 9 * si) * 64 + rt * 128
                                    dma(VSH[r0:r0 + 128, :], cb_[b], reads=[B_cb[b]], writes=[B_VSH])
                                else:
                                    pv = psbf(4 + b)
                                    for h in range(4):
                                        op("pe", lambda h=h, b=b, pv=pv: PE.transpose(out=pv[:, h * 128:(h + 1) * 128], in_=cb_[b][:, h * 128:(h + 1) * 128],
                                                                                      identity=ident[:]),
                                           reads=[B_cb[b], B_const], writes=[PB[4 + b]], mark=(h == 3))
                                    op("act", lambda b=b, pv=pv: ACT.activation(out=kt_[b], in_=pv[:, 0:512].rearrange("p (h t) -> p h t", h=4), func=AF.Copy),
                                       reads=[PB[4 + b]], writes=[B_kt[b]])
                                    dma(KTss[si][:, :, rt * 128:(rt + 1) * 128].rearrange("h p t -> p h t"), kt_[b], reads=[B_kt[b]], writes=[B_KTs])

            def phaseQ(l, j):
                wbs, B_wb, otf, B_ot = gemm_setup()
                gain = NG[:, (l * 4 + 0) * 16:(l * 4 + 0) * 16 + 16]
                Wq = att_w_q[j]
                load_wblock(wbs[0], B_wb[0], Wq, KC, 0, 512, gain)
                rr = 0
                for blk in range(4):
                    cur = blk % 2
                    if blk + 1 < 4:
                        load_wblock(wbs[1 - cur], B_wb[1 - cur], Wq, KC, (blk + 1) * 512, 512, gain)
                    for pl in range(2):
                        pair = blk * 2 + pl
                        for (t0, n) in tblocks:
                            oi = rr % 4
                            o = otf[oi].bitcast(BF16)[:, 0:2 * n].rearrange("p (c h t) -> p c h t", h=2, t=64)
                            for h2 in range(2):
                                bank = (2 * rr + h2) % 4
                                mm_feat(bank, wbs[cur], B_wb[cur], (pl * 2 + h2) * 128, 128, t0, n)
                                op("act", lambda o=o, bank=bank, n=n, h2=h2: ACT.activation(
                                    out=o[:, :, h2, :], in_=psb[bank][:, 0:n].rearrange("p (c t) -> p c t", t=64), func=AF.Copy,
                                    scale=float(128 ** -0.5)),
                                    reads=[PB[bank]], writes=[B_ot[oi]])
                            rr += 1
                            dma(QTs[pair][:, 2 * t0:2 * (t0 + n)], otf[oi].bitcast(BF16)[:, 0:2 * n], reads=[B_ot[oi]], writes=[B_QTs])

            def phaseATT(l, j):
                kb.fence()
                W = WOFF
                Bt = cvf(W + 0, 4608).rearrange("p (g k) -> p g k", g=8)
                qb = [cvb(W + 4608 + i * 2048, 4096).rearrange("p (g t) -> p g t", g=8) for i in range(2)]
                KTb = [cvb(W + 8704 + i * 1152, 2304).rearrange("p (h t) -> p h t", h=4) for i in range(2)]
                Vb = [cvb(W + 11008 + i * 2304, 4608)[0:64, :].rearrange("p (c n) -> p c n", c=9) for i in range(2)]
                sbf = [cvf(W + 15616 + i * 576, 576) for i in range(2)]
                pn = [cvb(W + 16768 + i * 288, 576) for i in range(2)]
                pTs = [cvb(W + 17344 + i * 576, 1152)[0:64, :].rearrange("p (c t) -> p c t", c=9) for i in range(2)]
                small = cvf(W + 18496, 64)
                rbt = cvf(SOFF, 513)[0:16, :]
                et = cvf(SOFF + 520, 640)[0:16, :]
                B_Bt, B_rbt, B_et = Buf(), Buf(), Buf()
                B_qb, B_KTb, B_Vb = [Buf(), Buf()], [Buf(), Buf()], [Buf(), Buf()]
                B_sbf, B_pn, B_pTs, B_sm = [Buf(), Buf()], [Buf(), Buf()], [Buf(), Buf()], [Buf(), Buf()]
                dma(rbt, att_rel_bias[j], writes=[B_rbt])
                op("dve", lambda: DVE.memset(et, 0.0), writes=[B_et])
                op("dve", lambda: DVE.tensor_copy(out=et[:, 0:319], in_=rbt[:, 512:513].broadcast_to([16, 319])),
                   reads=[B_rbt], writes=[B_et])
                op("dve", lambda: DVE.tensor_copy(out=et[:, 319:639], in_=rbt[:, 512:192:-1]), reads=[B_rbt], writes=[B_et])
                dma(EREL, et, reads=[B_et], writes=[B_EREL])
                for h in range(16):
                    dma(bass.AP(ZREL.tensor, h * 41024, [[641, 64], [1, 640]]), EREL[h].partition_broadcast(64),
                        reads=[B_EREL], writes=[B_ZREL])
                for h in range(16):
                    src = bass.AP(ZREL.tensor, h * 41024 + 63, [[640, 64], [1, 576]])
                    dma(Bt[(h % 2) * 64:(h % 2 + 1) * 64, h // 2, :], src, reads=[B_ZREL], writes=[B_Bt])
                cidx = 0
                pidx = 0
                qi = -1
                qblk_of = {}
                for (c0, ncs, kind, sidx) in segs:
                    for c in range(c0, c0 + ncs):
                        qb0 = (c // 4) * 4
                        if kind == "s":
                            qb0 = segs[1][0]
                        if qb0 not in qblk_of:
                            qi += 1
                            nqc = min(4, (NPCH if kind == "p" else NPCH + NS) - qb0)
                            qblk_of[qb0] = qi % 2
                            dma(qb[qi % 2][:, :, 0:nqc * 128], QTs[:, :, qb0 * 128:(qb0 + nqc) * 128].rearrange("g p t -> p g t"),
                                reads=[B_QTs], writes=[B_qb[qi % 2]])
                        qsel = qblk_of[qb0]
                        cl = c - qb0
                        bi = cidx % 2
                        cidx += 1
                        if kind == "p":
                            gc = st * NPCH + c
                            j0 = max(0, 8 - gc) * 64
                            dma(KTb[bi], KTs[:, :, gc * 64:gc * 64 + 576].rearrange("h p t -> p h t"), reads=[B_KTs], writes=[B_KTb[bi]])
                            dma(Vb[bi], VSH[gc * 64:gc * 64 + 576, :].rearrange("(c p) n -> p c n", p=64), reads=[B_VSH], writes=[B_Vb[bi]])
                        else:
                            j0 = 0
                            r0 = (8 + NPC + 9 * sidx) * 64
                            dma(KTb[bi], KTss[sidx].rearrange("h p t -> p h t"), reads=[B_KTs], writes=[B_KTb[bi]])
                            dma(Vb[bi], VSH[r0:r0 + 576, :].rearrange("(c p) n -> p c n", p=64), reads=[B_VSH], writes=[B_Vb[bi]])
                        jb0 = j0 // 64
                        for p8 in range(8):
                            n_ = p8 // 2
                            pi = pidx % 2
                            pidx += 1
                            bA, bB = (0, 1) if pi == 0 else (2, 3)
                            bO = 6 + pi
                            lq = qb[qsel][:, p8, cl * 128:(cl + 1) * 128]
                            sm = small[:, pi * 8:(pi + 1) * 8]
                            if j0 < 512:
                                op("pe", lambda lq=lq, bA=bA, bi=bi, n_=n_, j0=j0: PE.matmul(psb[bA][:, j0:512], lhsT=lq, rhs=KTb[bi][:, n_, j0:512],
                                                                                            start=True, stop=True),
                                   reads=[B_qb[qsel], B_KTb[bi]], writes=[PB[bA]])
                            op("pe", lambda lq=lq, bB=bB, bi=bi, n_=n_: PE.matmul(psb[bB][:, 0:64], lhsT=lq, rhs=KTb[bi][:, n_, 512:576],
                                                                                 start=True, stop=True),
                               reads=[B_qb[qsel], B_KTb[bi]], writes=[PB[bB]])
                            if j0 < 512:
                                op("dve", lambda pi=pi, bA=bA, p8=p8, j0=j0: DVE.tensor_tensor(out=sbf[pi][:, j0:512], in0=psb[bA][:, j0:512],
                                                                                              in1=Bt[:, p8, j0:512], op=ALU.add),
                                   reads=[PB[bA], B_Bt], writes=[B_sbf[pi]])
                            op("dve", lambda pi=pi, bB=bB, p8=p8: DVE.tensor_tensor(out=sbf[pi][:, 512:576], in0=psb[bB][:, 0:64],
                                                                                   in1=Bt[:, p8, 512:576], op=ALU.add),
                               reads=[PB[bB], B_Bt], writes=[B_sbf[pi]])
                            op("dve", lambda pi=pi, sm=sm, j0=j0: DVE.reduce_max(out=sm[:, 0:1], in_=sbf[pi][:, j0:576], axis=AX.X),
                               reads=[B_sbf[pi]], writes=[B_sm[pi]])
                            op("dve", lambda sm=sm: DVE.tensor_scalar(out=sm[:, 1:2], in0=sm[:, 0:1], scalar1=-1.0, scalar2=None, op0=ALU.mult),
                               reads=[B_sm[pi]], writes=[B_sm[pi]])
                            op("act", lambda pi=pi, sm=sm, j0=j0: ACT.activation(out=sbf[pi][:, j0:576], in_=sbf[pi][:, j0:576], func=AF.Exp,
                                                                                bias=sm[:, 1:2], accum_out=sm[:, 2:3]),
                               reads=[B_sbf[pi], B_sm[pi]], writes=[B_sbf[pi], B_sm[pi]])
                            op("dve", lambda sm=sm: DVE.reciprocal(out=sm[:, 3:4], in_=sm[:, 2:3]), reads=[B_sm[pi]], writes=[B_sm[pi]])
                            op("dve", lambda pi=pi, sm=sm, j0=j0: DVE.tensor_scalar(out=pn[pi][:, j0:576], in0=sbf[pi][:, j0:576], scalar1=sm[:, 3:4],
                                                                                   scalar2=None, op0=ALU.mult),
                               reads=[B_sbf[pi], B_sm[pi]], writes=[B_pn[pi]])
                            pvA, pvB = psbf(4), psbf(5)
                            for jb in range(jb0, 9):
                                dstp = pvA[0:64, jb * 128:(jb + 1) * 128] if jb < 8 else pvB[0:64, 0:128]
                                bk = 4 if jb < 8 else 5
                                op("pe", lambda pi=pi, jb=jb, dstp=dstp: PE.transpose(out=dstp, in_=pn[pi][:, jb * 64:(jb + 1) * 64], identity=ident[:]),
                                   reads=[B_pn[pi], B_const], writes=[PB[bk]], mark=(jb >= 7))
                            if jb0 < 8:
                                op("act", lambda pi=pi, jb0=jb0: ACT.activation(out=pTs[pi][:, jb0:8, :],
                                                                               in_=pvA[0:64, jb0 * 128:1024].rearrange("p (c t) -> p c t", t=128), func=AF.Copy),
                                   reads=[PB[4]], writes=[B_pTs[pi]])
                            op("act", lambda pi=pi: ACT.activation(out=pTs[pi][:, 8, :], in_=pvB[0:64, 0:128], func=AF.Copy),
                               reads=[PB[5]], writes=[B_pTs[pi]])
                            for jb in range(jb0, 9):
                                op("pe", lambda pi=pi, jb=jb, bi=bi, n_=n_, bO=bO: PE.matmul(psb[bO][:, 0:128], lhsT=Vb[bi][:, jb, n_ * 128:(n_ + 1) * 128],
                                                                                           rhs=pTs[pi][:, jb, :], start=(jb == jb0), stop=(jb == 8)),
                                   reads=[B_Vb[bi], B_pTs[pi]], writes=[PB[bO]], mark=(jb == 8))
                            op("act", lambda p8=p8, c=c, bO=bO: ACT.activation(out=actT[:, 2 * p8:2 * p8 + 2, c * 64:(c + 1) * 64],
                                                                              in_=psb[bO][:, 0:128].rearrange("p (h t) -> p h t", h=2), func=AF.Copy),
                               reads=[PB[bO]], writes=[B_actc[c]])

            Xrows = lambda tt: Xs[tt * 128:(tt + 1) * 128, :]
            BX = lambda tt: [B_Xs[tt]]
            resnorm(lambda tt: xin[xrow(tt):xrow(tt) + 128, :], lambda tt: [], None, None, Xrows, BX, True)
            if dbg is not None and dbg[0] == "h0":
                return "stop"
            for l in range(n_layers):
                if l < 2:
                    phaseA(l)
                    if dbg is not None and dbg[0] == "projA" and l == DL:
                        return "stop"
                    phaseB(l)
                    if dbg is not None and dbg[0] == "scan" and l == DL:
                        return "stop"
                    phaseC(gla_w_o[l], HNx[:, l, :, :].rearrange("p h v -> p (h v)"))
                else:
                    j = l - 2
                    if l == 2:
                        phaseKV()
                    if dbg is not None and dbg[0] == "kv":
                        return "stop"
                    phaseQ(l, j)
                    if dbg is not None and dbg[0] == "q":
                        return "stop"
                    phaseATT(l, j)
                    if dbg is not None and dbg[0] == "att":
                        return "stop"
                    phaseC(att_w_o[j], None)
                resnorm(Xrows, BX, lambda tt: Ms[tt * 128:(tt + 1) * 128, :], norm_gains[l, 1], Xrows, BX, True)
                if dbg is not None and dbg[0] == "mix" and l == DL:
                    return "stop"
                phaseE(l)
                phaseF(l)
                fin = (l == n_layers - 1)
                if fin:
                    resnorm(Xrows, BX, lambda tt: Ms[tt * 128:(tt + 1) * 128, :], norm_gains[l, 3],
                            lambda tt: yout[xrow(tt):xrow(tt) + 128, :], lambda tt: [], False, is_out=True)
                else:
                    resnorm(Xrows, BX, lambda tt: Ms[tt * 128:(tt + 1) * 128, :], norm_gains[l, 3], Xrows, BX, True)
            return None

        for st in range(n_st):
            r = run_supertile(st)
            if r == "stop":
                break
        if dbg is not None:
            kb.fence()
            Bt = Buf()
            nt_dbg = dbg[1][0] // 128
            if dbg[0] == "projA":
                tb = cvb(WOFF, 6144)
                for tt in range(nt_dbg):
                    dma(tb, PROJ[tt * 128:(tt + 1) * 128, :], reads=[B_PROJ[tt]], writes=[Bt])
                    dma(dbg_out[tt * 128:(tt + 1) * 128, :], tb, reads=[Bt], is_out=True)
            elif dbg[0] in ("h0", "scan", "att"):
                dma(dbg_out.rearrange("k p t -> p k t"), actT[:, :, 0:dbg[1][2]], reads=B_actc, is_out=True)
            elif dbg[0] == "mix":
                tb = cvf(WOFF, 2048)
                for tt in range(nt_dbg):
                    dma(tb, Xs[tt * 128:(tt + 1) * 128, :], reads=[B_Xs[tt]], writes=[Bt])
                    dma(dbg_out[tt * 128:(tt + 1) * 128, :], tb, reads=[Bt], is_out=True)
        kb.finish()
    return nc


WEIGHT_KEYS = ["norm_gains", "gla_w_in", "gla_w_gate", "gla_b_gate", "gla_head_norm", "gla_w_o", "kv_norm",
               "att_w_kv", "att_w_q", "att_rel_bias", "att_w_o", "ffn_w_up", "ffn_conv_w", "ffn_conv_b", "ffn_w_down"]


def make_in_maps(inp, npt):
    f = lambda a: np.ascontiguousarray(np.asarray(a, dtype=np.float32))
    w = {k: f(inp[k]) for k in WEIGHT_KEYS}
    xp, xs = f(inp["x_prompt"]), f(inp["x_sample"])
    sg, sc = f(inp["state_gla"]), f(inp["state_ffn_conv"])
    ck, cv = f(inp["cache_k"]), f(inp["cache_v"])
    maps = []
    for c in range(N_CORES):
        xin = np.zeros((npt + 128, D), np.float32)
        if c < xp.shape[0]:
            xin[:npt] = xp[c, :npt]
        xin[npt:] = xs[2 * c:2 * c + 2].reshape(128, D)
        m = dict(w)
        m["xin"] = xin
        m["gla_s0"] = np.ascontiguousarray(sg[:, 2 * c:2 * c + 2])
        m["conv_s0"] = np.ascontiguousarray(sc[:, 2 * c:2 * c + 2])
        m["ck"] = np.ascontiguousarray(ck[2 * c:2 * c + 2].reshape(2, 512, 512))
        m["cv"] = np.ascontiguousarray(cv[2 * c:2 * c + 2].reshape(2, 512, 512))
        maps.append(m)
    return maps


_NC_CACHE = {}


def kernel(**inputs):
    if "full" not in _NC_CACHE:
        _NC_CACHE["full"] = build_program()
    nc = _NC_CACHE["full"]
    maps = make_in_maps(inputs, SEQ)
    res = run_bass_kernel_spmd(nc, maps, core_ids=list(range(N_CORES))).results
    B = 4
    y_p = np.stack([res[c]["yout"][:SEQ] for c in range(B)])
    y_s = np.concatenate([res[c]["yout"][SEQ:].reshape(2, 64, D) for c in range(N_CORES)])
    gla_p = np.stack([res[c]["o_gla_p"] for c in range(B)], axis=1)
    gla_s = np.concatenate([res[c]["o_gla_s"] for c in range(N_CORES)], axis=1)
    conv_p = np.stack([res[c]["o_conv_p"] for c in range(B)], axis=1)
    conv_s = np.concatenate([res[c]["o_conv_s"] for c in range(N_CORES)], axis=1)
    k_p = np.stack([res[c]["o_k_p"].reshape(512, 4, 128) for c in range(B)])
    v_p = np.stack([res[c]["o_v_p"].reshape(512, 4, 128) for c in range(B)])
    k_s = np.concatenate([res[c]["o_k_s"].reshape(2, 64, 4, 128) for c in range(N_CORES)])
    v_s = np.concatenate([res[c]["o_v_s"].reshape(2, 64, 4, 128) for c in range(N_CORES)])
    outs = (y_p, y_s, gla_p, gla_s, conv_p, conv_s, k_p, v_p, k_s, v_s)
    return tuple(np.ascontiguousarray(o, dtype=np.float32) for o in outs)
```

```python
import numpy as np
from contextlib import ExitStack
import concourse.bass as bass
import concourse.mybir as mybir
from concourse.bass_utils import run_bass_kernel_spmd

F32 = mybir.dt.float32
BF16 = mybir.dt.bfloat16
AF = mybir.ActivationFunctionType
ALU = mybir.AluOpType
AX = mybir.AxisListType

D = 2048
KC = 16
DFF = 5632
FC = 44
GLA_IN = 6160
EPS = 1e-6
N_CORES = 8
SEQ = 4096
NPH = 2048
EPOCH = 16000


class Buf:
    __slots__ = ("w", "r", "name")

    def __init__(self, name=""):
        self.w = None
        self.r = {}
        self.name = name


class KB:
    def __init__(self, nc, es):
        self.nc = nc
        self.es = es
        self.eng = dict(pe=nc.tensor, act=nc.scalar, dve=nc.vector, pool=nc.gpsimd, sp=nc.sync)
        self.sems = {}
        self.cnt = {}
        self.epoch = {}
        for e in ("pe", "act", "dve", "pool"):
            self.epoch[e] = 0
            self._newsem(self._ekey(e))
        self.ndma = 24
        self.dkeys = []
        for j in range(self.ndma):
            k = "d%d_0" % j
            self._newsem(k)
            self.dkeys.append(k)
        self.dma_rr = 0
        self.waited = {e: {} for e in self.eng}
        self.out_events = []
        self.fence_evs = []

    def _ekey(self, e):
        return "%s_%d" % (e, self.epoch[e])

    def _newsem(self, key):
        self.sems[key] = self.es.enter_context(self.nc.semaphore("s_" + key))
        self.cnt[key] = 0

    def fence(self):
        self.fence_evs = [(k, c) for k, c in self.cnt.items() if c > 0]

    def _wait(self, e, evs):
        best = {}
        for (k, v) in list(evs) + self.fence_evs:
            if v > best.get(k, 0):
                best[k] = v
        for k, v in best.items():
            if e == "pe" and k.startswith("pe_"):
                continue
            if self.waited[e].get(k, 0) >= v:
                continue
            if e != "sp" and k == self._ekey(e) and v > self.cnt[k]:
                continue
            self.eng[e].wait_ge(self.sems[k], v)
            self.waited[e][k] = v

    @staticmethod
    def _deps(reads, writes):
        evs = []
        for b in reads:
            if b.w is not None:
                evs.append(b.w)
        for b in writes:
            if b.w is not None:
                evs.append(b.w)
            evs.extend(b.r.items())
        return evs

    @staticmethod
    def _upd(ev, reads, writes):
        for b in reads:
            if ev[1] > b.r.get(ev[0], 0):
                b.r[ev[0]] = ev[1]
        for b in writes:
            b.w = ev
            b.r = {}

    def op(self, e, fn, reads=(), writes=(), mark=True):
        self._wait(e, self._deps(reads, writes))
        ins = fn()
        key = self._ekey(e)
        ev = (key, self.cnt[key] + 1)
        if mark:
            ins.then_inc(self.sems[key], 1)
            self.cnt[key] += 1
            if self.cnt[key] >= EPOCH:
                self.epoch[e] += 1
                self._newsem(self._ekey(e))
        self._upd(ev, reads, writes)
        return ins

    def dma(self, out_ap, in_ap, reads=(), writes=(), q="sp", is_out=False, slow=False):
        j = self.dma_rr
        self.dma_rr = (j + 1) % self.ndma
        key = self.dkeys[j]
        if self.cnt[key] >= EPOCH:
            nk = "d%d_%d" % (j, int(key.split("_")[1]) + 1)
            self._wait(q, [(key, self.cnt[key])])
            self._newsem(nk)
            self.dkeys[j] = nk
            key = nk
        evs = self._deps(reads, writes)
        if self.cnt[key] > 0:
            evs.append((key, self.cnt[key]))
        self._wait(q, evs)
        if slow:
            self.eng[q].dma_start(out=out_ap, in_=in_ap, allow_slow_non_contiguous=True).then_inc(self.sems[key], 16)
        else:
            self.eng[q].dma_start(out=out_ap, in_=in_ap).then_inc(self.sems[key], 16)
        self.cnt[key] += 16
        ev = (key, self.cnt[key])
        self._upd(ev, reads, writes)
        if is_out:
            self.out_events.append(ev)
        return ev

    def finish(self):
        evs = list(self.out_events)
        for k in self.dkeys:
            if self.cnt[k] > 0:
                evs.append((k, self.cnt[k]))
        self._wait("sp", evs)


def build_program(n_layers=4, np_tok=NPH, n_st=2, dbg=None):
    nc = bass.Bass("TRN2", target_bir_lowering=False)
    NPT = np_tok * n_st
    XR = NPT + 128
    dt = nc.dram_tensor

    def din(name, shape, dtype=F32):
        return dt(name, list(shape), dtype, kind="ExternalInput").ap()

    def dout(name, shape, dtype=F32):
        return dt(name, list(shape), dtype, kind="ExternalOutput").ap()

    def dscr(name, shape, dtype):
        return dt(name, list(shape), dtype, kind="Internal").ap()

    xin = din("xin", [XR, D])
    gla_s0 = din("gla_s0", [2, 2, 4, 256, 512])
    conv_s0 = din("conv_s0", [4, 2, 2, DFF])
    ck = din("ck", [2, 512, 512])
    cv = din("cv", [2, 512, 512])
    norm_gains = din("norm_gains", [4, 4, D])
    gla_w_in = din("gla_w_in", [2, D, GLA_IN])
    gla_w_gate = din("gla_w_gate", [2, 16, 1024])
    gla_b_gate = din("gla_b_gate", [2, 1024])
    gla_head_norm = din("gla_head_norm", [2, 512])
    gla_w_o = din("gla_w_o", [2, D, D])
    kv_norm = din("kv_norm", [D])
    att_w_kv = din("att_w_kv", [D, 1024])
    att_w_q = din("att_w_q", [2, D, D])
    att_rel_bias = din("att_rel_bias", [2, 16, 513])
    att_w_o = din("att_w_o", [2, D, D])
    ffn_w_up = din("ffn_w_up", [4, D, 2 * DFF])
    ffn_conv_w = din("ffn_conv_w", [4, 3, DFF])
    ffn_conv_b = din("ffn_conv_b", [4, DFF])
    ffn_w_down = din("ffn_w_down", [4, DFF, D])
    yout = dout("yout", [XR, D])
    o_gla_p = dout("o_gla_p", [2, 4, 256, 512])
    o_gla_s = dout("o_gla_s", [2, 2, 4, 256, 512])
    o_conv_p = dout("o_conv_p", [4, 2, DFF])
    o_conv_s = dout("o_conv_s", [4, 2, 2, DFF])
    o_k_p = dout("o_k_p", [512, 512])
    o_v_p = dout("o_v_p", [512, 512])
    o_k_s = dout("o_k_s", [128, 512])
    o_v_s = dout("o_v_s", [128, 512])
    TMAX = np_tok + 128
    Xs = dscr("Xs", [TMAX, D], F32)
    Ms = dscr("Ms", [TMAX, D], F32)
    PROJ = dscr("PROJ", [TMAX, 6144], BF16)
    GLs = dscr("GLs", [16, TMAX], F32)
    ATs = dscr("ATs", [FC, 128, TMAX], BF16)
    CARRY_S = dscr("CARRY_S", [2, 128, 8, 512], F32)
    NPC = NPT // 64
    VSH = dscr("VSH", [(8 + NPC + 18) * 64, 512], BF16)
    KTs = dscr("KTs", [4, 128, 512 + NPT], BF16)
    KTss = dscr("KTss", [2, 4, 128, 576], BF16)
    QTs = dscr("QTs", [8, 128, 2 * TMAX], BF16)
    EREL = dscr("EREL", [16, 640], F32)
    ZREL = dscr("ZREL", [16, 41024], F32)
    dbg_out = None
    DL = 0
    if dbg is not None:
        if ':' in dbg[0]:
            DL = int(dbg[0].split(':')[1])
            dbg = (dbg[0].split(':')[0],) + tuple(dbg[1:])
        dbg_out = dout("dbg", dbg[1], dbg[2])

    es = ExitStack()
    with es:
        kb = KB(nc, es)
        op, dma = kb.op, kb.dma
        PE, ACT, DVE, POOL = nc.tensor, nc.scalar, nc.vector, nc.gpsimd

        def sb(name, shape, dtype=F32):
            return es.enter_context(nc.sbuf_tensor(name, list(shape), dtype))

        psb = [es.enter_context(nc.psum_tensor("ps%d" % i, [128, 512], F32)) for i in range(8)]
        PB = [Buf("ps%d" % i) for i in range(8)]

        def psbf(i):
            return psb[i][:].bitcast(BF16)

        identf = sb("identf", [128, 128], F32)
        ident = sb("ident", [128, 128], BF16)
        c_one = sb("c_one", [128, 1], F32)
        c_eps = sb("c_eps", [128, 1], F32)
        c_ln16 = sb("c_ln16", [128, 1], F32)
        triI = sb("triI", [64, 64], F32)
        triU = sb("triU", [64, 64], F32)
        maskT = sb("maskT", [64, 4, 64], F32)
        B_const = Buf("const")

        def cst(fn):
            op("pool", fn, writes=[B_const])

        cst(lambda: POOL.memset(identf[:], 0.0))
        cst(lambda: POOL.affine_select(out=identf[:], in_=identf[:], pattern=[[-1, 128]], compare_op=ALU.not_equal,
                                       fill=1.0, base=0, channel_multiplier=1))
        cst(lambda: POOL.tensor_copy(out=ident[:], in_=identf[:]))
        cst(lambda: POOL.memset(c_one[:], 1.0))
        cst(lambda: POOL.memset(c_eps[:], EPS))
        cst(lambda: POOL.memset(c_ln16[:], float(np.log(1.0 / 16.0))))
        cst(lambda: POOL.memset(triI[:], -1.0 / 16.0))
        cst(lambda: POOL.affine_select(out=triI[:], in_=triI[:], pattern=[[1, 64]], compare_op=ALU.is_ge, fill=0.0,
                                       base=0, channel_multiplier=-1))
        cst(lambda: POOL.memset(triU[:], -1.0 / 16.0))
        cst(lambda: POOL.affine_select(out=triU[:], in_=triU[:], pattern=[[-1, 64]], compare_op=ALU.is_gt, fill=0.0,
                                       base=0, channel_multiplier=1))
        cst(lambda: POOL.memset(maskT[:], 1.0))
        cst(lambda: POOL.affine_select(out=maskT[:], in_=maskT[:], pattern=[[0, 4], [1, 64]], compare_op=ALU.is_ge,
                                       fill=0.0, base=0, channel_multiplier=-1))

        NG = sb("NG", [128, 256], F32)
        KVN = sb("KVN", [128, 16], F32)
        HNx = sb("HNx", [128, 2, 4, 4], F32)
        CG = sb("CG", [128, 4, FC, 2], F32)
        prm = sb("prm", [128, 2, 128], F32)
        prm2 = sb("prm2", [24, 128], F32)
        B_prm = Buf("prm")
        B_par = Buf("par")
        B_CG = Buf("cg")
        dma(prm[:], norm_gains.rearrange("l i (k p) -> (l i k) p", p=128).rearrange("(a r) p -> r a p", r=128),
            writes=[B_prm])
        dma(prm2[0:16, :], kv_norm.rearrange("(k p) -> k p", p=128), writes=[B_prm])
        dma(prm2[16:24, :], gla_head_norm.rearrange("l (k p) -> (l k) p", p=128), writes=[B_prm])
        for a in range(2):
            op("pe", lambda a=a: PE.transpose(out=psb[0][:, a * 128:(a + 1) * 128], in_=prm[:, a, :], identity=identf[:]),
               reads=[B_prm, B_const], writes=[PB[0]])
        op("pe", lambda: PE.transpose(out=psb[0][:, 256:280], in_=prm2[0:24, :], identity=identf[0:24, 0:24]),
           reads=[B_prm, B_const], writes=[PB[0]])
        op("dve", lambda: DVE.tensor_copy(out=NG[:], in_=psb[0][:, 0:256]), reads=[PB[0]], writes=[B_par])
        op("dve", lambda: DVE.tensor_copy(out=KVN[:], in_=psb[0][:, 256:272]), reads=[PB[0]], writes=[B_par])
        for l_ in range(2):
            op("dve", lambda l_=l_: DVE.tensor_copy(
                out=HNx[:, l_, :, :],
                in_=psb[0][:, 272 + l_ * 4:276 + l_ * 4].unsqueeze(1).broadcast_to([128, 4, 4])),
                reads=[PB[0]], writes=[B_par])
        op("pool", lambda: POOL.memset(CG[:], 0.0), writes=[B_CG])

        ACT_F = max(KC * TMAX // 2, 17408)
        WORK_F = 19456
        STG_F = 3 * 4096
        big = sb("big", [128, ACT_F + WORK_F + STG_F], F32)
        WOFF = ACT_F
        SOFF = ACT_F + WORK_F

        def cvf(off, n):
            return big[:, off:off + n]

        def cvb(off, n):
            return big[:, off:off + n // 2].bitcast(BF16)

        actT = cvb(0, KC * TMAX).rearrange("p (k t) -> p k t", k=KC)
        B_actc = [Buf("act%d" % c) for c in range(TMAX // 64)]
        stg = [cvf(SOFF + i * 4096, 4096).rearrange("p (k n) -> p k n", k=8) for i in range(3)]
        B_stg = [Buf("stg%d" % i) for i in range(3)]
        stg_rr = [0]
        cast_rr = [0]

        NTM = TMAX // 128
        B_Xs = [Buf() for _ in range(NTM)]
        B_Ms = [Buf() for _ in range(NTM)]
        B_PROJ = [Buf() for _ in range(NTM)]
        B_GLs = Buf()
        B_ATs = Buf()
        B_CS = [Buf(), Buf()]
        B_VSH = Buf()
        B_KTs = Buf()
        B_QTs = Buf()
        B_EREL = Buf()
        B_ZREL = Buf()

        def actbufs(tt):
            return [B_actc[2 * tt], B_actc[2 * tt + 1]]

        def load_wblock(wb_ap, B_wb, Wrows, kcs, c0, ncols, gain=None):
            for k0 in range(0, kcs, 8):
                nk = min(8, kcs - k0)
                i = stg_rr[0]
                stg_rr[0] = (i + 1) % 3
                src = Wrows[k0 * 128:(k0 + nk) * 128, c0:c0 + ncols].rearrange("(k p) n -> p k n", p=128)
                dma(stg[i][:, 0:nk, 0:ncols], src, writes=[B_stg[i]])
                ce = "act" if cast_rr[0] % 2 == 0 else "dve"
                cast_rr[0] += 1
                if gain is None:
                    if ce == "act":
                        op("act", lambda i=i, k0=k0, nk=nk: ACT.activation(out=wb_ap[:, k0:k0 + nk, 0:ncols],
                                                                          in_=stg[i][:, 0:nk, 0:ncols], func=AF.Copy),
                           reads=[B_stg[i]], writes=[B_wb])
                    else:
                        op("dve", lambda i=i, k0=k0, nk=nk: DVE.tensor_copy(out=wb_ap[:, k0:k0 + nk, 0:ncols],
                                                                           in_=stg[i][:, 0:nk, 0:ncols]),
                           reads=[B_stg[i]], writes=[B_wb])
                else:
                    for j in range(nk):
                        if ce == "act":
                            op("act", lambda i=i, j=j, k0=k0: ACT.activation(
                                out=wb_ap[:, k0 + j, 0:ncols], in_=stg[i][:, j, 0:ncols], func=AF.Copy,
                                scale=gain[:, k0 + j:k0 + j + 1]),
                                reads=[B_stg[i], B_par], writes=[B_wb], mark=(j == nk - 1))
                        else:
                            op("dve", lambda i=i, j=j, k0=k0: DVE.tensor_scalar(
                                out=wb_ap[:, k0 + j, 0:ncols], in0=stg[i][:, j, 0:ncols],
                                scalar1=gain[:, k0 + j:k0 + j + 1], scalar2=None, op0=ALU.mult),
                                reads=[B_stg[i], B_par], writes=[B_wb], mark=(j == nk - 1))

        def run_supertile(st):
            NS = 2 if st == 0 else 0
            T = np_tok + 64 * NS
            NT = T // 128
            NPCH = np_tok // 64
            first = (st == 0)
            last = (st == n_st - 1)
            tblocks = [(t0, min(512, np_tok - t0)) for t0 in range(0, np_tok, 512)]
            if NS:
                tblocks.append((np_tok, 128))
            segs = [(0, NPCH, "p", 0)] + [(NPCH + i, 1, "s", i) for i in range(NS)]

            def xrow(tt):
                if tt < np_tok // 128:
                    return st * np_tok + tt * 128
                return NPT

            def resnorm(x_src, Bx_src, m_src, gain_row, x_dst, Bx_dst, want_hT, is_out=False):
                kb.fence()
                W = WOFF
                xt = [cvf(W + 0, 2048), cvf(W + 2048, 2048)]
                mt = [cvf(W + 4096, 2048), cvf(W + 6144, 2048)]
                gB = cvf(W + 8192, 2048)
                xb = cvb(W + 10240, 2048)
                junk = cvb(W + 11264, 2048)
                small = cvf(W + 12288, 32)
                B_xt = [Buf(), Buf()]
                B_mt = [Buf(), Buf()]
                B_gb, B_xb, B_jk, B_sm = Buf(), Buf(), Buf(), [Buf(), Buf()]
                if m_src is not None:
                    dma(gB, gain_row.partition_broadcast(128), writes=[B_gb])
                for tt in range(NT):
                    p = tt % 2
                    sm = small[:, p * 8:(p + 1) * 8]
                    dma(xt[p], x_src(tt), reads=Bx_src(tt), writes=[B_xt[p]])
                    if m_src is not None:
                        dma(mt[p], m_src(tt), reads=[B_Ms[tt]], writes=[B_mt[p]])
                        op("act", lambda p=p, sm=sm: ACT.activation(out=junk, in_=mt[p], func=AF.Square,
                                                                     accum_out=sm[:, 0:1]),
                           reads=[B_mt[p]], writes=[B_jk, B_sm[p]])
                        op("act", lambda sm=sm: ACT.activation(out=sm[:, 1:2], in_=sm[:, 0:1], func=AF.Sqrt,
                                                               scale=1.0 / D, bias=c_eps[:]),
                           reads=[B_sm[p], B_const], writes=[B_sm[p]])
                        op("dve", lambda sm=sm: DVE.reciprocal(out=sm[:, 2:3], in_=sm[:, 1:2]),
                           reads=[B_sm[p]], writes=[B_sm[p]])
                        op("dve", lambda p=p, sm=sm: DVE.scalar_tensor_tensor(
                            out=mt[p], in0=mt[p], scalar=sm[:, 2:3], in1=gB, op0=ALU.mult, op1=ALU.mult),
                            reads=[B_sm[p], B_gb, B_mt[p]], writes=[B_mt[p]])
                        op("dve", lambda p=p: DVE.tensor_tensor(out=xt[p], in0=xt[p], in1=mt[p], op=ALU.add),
                           reads=[B_mt[p], B_xt[p]], writes=[B_xt[p]])
                    if x_dst is not None:
                        dma(x_dst(tt), xt[p], reads=[B_xt[p]], writes=Bx_dst(tt), is_out=is_out)
                    if want_hT:
                        op("act", lambda p=p, sm=sm: ACT.activation(out=junk, in_=xt[p], func=AF.Square,
                                                                     accum_out=sm[:, 3:4]),
                           reads=[B_xt[p]], writes=[B_jk, B_sm[p]])
                        op("act", lambda sm=sm: ACT.activation(out=sm[:, 4:5], in_=sm[:, 3:4], func=AF.Sqrt,
                                                               scale=1.0 / D, bias=c_eps[:]),
                           reads=[B_sm[p], B_const], writes=[B_sm[p]])
                        op("dve", lambda sm=sm: DVE.reciprocal(out=sm[:, 5:6], in_=sm[:, 4:5]),
                           reads=[B_sm[p]], writes=[B_sm[p]])
                        op("act", lambda p=p, sm=sm: ACT.activation(out=xb, in_=xt[p], func=AF.Copy, scale=sm[:, 5:6]),
                           reads=[B_sm[p], B_xt[p]], writes=[B_xb])
                        for g in range(2):
                            bank = 6 + g
                            pv = psbf(bank)
                            for j in range(8):
                                kc = g * 8 + j
                                op("pe", lambda kc=kc, j=j, pv=pv: PE.transpose(
                                    out=pv[:, j * 128:(j + 1) * 128], in_=xb[:, kc * 128:(kc + 1) * 128],
                                    identity=ident[:]),
                                    reads=[B_xb, B_const], writes=[PB[bank]], mark=(j == 7))
                            dst = actT[:, g * 8:(g + 1) * 8, tt * 128:(tt + 1) * 128]
                            srcv = pv.rearrange("p (k t) -> p k t", k=8)
                            if g == 0:
                                op("dve", lambda dst=dst, srcv=srcv: DVE.tensor_copy(out=dst, in_=srcv),
                                   reads=[PB[bank]], writes=actbufs(tt))
                            else:
                                op("act", lambda dst=dst, srcv=srcv: ACT.activation(out=dst, in_=srcv, func=AF.Copy),
                                   reads=[PB[bank]], writes=actbufs(tt))

            def gemm_setup():
                kb.fence()
                wbs = [cvb(WOFF + i * 4096, 8192).rearrange("p (k n) -> p k n", k=16) for i in range(2)]
                otf = [cvf(WOFF + 8192 + i * 512, 512) for i in range(4)]
                return wbs, [Buf(), Buf()], otf, [Buf() for _ in range(4)]

            def gemm_tok(Wrows, col_blocks, gain, evac, extra=None):
                wbs, B_wb, otf, B_ot = gemm_setup()
                rr = [0, 0]

                def next_ot():
                    i = rr[1]
                    rr[1] = (i + 1) % 4
                    return i, otf[i], B_ot[i]

                load_wblock(wbs[0], B_wb[0], Wrows, KC, col_blocks[0][0], col_blocks[0][1], gain)
                for bi, (c0, ncols) in enumerate(col_blocks):
                    cur = bi % 2
                    if bi + 1 < len(col_blocks):
                        load_wblock(wbs[1 - cur], B_wb[1 - cur], Wrows, KC, col_blocks[bi + 1][0],
                                    col_blocks[bi + 1][1], gain)
                    if extra is not None and extra(bi, c0, ncols, wbs[cur], B_wb[cur], next_ot):
                        continue
                    for tt in range(NT):
                        bank = rr[0] % 4
                        rr[0] += 1
                        for kc in range(KC):
                            op("pe", lambda kc=kc, tt=tt, bank=bank, cur=cur, ncols=ncols: PE.matmul(
                                psb[bank][:, 0:ncols], lhsT=actT[:, kc, tt * 128:(tt + 1) * 128],
                                rhs=wbs[cur][:, kc, 0:ncols], start=(kc == 0), stop=(kc == KC - 1)),
                                reads=actbufs(tt) + [B_wb[cur]], writes=[PB[bank]], mark=(kc == KC - 1))
                        evac(bi, c0, ncols, tt, bank, next_ot)

            def mm_feat(bank, wb, B_wb, f0, m, t0, n):
                c0_ = t0 // 64
                rb_ = [B_actc[c] for c in range(c0_, (t0 + n) // 64)]
                for kc in range(KC):
                    op("pe", lambda kc=kc: PE.matmul(psb[bank][0:m, 0:n], lhsT=wb[:, kc, f0:f0 + m],
                                                     rhs=actT[:, kc, t0:t0 + n], start=(kc == 0), stop=(kc == KC - 1)),
                       reads=rb_ + [B_wb], writes=[PB[bank]], mark=(kc == KC - 1))

            def evac_to_Ms(bi, c0, ncols, tt, bank, next_ot):
                i, o, Bo = next_ot()
                op("act", lambda o=o, bank=bank: ACT.activation(out=o, in_=psb[bank][:, 0:512], func=AF.Copy),
                   reads=[PB[bank]], writes=[Bo])
                dma(Ms[tt * 128:(tt + 1) * 128, c0:c0 + 512], o, reads=[Bo], writes=[B_Ms[tt]], q="act")

            def phaseA(l):
                gain = NG[:, (l * 4 + 0) * 16:(l * 4 + 0) * 16 + 16]
                import os
                blocks = [(c, 512) for c in range(0, 6144, 512)] + [(6144, 16)]
                if os.environ.get("DBG_NOG"): blocks = blocks[:int(os.environ["DBG_NOG"])]

                def evacA(bi, c0, ncols, tt, bank, next_ot):
                    i, o_, Bo = next_ot()
                    o = o_.bitcast(BF16)[:, 0:512]
                    fn = AF.Silu if c0 >= 4096 else AF.Copy
                    op("act", lambda o=o, bank=bank, fn=fn: ACT.activation(out=o, in_=psb[bank][:, 0:512], func=fn),
                       reads=[PB[bank]], writes=[Bo])
                    dma(PROJ[tt * 128:(tt + 1) * 128, c0:c0 + 512], o, reads=[Bo], writes=[B_PROJ[tt]], q="act")

                def extraA(bi, c0, ncols, wb, B_wb, next_ot):
                    if c0 != 6144:
                        return False
                    for (t0, n) in tblocks:
                        bank = 4
                        mm_feat(bank, wb, B_wb, 0, 16, t0, n)
                        i, o, Bo = next_ot()
                        op("act", lambda o=o, n=n: ACT.activation(out=o[0:16, 0:n], in_=psb[4][0:16, 0:n], func=AF.Copy),
                           reads=[PB[4]], writes=[Bo])
                        dma(GLs[:, t0:t0 + n], o[0:16, 0:n], reads=[Bo], writes=[B_GLs])
                    return True

                gemm_tok(gla_w_in[l], blocks, gain, evacA, extraA)

            def phaseB(l):
                kb.fence()
                W = WOFF
                Pb = [cvb(SOFF + i * 3072, 6144)[0:64, :] for i in range(3)]
                B_P = [Buf() for _ in range(3)]
                S = cvf(W + 0, 4096).rearrange("p (k v) -> p k v", k=8)
                Sb = cvb(W + 4096, 4096).rearrange("p (k v) -> p k v", k=8)
                sp = cvf(W + 6144, 1024)[0:64, :]
                XD = cvf(W + 7168, 1024)[0:64, :]
                EQ = cvf(W + 8192, 512).rearrange("p (k t) -> p k t", k=8)
                EK = cvf(W + 8704, 512).rearrange("p (k t) -> p k t", k=8)
                qt = cvb(W + 9216, 512).rearrange("p (k t) -> p k t", k=8)
                kt = cvb(W + 9472, 512).rearrange("p (k t) -> p k t", k=8)
                kp = cvb(W + 9728, 1024)[0:64, :]
                am = cvb(W + 10240, 256)[0:64, :].rearrange("p (h t) -> p h t", h=4)
                osb = [cvf(W + 10368 + i * 512, 512)[0:64, :] for i in range(2)]
                on = cvb(W + 11392, 2048)[0:64, :]
                small = cvf(W + 12416, 32)
                junk = cvb(W + 12448, 512)[0:64, :]
                WG = cvf(W + 12704, 1024)[0:17, :]
                GLc = [cvf(W + 13728 + i * 64, 64)[0:17, :] for i in range(2)]
                Eend = cvf(W + 13856, 8)
                B_S, B_Sb, B_sp, B_XD, B_EQ, B_EK, B_qt, B_kt, B_kp, B_am = [Buf() for _ in range(10)]
                B_osb = [Buf(), Buf()]
                B_on, B_small, B_junk, B_WG, B_Eend = Buf(), Buf(), Buf(), Buf(), Buf()
                B_GLc = [Buf(), Buf()]
                dma(WG[0:16, :], gla_w_gate[l], writes=[B_WG])
                dma(WG[16:17, :], gla_b_gate[l].unsqueeze(0), writes=[B_WG])
                for i in range(2):
                    op("pool", lambda i=i: POOL.memset(GLc[i], 1.0), writes=[B_GLc[i]])
                cidx = 0
                for (c0, ncs, kind, sidx) in segs:
                    if kind == "p":
                        if first:
                            op("pool", lambda: POOL.memset(S, 0.0), writes=[B_S])
                        else:
                            dma(S, CARRY_S[l], reads=[B_CS[l]], writes=[B_S])
                    else:
                        dma(S, gla_s0[l, sidx].rearrange("h (j p) v -> p (h j) v", p=128), writes=[B_S])
                    op("act", lambda: ACT.activation(out=Sb, in_=S, func=AF.Copy), reads=[B_S], writes=[B_Sb])
                    for c in range(c0, c0 + ncs):
                        pi = cidx % 3
                        gi = cidx % 2
                        cidx += 1
                        P = Pb[pi]
                        dma(P, PROJ[c * 64:(c + 1) * 64, :], reads=[B_PROJ[c // 2]], writes=[B_P[pi]])
                        dma(GLc[gi][0:16, :], GLs[:, c * 64:(c + 1) * 64], reads=[B_GLs], writes=[B_GLc[gi]])
                        for hf in range(2):
                            op("pe", lambda hf=hf, gi=gi: PE.matmul(psb[hf][0:64, :], lhsT=GLc[gi], rhs=WG[:, hf * 512:(hf + 1) * 512],
                                                                    start=True, stop=True),
                               reads=[B_GLc[gi], B_WG], writes=[PB[hf]])
                            op("act", lambda hf=hf: ACT.activation(out=sp[:, hf * 512:(hf + 1) * 512], in_=psb[hf][0:64, :],
                                                                   func=AF.Exp, scale=-1.0),
                               reads=[PB[hf]], writes=[B_sp])
                        op("act", lambda: ACT.activation(out=sp, in_=sp, func=AF.Ln, bias=c_one[0:64, :]),
                           reads=[B_sp, B_const], writes=[B_sp])
                        bps = psb[2][:].rearrange("p (k t) -> p k t", k=8)
                        for kc in range(8):
                            op("pe", lambda kc=kc: PE.matmul(bps[:, kc, :], lhsT=sp[:, kc * 128:(kc + 1) * 128], rhs=triI[:],
                                                             start=True, stop=True),
                               reads=[B_sp, B_const], writes=[PB[2]], mark=(kc == 7))
                        for hf in range(2):
                            op("pe", lambda hf=hf: PE.matmul(psb[hf][0:64, :], lhsT=triU[:], rhs=sp[:, hf * 512:(hf + 1) * 512],
                                                             start=True, stop=True),
                               reads=[B_sp, B_const], writes=[PB[hf]])
                            op("act", lambda hf=hf: ACT.activation(out=XD[:, hf * 512:(hf + 1) * 512], in_=psb[hf][0:64, :],
                                                                   func=AF.Exp),
                               reads=[PB[hf]], writes=[B_XD])
                        op("act", lambda: ACT.activation(out=EQ, in_=bps, func=AF.Exp, bias=c_ln16[:]),
                           reads=[PB[2], B_const], writes=[B_EQ])
                        op("act", lambda: ACT.activation(out=EK, in_=bps, func=AF.Exp, scale=-1.0),
                           reads=[PB[2]], writes=[B_EK])
                        op("act", lambda: ACT.activation(out=Eend, in_=bps[:, :, 63], func=AF.Exp),
                           reads=[PB[2]], writes=[B_Eend])
                        qv = psbf(3)[:, 0:512].rearrange("p (k t) -> p k t", k=8)
                        kv_ = psbf(3)[:, 512:1024].rearrange("p (k t) -> p k t", k=8)
                        for kc in range(8):
                            op("pe", lambda kc=kc: PE.transpose(out=qv[:, kc, :], in_=P[:, kc * 128:(kc + 1) * 128],
                                                                identity=ident[0:64, 0:64]),
                               reads=[B_P[pi], B_const], writes=[PB[3]], mark=False)
                        for kc in range(8):
                            op("pe", lambda kc=kc: PE.transpose(out=kv_[:, kc, :], in_=P[:, 1024 + kc * 128:1024 + (kc + 1) * 128],
                                                                identity=ident[0:64, 0:64]),
                               reads=[B_P[pi], B_const], writes=[PB[3]], mark=(kc == 7))
                        op("dve", lambda: DVE.tensor_tensor(out=qt, in0=qv, in1=EQ, op=ALU.mult),
                           reads=[PB[3], B_EQ], writes=[B_qt])
                        op("dve", lambda: DVE.tensor_tensor(out=kt, in0=kv_, in1=EK, op=ALU.mult),
                           reads=[PB[3], B_EK], writes=[B_kt])
                        op("dve", lambda: DVE.tensor_tensor(out=kp, in0=P[:, 1024:2048], in1=XD, op=ALU.mult),
                           reads=[B_P[pi], B_XD], writes=[B_kp])
                        aps = psb[4][0:64, 0:256].rearrange("p (h t) -> p h t", h=4)
                        for h in range(4):
                            for j in range(2):
                                op("pe", lambda h=h, j=j: PE.matmul(aps[:, h, :], lhsT=kt[:, 2 * h + j, :], rhs=qt[:, 2 * h + j, :],
                                                                    start=(j == 0), stop=(j == 1)),
                                   reads=[B_kt, B_qt], writes=[PB[4]], mark=(h == 3 and j == 1))
                        op("dve", lambda: DVE.tensor_tensor(out=am, in0=aps, in1=maskT[:], op=ALU.mult),
                           reads=[PB[4], B_const], writes=[B_am])
                        for h in range(4):
                            oi = h % 2
                            vh = P[:, 2048 + h * 512:2048 + (h + 1) * 512]
                            op("pe", lambda h=h, vh=vh: PE.matmul(psb[5][0:64, :], lhsT=am[:, h, :], rhs=vh, start=True, stop=False),
                               reads=[B_am, B_P[pi]], writes=[PB[5]], mark=False)
                            for j in range(2):
                                op("pe", lambda h=h, j=j: PE.matmul(psb[5][0:64, :], lhsT=qt[:, 2 * h + j, :], rhs=Sb[:, 2 * h + j, :],
                                                                    start=False, stop=(j == 1)),
                                   reads=[B_qt, B_Sb], writes=[PB[5]], mark=(j == 1))
                            op("act", lambda oi=oi: ACT.activation(out=osb[oi], in_=psb[5][0:64, :], func=AF.Copy),
                               reads=[PB[5]], writes=[B_osb[oi]])
                            op("act", lambda oi=oi, h=h: ACT.activation(out=junk, in_=osb[oi], func=AF.Square,
                                                                        accum_out=small[0:64, h:h + 1]),
                               reads=[B_osb[oi]], writes=[B_junk, B_small])
                            op("act", lambda h=h: ACT.activation(out=small[0:64, 4 + h:5 + h], in_=small[0:64, h:h + 1],
                                                                 func=AF.Sqrt, scale=1.0 / 512, bias=c_eps[0:64, :]),
                               reads=[B_small, B_const], writes=[B_small])
                            op("dve", lambda h=h: DVE.reciprocal(out=small[0:64, 8 + h:9 + h], in_=small[0:64, 4 + h:5 + h]),
                               reads=[B_small], writes=[B_small])
                            op("dve", lambda h=h, oi=oi: DVE.scalar_tensor_tensor(
                                out=on[:, h * 512:(h + 1) * 512], in0=osb[oi], scalar=small[0:64, 8 + h:9 + h],
                                in1=P[:, 4096 + h * 512:4096 + (h + 1) * 512], op0=ALU.mult, op1=ALU.mult),
                                reads=[B_small, B_osb[oi], B_P[pi]], writes=[B_on])
                            for j in range(2):
                                kc = 2 * h + j
                                op("pe", lambda kc=kc, vh=vh: PE.matmul(psb[6][:, :], lhsT=kp[:, kc * 128:(kc + 1) * 128], rhs=vh,
                                                                        start=True, stop=True),
                                   reads=[B_kp, B_P[pi]], writes=[PB[6]])
                                op("dve", lambda kc=kc: DVE.scalar_tensor_tensor(
                                    out=S[:, kc, :], in0=S[:, kc, :], scalar=Eend[:, kc:kc + 1], in1=psb[6][:, :],
                                    op0=ALU.mult, op1=ALU.add),
                                    reads=[B_Eend, PB[6], B_S, B_Sb], writes=[B_S])
                                op("act", lambda kc=kc: ACT.activation(out=Sb[:, kc, :], in_=S[:, kc, :], func=AF.Copy),
                                   reads=[B_S], writes=[B_Sb])
                        tv = psbf(7).rearrange("p (k t) -> p k t", k=16)
                        for kc in range(16):
                            op("pe", lambda kc=kc: PE.transpose(out=tv[:, kc, :], in_=on[:, kc * 128:(kc + 1) * 128],
                                                                identity=ident[0:64, 0:64]),
                               reads=[B_on, B_const], writes=[PB[7]], mark=(kc == 15))
                        op("dve", lambda c=c: DVE.tensor_copy(out=actT[:, :, c * 64:(c + 1) * 64], in_=tv),
                           reads=[PB[7]], writes=[B_actc[c]])
                    sview = lambda ap: ap.rearrange("h (j p) v -> p (h j) v", p=128)
                    if kind == "p":
                        if last:
                            dma(sview(o_gla_p[l]), S, reads=[B_S], is_out=True)
                        else:
                            dma(CARRY_S[l], S, reads=[B_S], writes=[B_CS[l]])
                    else:
                        dma(sview(o_gla_s[l, sidx]), S, reads=[B_S], is_out=True)

            def phaseC(Wrows, gain):
                gemm_tok(Wrows, [(c, 512) for c in range(0, D, 512)], gain, evac_to_Ms)

            def phaseE(l):
                kb.fence()
                W = WOFF
                prw = cvf(SOFF, DFF)[0:8, :]
                B_prw = Buf()
                cp = cvf(W + 17000, FC * 8).rearrange("p (f j) -> p f j", f=FC)
                GT = cvf(W + 17352, FC * 6).rearrange("p (f j) -> p f j", f=FC)
                B_cp, B_GT = Buf(), Buf()
                op("pool", lambda: POOL.memset(prw, 0.0), writes=[B_prw])
                dma(prw[0:3, :], ffn_conv_w[l], writes=[B_prw])
                dma(prw[3:4, :], ffn_conv_b[l].unsqueeze(0), writes=[B_prw])
                if NS:
                    dma(prw[4:8, :], conv_s0[l].rearrange("s j f -> (s j) f"), writes=[B_prw])
                for f in range(FC):
                    op("pe", lambda f=f: PE.transpose(out=psb[0][:, f * 8:(f + 1) * 8], in_=prw[:, f * 128:(f + 1) * 128],
                                                      identity=identf[0:8, 0:8]),
                       reads=[B_prw, B_const], writes=[PB[0]], mark=(f == FC - 1))
                op("dve", lambda: DVE.tensor_copy(out=cp, in_=psb[0][:, 0:FC * 8].rearrange("p (f j) -> p f j", f=FC)),
                   reads=[PB[0]], writes=[B_cp])
                kb.fence()
                wbs = [cvb(W + i * 4096, 8192).rearrange("p (g k n) -> p g k n", g=2, k=16) for i in range(2)]
                B_wb = [Buf(), Buf()]
                TG = T + 6
                Gx = cvf(W + 8192, TG)
                Vx = [cvb(W + 8192 + 2184 + i * 1088, 2176)[:, 0:T] for i in range(2)]
                Cx = cvf(W + 8192 + 2184 + 2176, 2176)[:, 0:T]
                At = [cvb(W + 8192 + 2184 + 2176 + 2176 + i * 1088, 2176)[:, 0:T] for i in range(2)]
                B_Gx, B_Cx = Buf(), Buf()
                B_Vx = [Buf(), Buf()]
                B_At = [Buf(), Buf()]
                goff = {}
                for (c0, ncs, kind, sidx) in segs:
                    goff[(kind, sidx)] = 0 if kind == "p" else np_tok + 2 + 66 * sidx
                gain = NG[:, (l * 4 + 2) * 16:(l * 4 + 2) * 16 + 16]
                Wup = ffn_w_up[l]
                NG2 = DFF // 256

                def load_grp(g, slot):
                    load_wblock(wbs[slot][:, 0, :, :], B_wb[slot], Wup, KC, g * 256, 256, gain)
                    load_wblock(wbs[slot][:, 1, :, :], B_wb[slot], Wup, KC, DFF + g * 256, 256, gain)

                load_grp(0, 0)
                fbi = 0
                for g in range(NG2):
                    cur = g % 2
                    if g + 1 < NG2:
                        load_grp(g + 1, 1 - cur)
                    for fl in range(2):
                        fb = g * 2 + fl
                        vi = fbi % 2
                        fbi += 1
                        if first:
                            op("pool", lambda: POOL.memset(Gx[:, 0:2], 0.0), writes=[B_Gx])
                        else:
                            op("pool", lambda fb=fb: POOL.tensor_copy(out=Gx[:, 0:2], in_=CG[:, l, fb, :]),
                               reads=[B_CG], writes=[B_Gx])
                        for i in range(NS):
                            o_ = goff[("s", i)]
                            op("pool", lambda fb=fb, i=i, o_=o_: POOL.tensor_copy(out=Gx[:, o_:o_ + 2], in_=cp[:, fb, 4 + 2 * i:6 + 2 * i]),
                               reads=[B_cp], writes=[B_Gx])
                        for bi, (t0, n) in enumerate(tblocks):
                            bg = bi % 3
                            bv = 3 + bi % 3
                            mm_feat(bg, wbs[cur][:, 0, :, :], B_wb[cur], fl * 128, 128, t0, n)
                            mm_feat(bv, wbs[cur][:, 1, :, :], B_wb[cur], fl * 128, 128, t0, n)
                            if t0 < np_tok:
                                op("act", lambda bg=bg, t0=t0, n=n: ACT.activation(out=Gx[:, 2 + t0:2 + t0 + n], in_=psb[bg][:, 0:n], func=AF.Copy),
                                   reads=[PB[bg]], writes=[B_Gx])
                            else:
                                for i in range(NS):
                                    o_ = goff[("s", i)] + 2
                                    op("act", lambda bg=bg, i=i, o_=o_: ACT.activation(out=Gx[:, o_:o_ + 64], in_=psb[bg][:, i * 64:(i + 1) * 64], func=AF.Copy),
                                       reads=[PB[bg]], writes=[B_Gx])
                            op("dve", lambda bv=bv, t0=t0, n=n, vi=vi: DVE.tensor_copy(out=Vx[vi][:, t0:t0 + n], in_=psb[bv][:, 0:n]),
                               reads=[PB[bv]], writes=[B_Vx[vi]])
                        for (c0, ncs, kind, sidx) in segs:
                            o_ = goff[(kind, sidx)]
                            n = ncs * 64
                            t0 = c0 * 64
                            op("dve", lambda fb=fb, o_=o_, n=n, t0=t0: DVE.tensor_scalar(
                                out=Cx[:, t0:t0 + n], in0=Gx[:, o_ + 2:o_ + 2 + n], scalar1=cp[:, fb, 2:3], scalar2=cp[:, fb, 3:4],
                                op0=ALU.mult, op1=ALU.add), reads=[B_Gx, B_cp], writes=[B_Cx])
                            op("dve", lambda fb=fb, o_=o_, n=n, t0=t0: DVE.scalar_tensor_tensor(
                                out=Cx[:, t0:t0 + n], in0=Gx[:, o_ + 1:o_ + 1 + n], scalar=cp[:, fb, 1:2], in1=Cx[:, t0:t0 + n],
                                op0=ALU.mult, op1=ALU.add), reads=[B_Gx, B_cp, B_Cx], writes=[B_Cx])
                            op("dve", lambda fb=fb, o_=o_, n=n, t0=t0: DVE.scalar_tensor_tensor(
                                out=Cx[:, t0:t0 + n], in0=Gx[:, o_:o_ + n], scalar=cp[:, fb, 0:1], in1=Cx[:, t0:t0 + n],
                                op0=ALU.mult, op1=ALU.add), reads=[B_Gx, B_cp, B_Cx], writes=[B_Cx])
                            j0 = 0 if kind == "p" else 2 + 2 * sidx
                            op("pool", lambda fb=fb, o_=o_, n=n, j0=j0: POOL.tensor_copy(out=GT[:, fb, j0:j0 + 2], in_=Gx[:, o_ + n:o_ + n + 2]),
                               reads=[B_Gx], writes=[B_GT])
                            if kind == "p":
                                op("pool", lambda fb=fb, o_=o_, n=n: POOL.tensor_copy(out=CG[:, l, fb, :], in_=Gx[:, o_ + n:o_ + n + 2]),
                                   reads=[B_Gx], writes=[B_CG])
                        op("act", lambda: ACT.activation(out=Cx, in_=Cx, func=AF.Silu), reads=[B_Cx], writes=[B_Cx])
                        op("dve", lambda vi=vi: DVE.tensor_tensor(out=At[vi], in0=Cx, in1=Vx[vi], op=ALU.mult),
                           reads=[B_Cx, B_Vx[vi]], writes=[B_At[vi]])
                        dma(ATs[fb][:, 0:T], At[vi], reads=[B_At[vi]], writes=[B_ATs])
                kb.fence()
                orow = cvf(SOFF, 2048)[0:6, :]
                B_orow = Buf()
                for r0 in range(0, FC, 16):
                    nf = min(16, FC - r0)
                    for f in range(nf):
                        bank = f // 4
                        op("pe", lambda f=f, r0=r0, bank=bank: PE.transpose(
                            out=psb[bank][0:6, (f % 4) * 128:(f % 4 + 1) * 128], in_=GT[:, r0 + f, :], identity=identf[:]),
                            reads=[B_GT, B_const], writes=[PB[bank]], mark=(f % 4 == 3 or f == nf - 1))
                    for bank in range((nf + 3) // 4):
                        op("dve", lambda bank=bank: DVE.tensor_copy(out=orow[:, bank * 512:(bank + 1) * 512], in_=psb[bank][0:6, :]),
                           reads=[PB[bank]], writes=[B_orow])
                    cs = slice(r0 * 128, (r0 + nf) * 128)
                    if last:
                        dma(o_conv_p[l][:, cs], orow[0:2, 0:nf * 128], reads=[B_orow], is_out=True)
                    for i in range(NS):
                        dma(o_conv_s[l, i][:, cs], orow[2 + 2 * i:4 + 2 * i, 0:nf * 128], reads=[B_orow], is_out=True)

            def phaseF(l):
                kb.fence()
                wbs = [cvb(i * 11264, 22528).rearrange("p (k n) -> p k n", k=FC) for i in range(2)]
                aT = [cvb(22528 + i * 2816, 5632).rearrange("p (k t) -> p k t", k=FC) for i in range(2)]
                otf = [cvf(28160 + i * 512, 512) for i in range(4)]
                B_wb = [Buf(), Buf()]
                B_aT = [Buf(), Buf()]
                B_ot = [Buf() for _ in range(4)]
                Wd = ffn_w_down[l]
                load_wblock(wbs[0], B_wb[0], Wd, FC, 0, 512, None)
                rr = 0
                ai = 0
                for nb in range(4):
                    cur = nb % 2
                    if nb + 1 < 4:
                        load_wblock(wbs[1 - cur], B_wb[1 - cur], Wd, FC, (nb + 1) * 512, 512, None)
                    for tt in range(NT):
                        a = ai % 2
                        ai += 1
                        dma(aT[a], ATs[:, :, tt * 128:(tt + 1) * 128].rearrange("k p t -> p k t"), reads=[B_ATs], writes=[B_aT[a]])
                        bank = rr % 4
                        oi = rr % 4
                        rr += 1
                        for kc in range(FC):
                            op("pe", lambda kc=kc, a=a, bank=bank, cur=cur: PE.matmul(
                                psb[bank][:, :], lhsT=aT[a][:, kc, :], rhs=wbs[cur][:, kc, :], start=(kc == 0), stop=(kc == FC - 1)),
                                reads=[B_aT[a], B_wb[cur]], writes=[PB[bank]], mark=(kc == FC - 1))
                        op("act", lambda oi=oi, bank=bank: ACT.activation(out=otf[oi], in_=psb[bank][:, :], func=AF.Copy),
                           reads=[PB[bank]], writes=[B_ot[oi]])
                        dma(Ms[tt * 128:(tt + 1) * 128, nb * 512:(nb + 1) * 512], otf[oi], reads=[B_ot[oi]], writes=[B_Ms[tt]], q="act")

            def phaseKV():
                wbs, B_wb, otf, B_ot = gemm_setup()
                rr = [0, 0]

                def next_ot():
                    i = rr[1]
                    rr[1] = (i + 1) % 4
                    return i, otf[i], B_ot[i]

                load_wblock(wbs[0], B_wb[0], att_w_kv, KC, 0, 512, KVN)
                load_wblock(wbs[1], B_wb[1], att_w_kv, KC, 512, 512, KVN)

                def out_rows(tt):
                    if tt >= np_tok // 128:
                        return ("s", 0)
                    g0 = st * np_tok + tt * 128
                    if g0 >= NPT - 512:
                        return ("p", g0 - (NPT - 512))
                    return None

                if first:
                    i, o_, Bo = next_ot()
                    zt = o_.bitcast(BF16)[:, 0:512]
                    op("pool", lambda zt=zt: POOL.memset(zt, 0.0), writes=[Bo])
                    for h in range(4):
                        dma(KTs[h][:, 0:512], zt, reads=[Bo], writes=[B_KTs])
                        dma(VSH[h * 128:(h + 1) * 128, :], zt, reads=[Bo], writes=[B_VSH])
                for fb in range(4):
                    for (t0, n) in tblocks:
                        bank = rr[0] % 4
                        rr[0] += 1
                        mm_feat(bank, wbs[0], B_wb[0], fb * 128, 128, t0, n)
                        i, o_, Bo = next_ot()
                        o = o_.bitcast(BF16)
                        op("act", lambda o=o, bank=bank, n=n: ACT.activation(out=o[:, 0:n], in_=psb[bank][:, 0:n], func=AF.Copy),
                           reads=[PB[bank]], writes=[Bo])
                        if t0 < np_tok:
                            g0 = 512 + st * np_tok + t0
                            dma(KTs[fb][:, g0:g0 + n], o[:, 0:n], reads=[Bo], writes=[B_KTs], q="act")
                        else:
                            for si in range(NS):
                                dma(KTss[si, fb][:, 512:576], o[:, si * 64:(si + 1) * 64], reads=[Bo], writes=[B_KTs])
                import os
                if os.environ.get('DBG_KV') == '1':
                    return
                for tt in range(NT):
                    orow = out_rows(tt)
                    if orow is not None:
                        bank = rr[0] % 4
                        rr[0] += 1
                        for kc in range(KC):
                            op("pe", lambda kc=kc, tt=tt, bank=bank: PE.matmul(
                                psb[bank][:, :], lhsT=actT[:, kc, tt * 128:(tt + 1) * 128], rhs=wbs[0][:, kc, :],
                                start=(kc == 0), stop=(kc == KC - 1)),
                                reads=actbufs(tt) + [B_wb[0]], writes=[PB[bank]], mark=(kc == KC - 1))
                        i, o, Bo = next_ot()
                        op("act", lambda o=o, bank=bank: ACT.activation(out=o, in_=psb[bank][:, :], func=AF.Copy),
                           reads=[PB[bank]], writes=[Bo])
                        dst = o_k_p[orow[1]:orow[1] + 128, :] if orow[0] == "p" else o_k_s
                        dma(dst, o, reads=[Bo], is_out=True)
                    bank = rr[0] % 4
                    rr[0] += 1
                    for kc in range(KC):
                        op("pe", lambda kc=kc, tt=tt, bank=bank: PE.matmul(
                            psb[bank][:, :], lhsT=actT[:, kc, tt * 128:(tt + 1) * 128], rhs=wbs[1][:, kc, :],
                            start=(kc == 0), stop=(kc == KC - 1)),
                            reads=actbufs(tt) + [B_wb[1]], writes=[PB[bank]], mark=(kc == KC - 1))
                    vsrc, vB = psb[bank][:, :], PB[bank]
                    if orow is not None:
                        i, of, Bof = next_ot()
                        op("act", lambda of=of, bank=bank: ACT.activation(out=of, in_=psb[bank][:, :], func=AF.Copy),
                           reads=[PB[bank]], writes=[Bof])
                        dst = o_v_p[orow[1]:orow[1] + 128, :] if orow[0] == "p" else o_v_s
                        dma(dst, of, reads=[Bof], is_out=True)
                        vsrc, vB = of, Bof
                    i, o_, Bo = next_ot()
                    o = o_.bitcast(BF16)[:, 0:512]
                    op("dve", lambda o=o, vsrc=vsrc: DVE.tensor_copy(out=o, in_=vsrc), reads=[vB], writes=[Bo])
                    if tt < np_tok // 128:
                        r0 = 512 + st * np_tok + tt * 128
                        dma(VSH[r0:r0 + 128, :], o, reads=[Bo], writes=[B_VSH])
                    else:
                        for si in range(NS):
                            r0 = (8 + NPC + 9 * si) * 64 + 512
                            dma(VSH[r0:r0 + 64, :], o[si * 64:(si + 1) * 64, :], reads=[Bo], writes=[B_VSH])
                if os.environ.get('DBG_KV') == '2':
                    return
                if NS:
                    kb.fence()
                    cf = [cvf(SOFF + i * 512, 512) for i in range(2)]
                    cb_ = [cvb(SOFF + 1024 + i * 256, 512) for i in range(2)]
                    kt_ = [cvb(SOFF + 1536 + i * 256, 512).rearrange("p (h t) -> p h t", h=4) for i in range(2)]
                    B_cf, B_cb, B_kt = [Buf(), Buf()], [Buf(), Buf()], [Buf(), Buf()]
                    n_ = 0
                    for si in range(NS):
                        for rt in range(4):
                            for which in range(2):
                                b = n_ % 2
                                n_ += 1
                                src = (ck if which == 0 else cv)[si][rt * 128:(rt + 1) * 128, :]
                                dma(cf[b], src, writes=[B_cf[b]])
                                op("dve", lambda b=b: DVE.tensor_copy(out=cb_[b], in_=cf[b]), reads=[B_cf[b]], writes=[B_cb[b]])
                                if which == 1:
                                    r0 = (8 + NPC + 9 * si) * 64 + rt * 128
                                    dma(VSH[r0:r0 + 128, :], cb_[b], reads=[B_cb[b]], writes=[B_VSH])
                                else:
                                    pv = psbf(4 + b)
                                    for h in range(4):
                                        op("pe", lambda h=h, b=b, pv=pv: PE.transpose(out=pv[:, h * 128:(h + 1) * 128], in_=cb_[b][:, h * 128:(h + 1) * 128],
                                                                                      identity=ident[:]),
                                           reads=[B_cb[b], B_const], writes=[PB[4 + b]], mark=(h == 3))
                                    op("act", lambda b=b, pv=pv: ACT.activation(out=kt_[b], in_=pv[:, 0:512].rearrange("p (h t) -> p h t", h=4), func=AF.Copy),
                                       reads=[PB[4 + b]], writes=[B_kt[b]])
                                    dma(KTss[si][:, :, rt * 128:(rt + 1) * 128].rearrange("h p t -> p h t"), kt_[b], reads=[B_kt[b]], writes=[B_KTs])

            def phaseQ(l, j):
                wbs, B_wb, otf, B_ot = gemm_setup()
                gain = NG[:, (l * 4 + 0) * 16:(l * 4 + 0) * 16 + 16]
                Wq = att_w_q[j]
                load_wblock(wbs[0], B_wb[0], Wq, KC, 0, 512, gain)
                rr = 0
                for blk in range(4):
                    cur = blk % 2
                    if blk + 1 < 4:
                        load_wblock(wbs[1 - cur], B_wb[1 - cur], Wq, KC, (blk + 1) * 512, 512, gain)
                    for pl in range(2):
                        pair = blk * 2 + pl
                        for (t0, n) in tblocks:
                            oi = rr % 4
                            o = otf[oi].bitcast(BF16)[:, 0:2 * n].rearrange("p (c h t) -> p c h t", h=2, t=64)
                            for h2 in range(2):
                                bank = (2 * rr + h2) % 4
                                mm_feat(bank, wbs[cur], B_wb[cur], (pl * 2 + h2) * 128, 128, t0, n)
                                op("act", lambda o=o, bank=bank, n=n, h2=h2: ACT.activation(
                                    out=o[:, :, h2, :], in_=psb[bank][:, 0:n].rearrange("p (c t) -> p c t", t=64), func=AF.Copy,
                                    scale=float(128 ** -0.5)),
                                    reads=[PB[bank]], writes=[B_ot[oi]])
                            rr += 1
                            dma(QTs[pair][:, 2 * t0:2 * (t0 + n)], otf[oi].bitcast(BF16)[:, 0:2 * n], reads=[B_ot[oi]], writes=[B_QTs], q="act")

            def phaseATT(l, j):
                kb.fence()
                W = WOFF
                Bt = cvf(W + 0, 4608).rearrange("p (g k) -> p g k", g=8)
                qb = [cvb(W + 4608 + i * 2048, 4096).rearrange("p (g t) -> p g t", g=8) for i in range(2)]
                KTb = [cvb(W + 8704 + i * 1152, 2304).rearrange("p (h t) -> p h t", h=4) for i in range(2)]
                Vb = [cvb(W + 11008 + i * 2304, 4608)[0:64, :].rearrange("p (c n) -> p c n", c=9) for i in range(2)]
                sbf = [cvf(W + 15616 + i * 576, 576) for i in range(2)]
                pn = [cvb(W + 16768 + i * 288, 576) for i in range(2)]
                pTs = [cvb(W + 17344 + i * 576, 1152)[0:64, :].rearrange("p (c t) -> p c t", c=9) for i in range(2)]
                small = cvf(W + 18496, 64)
                rbt = cvf(SOFF, 513)[0:16, :]
                et = cvf(SOFF + 520, 640)[0:16, :]
                B_Bt, B_rbt, B_et = Buf(), Buf(), Buf()
                B_qb, B_KTb, B_Vb = [Buf(), Buf()], [Buf(), Buf()], [Buf(), Buf()]
                B_sbf, B_pn, B_pTs, B_sm = [Buf(), Buf()], [Buf(), Buf()], [Buf(), Buf()], [Buf(), Buf()]
                dma(rbt, att_rel_bias[j], writes=[B_rbt])
                op("dve", lambda: DVE.memset(et, 0.0), writes=[B_et])
                op("dve", lambda: DVE.tensor_copy(out=et[:, 0:319], in_=rbt[:, 512:513].broadcast_to([16, 319])),
                   reads=[B_rbt], writes=[B_et])
                op("dve", lambda: DVE.tensor_copy(out=et[:, 319:639], in_=rbt[:, 512:192:-1]), reads=[B_rbt], writes=[B_et])
                dma(EREL, et, reads=[B_et], writes=[B_EREL])
                for h in range(16):
                    dma(bass.AP(ZREL.tensor, h * 41024, [[641, 64], [1, 640]]), EREL[h].partition_broadcast(64),
                        reads=[B_EREL], writes=[B_ZREL])
                for h in range(16):
                    src = bass.AP(ZREL.tensor, h * 41024 + 63, [[640, 64], [1, 576]])
                    dma(Bt[(h % 2) * 64:(h % 2 + 1) * 64, h // 2, :], src, reads=[B_ZREL], writes=[B_Bt])
                cidx = 0
                pidx = 0
                qi = -1
                qblk_of = {}
                for (c0, ncs, kind, sidx) in segs:
                    for c in range(c0, c0 + ncs):
                        qb0 = (c // 4) * 4
                        if kind == "s":
                            qb0 = segs[1][0]
                        if qb0 not in qblk_of:
                            qi += 1
                            nqc = min(4, (NPCH if kind == "p" else NPCH + NS) - qb0)
                            qblk_of[qb0] = qi % 2
                            dma(qb[qi % 2][:, :, 0:nqc * 128], QTs[:, :, qb0 * 128:(qb0 + nqc) * 128].rearrange("g p t -> p g t"),
                                reads=[B_QTs], writes=[B_qb[qi % 2]])
                        qsel = qblk_of[qb0]
                        cl = c - qb0
                        bi = cidx % 2
                        cidx += 1
                        if kind == "p":
                            gc = st * NPCH + c
                            j0 = max(0, 8 - gc) * 64
                            dma(KTb[bi], KTs[:, :, gc * 64:gc * 64 + 576].rearrange("h p t -> p h t"), reads=[B_KTs], writes=[B_KTb[bi]])
                            dma(Vb[bi], VSH[gc * 64:gc * 64 + 576, :].rearrange("(c p) n -> p c n", p=64), reads=[B_VSH], writes=[B_Vb[bi]])
                        else:
                            j0 = 0
                            r0 = (8 + NPC + 9 * sidx) * 64
                            dma(KTb[bi], KTss[sidx].rearrange("h p t -> p h t"), reads=[B_KTs], writes=[B_KTb[bi]])
                            dma(Vb[bi], VSH[r0:r0 + 576, :].rearrange("(c p) n -> p c n", p=64), reads=[B_VSH], writes=[B_Vb[bi]])
                        jb0 = j0 // 64
                        for p8 in range(8):
                            n_ = p8 // 2
                            pi = pidx % 2
                            pidx += 1
                            bA, bB = (0, 1) if pi == 0 else (2, 3)
                            bO = 6 + pi
                            lq = qb[qsel][:, p8, cl * 128:(cl + 1) * 128]
                            sm = small[:, pi * 8:(pi + 1) * 8]
                            if j0 < 512:
                                op("pe", lambda lq=lq, bA=bA, bi=bi, n_=n_, j0=j0: PE.matmul(psb[bA][:, j0:512], lhsT=lq, rhs=KTb[bi][:, n_, j0:512],
                                                                                            start=True, stop=True),
                                   reads=[B_qb[qsel], B_KTb[bi]], writes=[PB[bA]])
                            op("pe", lambda lq=lq, bB=bB, bi=bi, n_=n_: PE.matmul(psb[bB][:, 0:64], lhsT=lq, rhs=KTb[bi][:, n_, 512:576],
                                                                                 start=True, stop=True),
                               reads=[B_qb[qsel], B_KTb[bi]], writes=[PB[bB]])
                            if j0 < 512:
                                op("dve", lambda pi=pi, bA=bA, p8=p8, j0=j0: DVE.tensor_tensor(out=sbf[pi][:, j0:512], in0=psb[bA][:, j0:512],
                                                                                              in1=Bt[:, p8, j0:512], op=ALU.add),
                                   reads=[PB[bA], B_Bt], writes=[B_sbf[pi]])
                            op("dve", lambda pi=pi, bB=bB, p8=p8: DVE.tensor_tensor(out=sbf[pi][:, 512:576], in0=psb[bB][:, 0:64],
                                                                                   in1=Bt[:, p8, 512:576], op=ALU.add),
                               reads=[PB[bB], B_Bt], writes=[B_sbf[pi]])
                            op("dve", lambda pi=pi, sm=sm, j0=j0: DVE.reduce_max(out=sm[:, 0:1], in_=sbf[pi][:, j0:576], axis=AX.X),
                               reads=[B_sbf[pi]], writes=[B_sm[pi]])
                            op("dve", lambda sm=sm: DVE.tensor_scalar(out=sm[:, 1:2], in0=sm[:, 0:1], scalar1=-1.0, scalar2=None, op0=ALU.mult),
                               reads=[B_sm[pi]], writes=[B_sm[pi]])
                            op("act", lambda pi=pi, sm=sm, j0=j0: ACT.activation(out=sbf[pi][:, j0:576], in_=sbf[pi][:, j0:576], func=AF.Exp,
                                                                                bias=sm[:, 1:2], accum_out=sm[:, 2:3]),
                               reads=[B_sbf[pi], B_sm[pi]], writes=[B_sbf[pi], B_sm[pi]])
                            op("dve", lambda sm=sm: DVE.reciprocal(out=sm[:, 3:4], in_=sm[:, 2:3]), reads=[B_sm[pi]], writes=[B_sm[pi]])
                            op("dve", lambda pi=pi, sm=sm, j0=j0: DVE.tensor_scalar(out=pn[pi][:, j0:576], in0=sbf[pi][:, j0:576], scalar1=sm[:, 3:4],
                                                                                   scalar2=None, op0=ALU.mult),
                               reads=[B_sbf[pi], B_sm[pi]], writes=[B_pn[pi]])
                            pvA, pvB = psbf(4), psbf(5)
                            for jb in range(jb0, 9):
                                dstp = pvA[0:64, jb * 128:(jb + 1) * 128] if jb < 8 else pvB[0:64, 0:128]
                                bk = 4 if jb < 8 else 5
                                op("pe", lambda pi=pi, jb=jb, dstp=dstp: PE.transpose(out=dstp, in_=pn[pi][:, jb * 64:(jb + 1) * 64], identity=ident[:]),
                                   reads=[B_pn[pi], B_const], writes=[PB[bk]], mark=(jb >= 7))
                            if jb0 < 8:
                                op("act", lambda pi=pi, jb0=jb0: ACT.activation(out=pTs[pi][:, jb0:8, :],
                                                                               in_=pvA[0:64, jb0 * 128:1024].rearrange("p (c t) -> p c t", t=128), func=AF.Copy),
                                   reads=[PB[4]], writes=[B_pTs[pi]])
                            op("act", lambda pi=pi: ACT.activation(out=pTs[pi][:, 8, :], in_=pvB[0:64, 0:128], func=AF.Copy),
                               reads=[PB[5]], writes=[B_pTs[pi]])
                            for jb in range(jb0, 9):
                                op("pe", lambda pi=pi, jb=jb, bi=bi, n_=n_, bO=bO: PE.matmul(psb[bO][:, 0:128], lhsT=Vb[bi][:, jb, n_ * 128:(n_ + 1) * 128],
                                                                                           rhs=pTs[pi][:, jb, :], start=(jb == jb0), stop=(jb == 8)),
                                   reads=[B_Vb[bi], B_pTs[pi]], writes=[PB[bO]], mark=(jb == 8))
                            op("act", lambda p8=p8, c=c, bO=bO: ACT.activation(out=actT[:, 2 * p8:2 * p8 + 2, c * 64:(c + 1) * 64],
                                                                              in_=psb[bO][:, 0:128].rearrange("p (h t) -> p h t", h=2), func=AF.Copy),
                               reads=[PB[bO]], writes=[B_actc[c]])

            Xrows = lambda tt: Xs[tt * 128:(tt + 1) * 128, :]
            BX = lambda tt: [B_Xs[tt]]
            resnorm(lambda tt: xin[xrow(tt):xrow(tt) + 128, :], lambda tt: [], None, None, Xrows, BX, True)
            if dbg is not None and dbg[0] == "h0":
                return "stop"
            for l in range(n_layers):
                if l < 2:
                    phaseA(l)
                    if dbg is not None and dbg[0] == "projA" and l == DL:
                        return "stop"
                    phaseB(l)
                    if dbg is not None and dbg[0] == "scan" and l == DL:
                        return "stop"
                    phaseC(gla_w_o[l], HNx[:, l, :, :].rearrange("p h v -> p (h v)"))
                else:
                    j = l - 2
                    if l == 2:
                        phaseKV()
                    if dbg is not None and dbg[0] == "kv":
                        return "stop"
                    phaseQ(l, j)
                    if dbg is not None and dbg[0] == "q":
                        return "stop"
                    phaseATT(l, j)
                    if dbg is not None and dbg[0] == "att":
                        return "stop"
                    phaseC(att_w_o[j], None)
                resnorm(Xrows, BX, lambda tt: Ms[tt * 128:(tt + 1) * 128, :], norm_gains[l, 1], Xrows, BX, True)
                if dbg is not None and dbg[0] == "mix" and l == DL:
                    return "stop"
                phaseE(l)
                phaseF(l)
                fin = (l == n_layers - 1)
                if fin:
                    resnorm(Xrows, BX, lambda tt: Ms[tt * 128:(tt + 1) * 128, :], norm_gains[l, 3],
                            lambda tt: yout[xrow(tt):xrow(tt) + 128, :], lambda tt: [], False, is_out=True)
                else:
                    resnorm(Xrows, BX, lambda tt: Ms[tt * 128:(tt + 1) * 128, :], norm_gains[l, 3], Xrows, BX, True)
            return None

        for st in range(n_st):
            r = run_supertile(st)
            if r == "stop":
                break
        if dbg is not None:
            kb.fence()
            Bt = Buf()
            nt_dbg = dbg[1][0] // 128
            if dbg[0] == "projA":
                tb = cvb(WOFF, 6144)
                for tt in range(nt_dbg):
                    dma(tb, PROJ[tt * 128:(tt + 1) * 128, :], reads=[B_PROJ[tt]], writes=[Bt])
                    dma(dbg_out[tt * 128:(tt + 1) * 128, :], tb, reads=[Bt], is_out=True)
            elif dbg[0] in ("h0", "scan", "att"):
                dma(dbg_out.rearrange("k p t -> p k t"), actT[:, :, 0:dbg[1][2]], reads=B_actc, is_out=True)
            elif dbg[0] == "mix":
                tb = cvf(WOFF, 2048)
                for tt in range(nt_dbg):
                    dma(tb, Xs[tt * 128:(tt + 1) * 128, :], reads=[B_Xs[tt]], writes=[Bt])
                    dma(dbg_out[tt * 128:(tt + 1) * 128, :], tb, reads=[Bt], is_out=True)
        kb.finish()
    return nc


WEIGHT_KEYS = ["norm_gains", "gla_w_in", "gla_w_gate", "gla_b_gate", "gla_head_norm", "gla_w_o", "kv_norm",
               "att_w_kv", "att_w_q", "att_rel_bias", "att_w_o", "ffn_w_up", "ffn_conv_w", "ffn_conv_b", "ffn_w_down"]


def make_in_maps(inp, npt):
    f = lambda a: np.ascontiguousarray(np.asarray(a, dtype=np.float32))
    w = {k: f(inp[k]) for k in WEIGHT_KEYS}
    xp, xs = f(inp["x_prompt"]), f(inp["x_sample"])
    sg, sc = f(inp["state_gla"]), f(inp["state_ffn_conv"])
    ck, cv = f(inp["cache_k"]), f(inp["cache_v"])
    maps = []
    for c in range(N_CORES):
        xin = np.zeros((npt + 128, D), np.float32)
        if c < xp.shape[0]:
            xin[:npt] = xp[c, :npt]
        xin[npt:] = xs[2 * c:2 * c + 2].reshape(128, D)
        m = dict(w)
        m["xin"] = xin
        m["gla_s0"] = np.ascontiguousarray(sg[:, 2 * c:2 * c + 2])
        m["conv_s0"] = np.ascontiguousarray(sc[:, 2 * c:2 * c + 2])
        m["ck"] = np.ascontiguousarray(ck[2 * c:2 * c + 2].reshape(2, 512, 512))
        m["cv"] = np.ascontiguousarray(cv[2 * c:2 * c + 2].reshape(2, 512, 512))
        maps.append(m)
    return maps


_NC_CACHE = {}


def kernel(**inputs):
    if "full" not in _NC_CACHE:
        _NC_CACHE["full"] = build_program()
    nc = _NC_CACHE["full"]
    maps = make_in_maps(inputs, SEQ)
    res = run_bass_kernel_spmd(nc, maps, core_ids=list(range(N_CORES))).results
    B = 4
    y_p = np.stack([res[c]["yout"][:SEQ] for c in range(B)])
    y_s = np.concatenate([res[c]["yout"][SEQ:].reshape(2, 64, D) for c in range(N_CORES)])
    gla_p = np.stack([res[c]["o_gla_p"] for c in range(B)], axis=1)
    gla_s = np.concatenate([res[c]["o_gla_s"] for c in range(N_CORES)], axis=1)
    conv_p = np.stack([res[c]["o_conv_p"] for c in range(B)], axis=1)
    conv_s = np.concatenate([res[c]["o_conv_s"] for c in range(N_CORES)], axis=1)
    k_p = np.stack([res[c]["o_k_p"].reshape(512, 4, 128) for c in range(B)])
    v_p = np.stack([res[c]["o_v_p"].reshape(512, 4, 128) for c in range(B)])
    k_s = np.concatenate([res[c]["o_k_s"].reshape(2, 64, 4, 128) for c in range(N_CORES)])
    v_s = np.concatenate([res[c]["o_v_s"].reshape(2, 64, 4, 128) for c in range(N_CORES)])
    outs = (y_p, y_s, gla_p, gla_s, conv_p, conv_s, k_p, v_p, k_s, v_s)
    return tuple(np.ascontiguousarray(o, dtype=np.float32) for o in outs)
```

```python
import numpy as np
from contextlib import ExitStack
import concourse.bass as bass
import concourse.mybir as mybir
from concourse.bass_utils import run_bass_kernel_spmd

F32 = mybir.dt.float32
BF16 = mybir.dt.bfloat16
AF = mybir.ActivationFunctionType
ALU = mybir.AluOpType
AX = mybir.AxisListType

D = 2048
KC = 16
DFF = 5632
FC = 44
GLA_IN = 6160
EPS = 1e-6
N_CORES = 8
SEQ = 4096
NPH = 2048
EPOCH = 16000


class Buf:
    __slots__ = ("w", "r", "name")

    def __init__(self, name=""):
        self.w = None
        self.r = {}
        self.name = name


class KB:
    def __init__(self, nc, es):
        self.nc = nc
        self.es = es
        self.eng = dict(pe=nc.tensor, act=nc.scalar, dve=nc.vector, pool=nc.gpsimd, sp=nc.sync)
        self.sems = {}
        self.cnt = {}
        self.epoch = {}
        for e in ("pe", "act", "dve", "pool"):
            self.epoch[e] = 0
            self._newsem(self._ekey(e))
        self.ndma = 24
        self.dkeys = []
        for j in range(self.ndma):
            k = "d%d_0" % j
            self._newsem(k)
            self.dkeys.append(k)
        self.dma_rr = 0
        self.waited = {e: {} for e in self.eng}
        self.out_events = []
        self.fence_evs = []

    def _ekey(self, e):
        return "%s_%d" % (e, self.epoch[e])

    def _newsem(self, key):
        self.sems[key] = self.es.enter_context(self.nc.semaphore("s_" + key))
        self.cnt[key] = 0

    def fence(self):
        self.fence_evs = [(k, c) for k, c in self.cnt.items() if c > 0]

    def _wait(self, e, evs):
        best = {}
        for (k, v) in list(evs) + self.fence_evs:
            if v > best.get(k, 0):
                best[k] = v
        for k, v in best.items():
            if e == "pe" and k.startswith("pe_"):
                continue
            if self.waited[e].get(k, 0) >= v:
                continue
            if e != "sp" and k == self._ekey(e) and v > self.cnt[k]:
                continue
            self.eng[e].wait_ge(self.sems[k], v)
            self.waited[e][k] = v

    @staticmethod
    def _deps(reads, writes):
        evs = []
        for b in reads:
            if b.w is not None:
                evs.append(b.w)
        for b in writes:
            if b.w is not None:
                evs.append(b.w)
            evs.extend(b.r.items())
        return evs

    @staticmethod
    def _upd(ev, reads, writes):
        for b in reads:
            if ev[1] > b.r.get(ev[0], 0):
                b.r[ev[0]] = ev[1]
        for b in writes:
            b.w = ev
            b.r = {}

    def op(self, e, fn, reads=(), writes=(), mark=True):
        self._wait(e, self._deps(reads, writes))
        ins = fn()
        key = self._ekey(e)
        ev = (key, self.cnt[key] + 1)
        if mark:
            ins.then_inc(self.sems[key], 1)
            self.cnt[key] += 1
            if self.cnt[key] >= EPOCH:
                self.epoch[e] += 1
                self._newsem(self._ekey(e))
        self._upd(ev, reads, writes)
        return ins

    def dma(self, out_ap, in_ap, reads=(), writes=(), q="sp", is_out=False, slow=False):
        j = self.dma_rr
        self.dma_rr = (j + 1) % self.ndma
        key = self.dkeys[j]
        if self.cnt[key] >= EPOCH:
            nk = "d%d_%d" % (j, int(key.split("_")[1]) + 1)
            self._wait(q, [(key, self.cnt[key])])
            self._newsem(nk)
            self.dkeys[j] = nk
            key = nk
        evs = self._deps(reads, writes)
        if self.cnt[key] > 0:
            evs.append((key, self.cnt[key]))
        self._wait(q, evs)
        if slow:
            self.eng[q].dma_start(out=out_ap, in_=in_ap, allow_slow_non_contiguous=True).then_inc(self.sems[key], 16)
        else:
            self.eng[q].dma_start(out=out_ap, in_=in_ap).then_inc(self.sems[key], 16)
        self.cnt[key] += 16
        ev = (key, self.cnt[key])
        self._upd(ev, reads, writes)
        if is_out:
            self.out_events.append(ev)
        return ev

    def finish(self):
        evs = list(self.out_events)
        for k in self.dkeys:
            if self.cnt[k] > 0:
                evs.append((k, self.cnt[k]))
        self._wait("sp", evs)


def build_program(n_layers=4, np_tok=NPH, n_st=2, dbg=None):
    nc = bass.Bass("TRN2", target_bir_lowering=False)
    NPT = np_tok * n_st
    XR = NPT + 128
    dt = nc.dram_tensor

    def din(name, shape, dtype=F32):
        return dt(name, list(shape), dtype, kind="ExternalInput").ap()

    def dout(name, shape, dtype=F32):
        return dt(name, list(shape), dtype, kind="ExternalOutput").ap()

    def dscr(name, shape, dtype):
        return dt(name, list(shape), dtype, kind="Internal").ap()

    xin = din("xin", [XR, D])
    gla_s0 = din("gla_s0", [2, 2, 4, 256, 512])
    conv_s0 = din("conv_s0", [4, 2, 2, DFF])
    ck = din("ck", [2, 512, 512])
    cv = din("cv", [2, 512, 512])
    norm_gains = din("norm_gains", [4, 4, D])
    gla_w_in = din("gla_w_in", [2, D, GLA_IN])
    gla_w_gate = din("gla_w_gate", [2, 16, 1024])
    gla_b_gate = din("gla_b_gate", [2, 1024])
    gla_head_norm = din("gla_head_norm", [2, 512])
    gla_w_o = din("gla_w_o", [2, D, D])
    kv_norm = din("kv_norm", [D])
    att_w_kv = din("att_w_kv", [D, 1024])
    att_w_q = din("att_w_q", [2, D, D])
    att_rel_bias = din("att_rel_bias", [2, 16, 513])
    att_w_o = din("att_w_o", [2, D, D])
    ffn_w_up = din("ffn_w_up", [4, D, 2 * DFF])
    ffn_conv_w = din("ffn_conv_w", [4, 3, DFF])
    ffn_conv_b = din("ffn_conv_b", [4, DFF])
    ffn_w_down = din("ffn_w_down", [4, DFF, D])
    yout = dout("yout", [XR, D])
    o_gla_p = dout("o_gla_p", [2, 4, 256, 512])
    o_gla_s = dout("o_gla_s", [2, 2, 4, 256, 512])
    o_conv_p = dout("o_conv_p", [4, 2, DFF])
    o_conv_s = dout("o_conv_s", [4, 2, 2, DFF])
    o_k_p = dout("o_k_p", [512, 512])
    o_v_p = dout("o_v_p", [512, 512])
    o_k_s = dout("o_k_s", [128, 512])
    o_v_s = dout("o_v_s", [128, 512])
    TMAX = np_tok + 128
    Xs = dscr("Xs", [TMAX, D], F32)
    Ms = dscr("Ms", [TMAX, D], F32)
    PROJ = dscr("PROJ", [TMAX, 6144], BF16)
    GLs = dscr("GLs", [16, TMAX], F32)
    ATs = dscr("ATs", [FC, 128, TMAX], BF16)
    CARRY_S = dscr("CARRY_S", [2, 128, 8, 512], F32)
    NPC = NPT // 64
    VSH = dscr("VSH", [(8 + NPC + 18) * 64, 512], BF16)
    KTs = dscr("KTs", [4, 128, 512 + NPT], BF16)
    KTss = dscr("KTss", [2, 4, 128, 576], BF16)
    QTs = dscr("QTs", [8, 128, 2 * TMAX], BF16)
    EREL = dscr("EREL", [16, 640], F32)
    ZREL = dscr("ZREL", [16, 41024], F32)
    dbg_out = None
    DL = 0
    if dbg is not None:
        if ':' in dbg[0]:
            DL = int(dbg[0].split(':')[1])
            dbg = (dbg[0].split(':')[0],) + tuple(dbg[1:])
        dbg_out = dout("dbg", dbg[1], dbg[2])

    es = ExitStack()
    with es:
        kb = KB(nc, es)
        op, dma = kb.op, kb.dma
        PE, ACT, DVE, POOL = nc.tensor, nc.scalar, nc.vector, nc.gpsimd

        def sb(name, shape, dtype=F32):
            return es.enter_context(nc.sbuf_tensor(name, list(shape), dtype))

        psb = [es.enter_context(nc.psum_tensor("ps%d" % i, [128, 512], F32)) for i in range(8)]
        PB = [Buf("ps%d" % i) for i in range(8)]

        def psbf(i):
            return psb[i][:].bitcast(BF16)

        identf = sb("identf", [128, 128], F32)
        ident = sb("ident", [128, 128], BF16)
        c_one = sb("c_one", [128, 1], F32)
        c_eps = sb("c_eps", [128, 1], F32)
        c_ln16 = sb("c_ln16", [128, 1], F32)
        triI = sb("triI", [64, 64], F32)
        triU = sb("triU", [64, 64], F32)
        maskT = sb("maskT", [64, 4, 64], F32)
        B_const = Buf("const")

        def cst(fn):
            op("pool", fn, writes=[B_const])

        cst(lambda: POOL.memset(identf[:], 0.0))
        cst(lambda: POOL.affine_select(out=identf[:], in_=identf[:], pattern=[[-1, 128]], compare_op=ALU.not_equal,
                                       fill=1.0, base=0, channel_multiplier=1))
        cst(lambda: POOL.tensor_copy(out=ident[:], in_=identf[:]))
        cst(lambda: POOL.memset(c_one[:], 1.0))
        cst(lambda: POOL.memset(c_eps[:], EPS))
        cst(lambda: POOL.memset(c_ln16[:], float(np.log(1.0 / 16.0))))
        cst(lambda: POOL.memset(triI[:], -1.0 / 16.0))
        cst(lambda: POOL.affine_select(out=triI[:], in_=triI[:], pattern=[[1, 64]], compare_op=ALU.is_ge, fill=0.0,
                                       base=0, channel_multiplier=-1))
        cst(lambda: POOL.memset(triU[:], -1.0 / 16.0))
        cst(lambda: POOL.affine_select(out=triU[:], in_=triU[:], pattern=[[-1, 64]], compare_op=ALU.is_gt, fill=0.0,
                                       base=0, channel_multiplier=1))
        cst(lambda: POOL.memset(maskT[:], 1.0))
        cst(lambda: POOL.affine_select(out=maskT[:], in_=maskT[:], pattern=[[0, 4], [1, 64]], compare_op=ALU.is_ge,
                                       fill=0.0, base=0, channel_multiplier=-1))

        NG = sb("NG", [128, 256], F32)
        KVN = sb("KVN", [128, 16], F32)
        HNx = sb("HNx", [128, 2, 4, 4], F32)
        CG = sb("CG", [128, 4, FC, 2], F32)
        prm = sb("prm", [128, 2, 128], F32)
        prm2 = sb("prm2", [24, 128], F32)
        B_prm = Buf("prm")
        B_par = Buf("par")
        B_CG = Buf("cg")
        dma(prm[:], norm_gains.rearrange("l i (k p) -> (l i k) p", p=128).rearrange("(a r) p -> r a p", r=128),
            writes=[B_prm])
        dma(prm2[0:16, :], kv_norm.rearrange("(k p) -> k p", p=128), writes=[B_prm])
        dma(prm2[16:24, :], gla_head_norm.rearrange("l (k p) -> (l k) p", p=128), writes=[B_prm])
        for a in range(2):
            op("pe", lambda a=a: PE.transpose(out=psb[0][:, a * 128:(a + 1) * 128], in_=prm[:, a, :], identity=identf[:]),
               reads=[B_prm, B_const], writes=[PB[0]])
        op("pe", lambda: PE.transpose(out=psb[0][:, 256:280], in_=prm2[0:24, :], identity=identf[0:24, 0:24]),
           reads=[B_prm, B_const], writes=[PB[0]])
        op("dve", lambda: DVE.tensor_copy(out=NG[:], in_=psb[0][:, 0:256]), reads=[PB[0]], writes=[B_par])
        op("dve", lambda: DVE.tensor_copy(out=KVN[:], in_=psb[0][:, 256:272]), reads=[PB[0]], writes=[B_par])
        for l_ in range(2):
            op("dve", lambda l_=l_: DVE.tensor_copy(
                out=HNx[:, l_, :, :],
                in_=psb[0][:, 272 + l_ * 4:276 + l_ * 4].unsqueeze(1).broadcast_to([128, 4, 4])),
                reads=[PB[0]], writes=[B_par])
        op("pool", lambda: POOL.memset(CG[:], 0.0), writes=[B_CG])

        ACT_F = max(KC * TMAX // 2, 17408)
        WORK_F = 19456
        STG_F = 3 * 4096
        big = sb("big", [128, ACT_F + WORK_F + STG_F], F32)
        WOFF = ACT_F
        SOFF = ACT_F + WORK_F

        def cvf(off, n):
            return big[:, off:off + n]

        def cvb(off, n):
            return big[:, off:off + n // 2].bitcast(BF16)

        actT = cvb(0, KC * TMAX).rearrange("p (k t) -> p k t", k=KC)
        B_actc = [Buf("act%d" % c) for c in range(TMAX // 64)]
        stg = [cvf(SOFF + i * 4096, 4096).rearrange("p (k n) -> p k n", k=8) for i in range(3)]
        B_stg = [Buf("stg%d" % i) for i in range(3)]
        stg_rr = [0]
        cast_rr = [0]

        NTM = TMAX // 128
        B_Xs = [Buf() for _ in range(NTM)]
        B_Ms = [Buf() for _ in range(NTM)]
        B_PROJ = [Buf() for _ in range(NTM)]
        B_GLs = Buf()
        B_ATs = Buf()
        B_CS = [Buf(), Buf()]
        B_VSH = Buf()
        B_KTs = Buf()
        B_QTs = Buf()
        B_EREL = Buf()
        B_ZREL = Buf()

        def actbufs(tt):
            return [B_actc[2 * tt], B_actc[2 * tt + 1]]

        def load_wblock(wb_ap, B_wb, Wrows, kcs, c0, ncols, gain=None):
            for k0 in range(0, kcs, 8):
                nk = min(8, kcs - k0)
                i = stg_rr[0]
                stg_rr[0] = (i + 1) % 3
                src = Wrows[k0 * 128:(k0 + nk) * 128, c0:c0 + ncols].rearrange("(k p) n -> p k n", p=128)
                dma(stg[i][:, 0:nk, 0:ncols], src, writes=[B_stg[i]])
                ce = "act" if cast_rr[0] % 2 == 0 else "dve"
                cast_rr[0] += 1
                if gain is None:
                    if ce == "act":
                        op("act", lambda i=i, k0=k0, nk=nk: ACT.activation(out=wb_ap[:, k0:k0 + nk, 0:ncols],
                                                                          in_=stg[i][:, 0:nk, 0:ncols], func=AF.Copy),
                           reads=[B_stg[i]], writes=[B_wb])
                    else:
                        op("dve", lambda i=i, k0=k0, nk=nk: DVE.tensor_copy(out=wb_ap[:, k0:k0 + nk, 0:ncols],
                                                                           in_=stg[i][:, 0:nk, 0:ncols]),
                           reads=[B_stg[i]], writes=[B_wb])
                else:
                    for j in range(nk):
                        if ce == "act":
                            op("act", lambda i=i, j=j, k0=k0: ACT.activation(
                                out=wb_ap[:, k0 + j, 0:ncols], in_=stg[i][:, j, 0:ncols], func=AF.Copy,
                                scale=gain[:, k0 + j:k0 + j + 1]),
                                reads=[B_stg[i], B_par], writes=[B_wb], mark=(j == nk - 1))
                        else:
                            op("dve", lambda i=i, j=j, k0=k0: DVE.tensor_scalar(
                                out=wb_ap[:, k0 + j, 0:ncols], in0=stg[i][:, j, 0:ncols],
                                scalar1=gain[:, k0 + j:k0 + j + 1], scalar2=None, op0=ALU.mult),
                                reads=[B_stg[i], B_par], writes=[B_wb], mark=(j == nk - 1))

        def run_supertile(st):
            NS = 2 if st == 0 else 0
            T = np_tok + 64 * NS
            NT = T // 128
            NPCH = np_tok // 64
            first = (st == 0)
            last = (st == n_st - 1)
            tblocks = [(t0, min(512, np_tok - t0)) for t0 in range(0, np_tok, 512)]
            if NS:
                tblocks.append((np_tok, 128))
            segs = [(0, NPCH, "p", 0)] + [(NPCH + i, 1, "s", i) for i in range(NS)]

            def xrow(tt):
                if tt < np_tok // 128:
                    return st * np_tok + tt * 128
                return NPT

            def resnorm(x_src, Bx_src, m_src, gain_row, x_dst, Bx_dst, want_hT, is_out=False):
                kb.fence()
                W = WOFF
                xt = [cvf(W + 0, 2048), cvf(W + 2048, 2048)]
                mt = [cvf(W + 4096, 2048), cvf(W + 6144, 2048)]
                gB = cvf(W + 8192, 2048)
                xb = cvb(W + 10240, 2048)
                junk = cvb(W + 11264, 2048)
                small = cvf(W + 12288, 32)
                B_xt = [Buf(), Buf()]
                B_mt = [Buf(), Buf()]
                B_gb, B_xb, B_jk, B_sm = Buf(), Buf(), Buf(), [Buf(), Buf()]
                if m_src is not None:
                    dma(gB, gain_row.partition_broadcast(128), writes=[B_gb])
                for tt in range(NT):
                    p = tt % 2
                    sm = small[:, p * 8:(p + 1) * 8]
                    dma(xt[p], x_src(tt), reads=Bx_src(tt), writes=[B_xt[p]])
                    if m_src is not None:
                        dma(mt[p], m_src(tt), reads=[B_Ms[tt]], writes=[B_mt[p]])
                        op("act", lambda p=p, sm=sm: ACT.activation(out=junk, in_=mt[p], func=AF.Square,
                                                                     accum_out=sm[:, 0:1]),
                           reads=[B_mt[p]], writes=[B_jk, B_sm[p]])
                        op("act", lambda sm=sm: ACT.activation(out=sm[:, 1:2], in_=sm[:, 0:1], func=AF.Sqrt,
                                                               scale=1.0 / D, bias=c_eps[:]),
                           reads=[B_sm[p], B_const], writes=[B_sm[p]])
                        op("dve", lambda sm=sm: DVE.reciprocal(out=sm[:, 2:3], in_=sm[:, 1:2]),
                           reads=[B_sm[p]], writes=[B_sm[p]])
                        op("dve", lambda p=p, sm=sm: DVE.scalar_tensor_tensor(
                            out=mt[p], in0=mt[p], scalar=sm[:, 2:3], in1=gB, op0=ALU.mult, op1=ALU.mult),
                            reads=[B_sm[p], B_gb, B_mt[p]], writes=[B_mt[p]])
                        op("dve", lambda p=p: DVE.tensor_tensor(out=xt[p], in0=xt[p], in1=mt[p], op=ALU.add),
                           reads=[B_mt[p], B_xt[p]], writes=[B_xt[p]])
                    if x_dst is not None:
                        dma(x_dst(tt), xt[p], reads=[B_xt[p]], writes=Bx_dst(tt), is_out=is_out)
                    if want_hT:
                        op("act", lambda p=p, sm=sm: ACT.activation(out=junk, in_=xt[p], func=AF.Square,
                                                                     accum_out=sm[:, 3:4]),
                           reads=[B_xt[p]], writes=[B_jk, B_sm[p]])
                        op("act", lambda sm=sm: ACT.activation(out=sm[:, 4:5], in_=sm[:, 3:4], func=AF.Sqrt,
                                                               scale=1.0 / D, bias=c_eps[:]),
                           reads=[B_sm[p], B_const], writes=[B_sm[p]])
                        op("dve", lambda sm=sm: DVE.reciprocal(out=sm[:, 5:6], in_=sm[:, 4:5]),
                           reads=[B_sm[p]], writes=[B_sm[p]])
                        op("act", lambda p=p, sm=sm: ACT.activation(out=xb, in_=xt[p], func=AF.Copy, scale=sm[:, 5:6]),
                           reads=[B_sm[p], B_xt[p]], writes=[B_xb])
                        for g in range(2):
                            bank = 6 + g
                            pv = psbf(bank)
                            for j in range(8):
                                kc = g * 8 + j
                                op("pe", lambda kc=kc, j=j, pv=pv: PE.transpose(
                                    out=pv[:, j * 128:(j + 1) * 128], in_=xb[:, kc * 128:(kc + 1) * 128],
                                    identity=ident[:]),
                                    reads=[B_xb, B_const], writes=[PB[bank]], mark=(j == 7))
                            dst = actT[:, g * 8:(g + 1) * 8, tt * 128:(tt + 1) * 128]
                            srcv = pv.rearrange("p (k t) -> p k t", k=8)
                            if g == 0:
                                op("dve", lambda dst=dst, srcv=srcv: DVE.tensor_copy(out=dst, in_=srcv),
                                   reads=[PB[bank]], writes=actbufs(tt))
                            else:
                                op("act", lambda dst=dst, srcv=srcv: ACT.activation(out=dst, in_=srcv, func=AF.Copy),
                                   reads=[PB[bank]], writes=actbufs(tt))

            def gemm_setup():
                kb.fence()
                wbs = [cvb(WOFF + i * 4096, 8192).rearrange("p (k n) -> p k n", k=16) for i in range(2)]
                otf = [cvf(WOFF + 8192 + i * 512, 512) for i in range(4)]
                return wbs, [Buf(), Buf()], otf, [Buf() for _ in range(4)]

            def gemm_tok(Wrows, col_blocks, gain, evac, extra=None):
                wbs, B_wb, otf, B_ot = gemm_setup()
                rr = [0, 0]

                def next_ot():
                    i = rr[1]
                    rr[1] = (i + 1) % 4
                    return i, otf[i], B_ot[i]

                load_wblock(wbs[0], B_wb[0], Wrows, KC, col_blocks[0][0], col_blocks[0][1], gain)
                for bi, (c0, ncols) in enumerate(col_blocks):
                    cur = bi % 2
                    if bi + 1 < len(col_blocks):
                        load_wblock(wbs[1 - cur], B_wb[1 - cur], Wrows, KC, col_blocks[bi + 1][0],
                                    col_blocks[bi + 1][1], gain)
                    if extra is not None and extra(bi, c0, ncols, wbs[cur], B_wb[cur], next_ot):
                        continue
                    for tt in range(NT):
                        bank = rr[0] % 4
                        rr[0] += 1
                        for kc in range(KC):
                            op("pe", lambda kc=kc, tt=tt, bank=bank, cur=cur, ncols=ncols: PE.matmul(
                                psb[bank][:, 0:ncols], lhsT=actT[:, kc, tt * 128:(tt + 1) * 128],
                                rhs=wbs[cur][:, kc, 0:ncols], start=(kc == 0), stop=(kc == KC - 1)),
                                reads=actbufs(tt) + [B_wb[cur]], writes=[PB[bank]], mark=(kc == KC - 1))
                        evac(bi, c0, ncols, tt, bank, next_ot)

            def mm_feat(bank, wb, B_wb, f0, m, t0, n):
                c0_ = t0 // 64
                rb_ = [B_actc[c] for c in range(c0_, (t0 + n) // 64)]
                for kc in range(KC):
                    op("pe", lambda kc=kc: PE.matmul(psb[bank][0:m, 0:n], lhsT=wb[:, kc, f0:f0 + m],
                                                     rhs=actT[:, kc, t0:t0 + n], start=(kc == 0), stop=(kc == KC - 1)),
                       reads=rb_ + [B_wb], writes=[PB[bank]], mark=(kc == KC - 1))

            def evac_to_Ms(bi, c0, ncols, tt, bank, next_ot):
                i, o, Bo = next_ot()
                op("act", lambda o=o, bank=bank: ACT.activation(out=o, in_=psb[bank][:, 0:512], func=AF.Copy),
                   reads=[PB[bank]], writes=[Bo])
                dma(Ms[tt * 128:(tt + 1) * 128, c0:c0 + 512], o, reads=[Bo], writes=[B_Ms[tt]], q="act")

            def phaseA(l):
                gain = NG[:, (l * 4 + 0) * 16:(l * 4 + 0) * 16 + 16]
                import os
                blocks = [(c, 512) for c in range(0, 6144, 512)] + [(6144, 16)]
                if os.environ.get("DBG_NOG"): blocks = blocks[:int(os.environ["DBG_NOG"])]

                def evacA(bi, c0, ncols, tt, bank, next_ot):
                    i, o_, Bo = next_ot()
                    o = o_.bitcast(BF16)[:, 0:512]
                    fn = AF.Silu if c0 >= 4096 else AF.Copy
                    op("act", lambda o=o, bank=bank, fn=fn: ACT.activation(out=o, in_=psb[bank][:, 0:512], func=fn),
                       reads=[PB[bank]], writes=[Bo])
                    dma(PROJ[tt * 128:(tt + 1) * 128, c0:c0 + 512], o, reads=[Bo], writes=[B_PROJ[tt]], q="act")

                def extraA(bi, c0, ncols, wb, B_wb, next_ot):
                    if c0 != 6144:
                        return False
                    for (t0, n) in tblocks:
                        bank = 4
                        mm_feat(bank, wb, B_wb, 0, 16, t0, n)
                        i, o, Bo = next_ot()
                        op("act", lambda o=o, n=n: ACT.activation(out=o[0:16, 0:n], in_=psb[4][0:16, 0:n], func=AF.Copy),
                           reads=[PB[4]], writes=[Bo])
                        dma(GLs[:, t0:t0 + n], o[0:16, 0:n], reads=[Bo], writes=[B_GLs])
                    return True

                gemm_tok(gla_w_in[l], blocks, gain, evacA, extraA)

            def phaseB(l):
                kb.fence()
                W = WOFF
                Pb = [cvb(SOFF + i * 3072, 6144)[0:64, :] for i in range(3)]
                B_P = [Buf() for _ in range(3)]
                S = cvf(W + 0, 4096).rearrange("p (k v) -> p k v", k=8)
                Sb = cvb(W + 4096, 4096).rearrange("p (k v) -> p k v", k=8)
                spD = [cvf(W + 6144 + d * 4232, 1024)[0:64, :] for d in range(2)]
                XDD = [cvf(W + 7168 + d * 4232, 1024)[0:64, :] for d in range(2)]
                EQD = [cvf(W + 8192 + d * 4232, 512).rearrange("p (k t) -> p k t", k=8) for d in range(2)]
                EKD = [cvf(W + 8704 + d * 4232, 512).rearrange("p (k t) -> p k t", k=8) for d in range(2)]
                qtD = [cvb(W + 9216 + d * 4232, 512).rearrange("p (k t) -> p k t", k=8) for d in range(2)]
                ktD = [cvb(W + 9472 + d * 4232, 512).rearrange("p (k t) -> p k t", k=8) for d in range(2)]
                kpD = [cvb(W + 9728 + d * 4232, 1024)[0:64, :] for d in range(2)]
                amD = [cvb(W + 10240 + d * 4232, 256)[0:64, :].rearrange("p (h t) -> p h t", h=4) for d in range(2)]
                EendD = [cvf(W + 10368 + d * 4232, 8) for d in range(2)]
                osb = [cvf(W + 14608 + i * 512, 512)[0:64, :] for i in range(2)]
                on = cvb(W + 15632, 2048)[0:64, :]
                small = cvf(W + 16656, 32)
                junk = cvb(W + 16688, 512)[0:64, :]
                WG = cvf(W + 16944, 1024)[0:17, :]
                GLc = [cvf(W + 17968 + i * 64, 64)[0:17, :] for i in range(2)]
                B_S, B_Sb = Buf(), Buf()
                B_spD, B_XDD, B_EQD, B_EKD, B_qtD, B_ktD, B_kpD, B_amD, B_EendD = [[Buf(), Buf()] for _ in range(9)]
                B_osb = [Buf(), Buf()]
                B_on, B_small, B_junk, B_WG = Buf(), Buf(), Buf(), Buf()
                B_GLc = [Buf(), Buf()]
                dma(WG[0:16, :], gla_w_gate[l], writes=[B_WG])
                dma(WG[16:17, :], gla_b_gate[l].unsqueeze(0), writes=[B_WG])
                for i in range(2):
                    op("pool", lambda i=i: POOL.memset(GLc[i], 1.0), writes=[B_GLc[i]])
                def sel(d):
                    return (spD[d], XDD[d], EQD[d], EKD[d], qtD[d], ktD[d], kpD[d], amD[d], EendD[d],
                            B_spD[d], B_XDD[d], B_EQD[d], B_EKD[d], B_qtD[d], B_ktD[d], B_kpD[d], B_amD[d], B_EendD[d])

                def stage1(c, pi, gi, d):
                    sp, XD, EQ, EK, qt, kt, kp, am, Eend, B_sp, B_XD, B_EQ, B_EK, B_qt, B_kt, B_kp, B_am, B_Eend = sel(d)
                    P = Pb[pi]
                    dma(P, PROJ[c * 64:(c + 1) * 64, :], reads=[B_PROJ[c // 2]], writes=[B_P[pi]])
                    dma(GLc[gi][0:16, :], GLs[:, c * 64:(c + 1) * 64], reads=[B_GLs], writes=[B_GLc[gi]])
                    for hf in range(2):
                        op("pe", lambda hf=hf, gi=gi: PE.matmul(psb[hf][0:64, :], lhsT=GLc[gi], rhs=WG[:, hf * 512:(hf + 1) * 512],
                                                                start=True, stop=True),
                           reads=[B_GLc[gi], B_WG], writes=[PB[hf]])
                        op("act", lambda hf=hf: ACT.activation(out=sp[:, hf * 512:(hf + 1) * 512], in_=psb[hf][0:64, :],
                                                               func=AF.Exp, scale=-1.0),
                           reads=[PB[hf]], writes=[B_sp])
                    op("act", lambda: ACT.activation(out=sp, in_=sp, func=AF.Ln, bias=c_one[0:64, :]),
                       reads=[B_sp, B_const], writes=[B_sp])
                    bps = psb[2][:].rearrange("p (k t) -> p k t", k=8)
                    for kc in range(8):
                        op("pe", lambda kc=kc: PE.matmul(bps[:, kc, :], lhsT=sp[:, kc * 128:(kc + 1) * 128], rhs=triI[:],
                                                         start=True, stop=True),
                           reads=[B_sp, B_const], writes=[PB[2]], mark=(kc == 7))
                    for hf in range(2):
                        op("pe", lambda hf=hf: PE.matmul(psb[hf][0:64, :], lhsT=triU[:], rhs=sp[:, hf * 512:(hf + 1) * 512],
                                                         start=True, stop=True),
                           reads=[B_sp, B_const], writes=[PB[hf]])
                        op("act", lambda hf=hf: ACT.activation(out=XD[:, hf * 512:(hf + 1) * 512], in_=psb[hf][0:64, :],
                                                               func=AF.Exp),
                           reads=[PB[hf]], writes=[B_XD])
                    op("act", lambda: ACT.activation(out=EQ, in_=bps, func=AF.Exp, bias=c_ln16[:]),
                       reads=[PB[2], B_const], writes=[B_EQ])
                    op("act", lambda: ACT.activation(out=EK, in_=bps, func=AF.Exp, scale=-1.0),
                       reads=[PB[2]], writes=[B_EK])
                    op("act", lambda: ACT.activation(out=Eend, in_=bps[:, :, 63], func=AF.Exp),
                       reads=[PB[2]], writes=[B_Eend])
                    qv = psbf(3)[:, 0:512].rearrange("p (k t) -> p k t", k=8)
                    kv_ = psbf(3)[:, 512:1024].rearrange("p (k t) -> p k t", k=8)
                    for kc in range(8):
                        op("pe", lambda kc=kc: PE.transpose(out=qv[:, kc, :], in_=P[:, kc * 128:(kc + 1) * 128],
                                                            identity=ident[0:64, 0:64]),
                           reads=[B_P[pi], B_const], writes=[PB[3]], mark=False)
                    for kc in range(8):
                        op("pe", lambda kc=kc: PE.transpose(out=kv_[:, kc, :], in_=P[:, 1024 + kc * 128:1024 + (kc + 1) * 128],
                                                            identity=ident[0:64, 0:64]),
                           reads=[B_P[pi], B_const], writes=[PB[3]], mark=(kc == 7))
                    op("dve", lambda: DVE.tensor_tensor(out=qt, in0=qv, in1=EQ, op=ALU.mult),
                       reads=[PB[3], B_EQ], writes=[B_qt])
                    op("dve", lambda: DVE.tensor_tensor(out=kt, in0=kv_, in1=EK, op=ALU.mult),
                       reads=[PB[3], B_EK], writes=[B_kt])
                    op("dve", lambda: DVE.tensor_tensor(out=kp, in0=P[:, 1024:2048], in1=XD, op=ALU.mult),
                       reads=[B_P[pi], B_XD], writes=[B_kp])
                    aps = psb[4][0:64, 0:256].rearrange("p (h t) -> p h t", h=4)
                    for h in range(4):
                        for j in range(2):
                            op("pe", lambda h=h, j=j: PE.matmul(aps[:, h, :], lhsT=kt[:, 2 * h + j, :], rhs=qt[:, 2 * h + j, :],
                                                                start=(j == 0), stop=(j == 1)),
                               reads=[B_kt, B_qt], writes=[PB[4]], mark=(h == 3 and j == 1))
                    op("dve", lambda: DVE.tensor_tensor(out=am, in0=aps, in1=maskT[:], op=ALU.mult),
                       reads=[PB[4], B_const], writes=[B_am])

                def stage2(c, pi, d):
                    sp, XD, EQ, EK, qt, kt, kp, am, Eend, B_sp, B_XD, B_EQ, B_EK, B_qt, B_kt, B_kp, B_am, B_Eend = sel(d)
                    P = Pb[pi]
                    for h in range(4):
                        oi = h % 2
                        vh = P[:, 2048 + h * 512:2048 + (h + 1) * 512]
                        op("pe", lambda h=h, vh=vh: PE.matmul(psb[5][0:64, :], lhsT=am[:, h, :], rhs=vh, start=True, stop=False),
                           reads=[B_am, B_P[pi]], writes=[PB[5]], mark=False)
                        for j in range(2):
                            op("pe", lambda h=h, j=j: PE.matmul(psb[5][0:64, :], lhsT=qt[:, 2 * h + j, :], rhs=Sb[:, 2 * h + j, :],
                                                                start=False, stop=(j == 1)),
                               reads=[B_qt, B_Sb], writes=[PB[5]], mark=(j == 1))
                        op("act", lambda oi=oi: ACT.activation(out=osb[oi], in_=psb[5][0:64, :], func=AF.Copy),
                           reads=[PB[5]], writes=[B_osb[oi]])
                        op("act", lambda oi=oi, h=h: ACT.activation(out=junk, in_=osb[oi], func=AF.Square,
                                                                    accum_out=small[0:64, h:h + 1]),
                           reads=[B_osb[oi]], writes=[B_junk, B_small])
                        op("act", lambda h=h: ACT.activation(out=small[0:64, 4 + h:5 + h], in_=small[0:64, h:h + 1],
                                                             func=AF.Sqrt, scale=1.0 / 512, bias=c_eps[0:64, :]),
                           reads=[B_small, B_const], writes=[B_small])
                        op("dve", lambda h=h: DVE.reciprocal(out=small[0:64, 8 + h:9 + h], in_=small[0:64, 4 + h:5 + h]),
                           reads=[B_small], writes=[B_small])
                        op("dve", lambda h=h, oi=oi: DVE.scalar_tensor_tensor(
                            out=on[:, h * 512:(h + 1) * 512], in0=osb[oi], scalar=small[0:64, 8 + h:9 + h],
                            in1=P[:, 4096 + h * 512:4096 + (h + 1) * 512], op0=ALU.mult, op1=ALU.mult),
                            reads=[B_small, B_osb[oi], B_P[pi]], writes=[B_on])
                        for j in range(2):
                            kc = 2 * h + j
                            op("pe", lambda kc=kc, vh=vh: PE.matmul(psb[6][:, :], lhsT=kp[:, kc * 128:(kc + 1) * 128], rhs=vh,
                                                                    start=True, stop=True),
                               reads=[B_kp, B_P[pi]], writes=[PB[6]])
                            op("dve", lambda kc=kc: DVE.scalar_tensor_tensor(
                                out=S[:, kc, :], in0=S[:, kc, :], scalar=Eend[:, kc:kc + 1], in1=psb[6][:, :],
                                op0=ALU.mult, op1=ALU.add),
                                reads=[B_Eend, PB[6], B_S, B_Sb], writes=[B_S])
                            op("act", lambda kc=kc: ACT.activation(out=Sb[:, kc, :], in_=S[:, kc, :], func=AF.Copy),
                               reads=[B_S], writes=[B_Sb])
                    tv = psbf(7).rearrange("p (k t) -> p k t", k=16)
                    for kc in range(16):
                        op("pe", lambda kc=kc: PE.transpose(out=tv[:, kc, :], in_=on[:, kc * 128:(kc + 1) * 128],
                                                            identity=ident[0:64, 0:64]),
                           reads=[B_on, B_const], writes=[PB[7]], mark=(kc == 15))
                    op("dve", lambda c=c: DVE.tensor_copy(out=actT[:, :, c * 64:(c + 1) * 64], in_=tv),
                       reads=[PB[7]], writes=[B_actc[c]])

                items = []
                for (c0, ncs, kind, sidx) in segs:
                    for c in range(c0, c0 + ncs):
                        items.append((c, kind, sidx, c == c0, c == c0 + ncs - 1))

                def seg_init(kind, sidx):

                    if kind == "p":
                        if first:
                            op("pool", lambda: POOL.memset(S, 0.0), writes=[B_S])
                        else:
                            dma(S, CARRY_S[l], reads=[B_CS[l]], writes=[B_S])
                    else:
                        dma(S, gla_s0[l, sidx].rearrange("h (j p) v -> p (h j) v", p=128), writes=[B_S])
                    op("act", lambda: ACT.activation(out=Sb, in_=S, func=AF.Copy), reads=[B_S], writes=[B_Sb])

                def seg_final(kind, sidx):

                    sview = lambda ap: ap.rearrange("h (j p) v -> p (h j) v", p=128)
                    if kind == "p":
                        if last:
                            dma(sview(o_gla_p[l]), S, reads=[B_S], is_out=True)
                        else:
                            dma(CARRY_S[l], S, reads=[B_S], writes=[B_CS[l]])
                    else:
                        dma(sview(o_gla_s[l, sidx]), S, reads=[B_S], is_out=True)


                for i, (c, kind, sidx, sfirst, slast) in enumerate(items):
                    if i == 0:
                        stage1(c, i % 3, i % 2, i % 2)
                    if i + 1 < len(items):
                        stage1(items[i + 1][0], (i + 1) % 3, (i + 1) % 2, (i + 1) % 2)
                    if sfirst:
                        seg_init(kind, sidx)
                    stage2(c, i % 3, i % 2)
                    if slast:
                        seg_final(kind, sidx)

            def phaseC(Wrows, gain):
                gemm_tok(Wrows, [(c, 512) for c in range(0, D, 512)], gain, evac_to_Ms)

            def phaseE(l):
                kb.fence()
                W = WOFF
                prw = cvf(SOFF, DFF)[0:8, :]
                B_prw = Buf()
                cp = cvf(W + 17000, FC * 8).rearrange("p (f j) -> p f j", f=FC)
                GT = cvf(W + 17352, FC * 6).rearrange("p (f j) -> p f j", f=FC)
                B_cp, B_GT = Buf(), Buf()
                op("pool", lambda: POOL.memset(prw, 0.0), writes=[B_prw])
                dma(prw[0:3, :], ffn_conv_w[l], writes=[B_prw])
                dma(prw[3:4, :], ffn_conv_b[l].unsqueeze(0), writes=[B_prw])
                if NS:
                    dma(prw[4:8, :], conv_s0[l].rearrange("s j f -> (s j) f"), writes=[B_prw])
                for f in range(FC):
                    op("pe", lambda f=f: PE.transpose(out=psb[0][:, f * 8:(f + 1) * 8], in_=prw[:, f * 128:(f + 1) * 128],
                                                      identity=identf[0:8, 0:8]),
                       reads=[B_prw, B_const], writes=[PB[0]], mark=(f == FC - 1))
                op("dve", lambda: DVE.tensor_copy(out=cp, in_=psb[0][:, 0:FC * 8].rearrange("p (f j) -> p f j", f=FC)),
                   reads=[PB[0]], writes=[B_cp])
                kb.fence()
                wbs = [cvb(W + i * 4096, 8192).rearrange("p (g k n) -> p g k n", g=2, k=16) for i in range(2)]
                B_wb = [Buf(), Buf()]
                TG = T + 6
                Gx = cvf(W + 8192, TG)
                Vx = [cvb(W + 8192 + 2184 + i * 1088, 2176)[:, 0:T] for i in range(2)]
                Cx = cvf(W + 8192 + 2184 + 2176, 2176)[:, 0:T]
                At = [cvb(W + 8192 + 2184 + 2176 + 2176 + i * 1088, 2176)[:, 0:T] for i in range(2)]
                B_Gx, B_Cx = Buf(), Buf()
                B_Vx = [Buf(), Buf()]
                B_At = [Buf(), Buf()]
                goff = {}
                for (c0, ncs, kind, sidx) in segs:
                    goff[(kind, sidx)] = 0 if kind == "p" else np_tok + 2 + 66 * sidx
                gain = NG[:, (l * 4 + 2) * 16:(l * 4 + 2) * 16 + 16]
                Wup = ffn_w_up[l]
                NG2 = DFF // 256

                def load_grp(g, slot):
                    load_wblock(wbs[slot][:, 0, :, :], B_wb[slot], Wup, KC, g * 256, 256, gain)
                    load_wblock(wbs[slot][:, 1, :, :], B_wb[slot], Wup, KC, DFF + g * 256, 256, gain)

                load_grp(0, 0)
                fbi = 0
                for g in range(NG2):
                    cur = g % 2
                    if g + 1 < NG2:
                        load_grp(g + 1, 1 - cur)
                    for fl in range(2):
                        fb = g * 2 + fl
                        vi = fbi % 2
                        fbi += 1
                        if first:
                            op("pool", lambda: POOL.memset(Gx[:, 0:2], 0.0), writes=[B_Gx])
                        else:
                            op("pool", lambda fb=fb: POOL.tensor_copy(out=Gx[:, 0:2], in_=CG[:, l, fb, :]),
                               reads=[B_CG], writes=[B_Gx])
                        for i in range(NS):
                            o_ = goff[("s", i)]
                            op("pool", lambda fb=fb, i=i, o_=o_: POOL.tensor_copy(out=Gx[:, o_:o_ + 2], in_=cp[:, fb, 4 + 2 * i:6 + 2 * i]),
                               reads=[B_cp], writes=[B_Gx])
                        for bi, (t0, n) in enumerate(tblocks):
                            bg = bi % 3
                            bv = 3 + bi % 3
                            mm_feat(bg, wbs[cur][:, 0, :, :], B_wb[cur], fl * 128, 128, t0, n)
                            mm_feat(bv, wbs[cur][:, 1, :, :], B_wb[cur], fl * 128, 128, t0, n)
                            if t0 < np_tok:
                                op("act", lambda bg=bg, t0=t0, n=n: ACT.activation(out=Gx[:, 2 + t0:2 + t0 + n], in_=psb[bg][:, 0:n], func=AF.Copy),
                                   reads=[PB[bg]], writes=[B_Gx])
                            else:
                                for i in range(NS):
                                    o_ = goff[("s", i)] + 2
                                    op("act", lambda bg=bg, i=i, o_=o_: ACT.activation(out=Gx[:, o_:o_ + 64], in_=psb[bg][:, i * 64:(i + 1) * 64], func=AF.Copy),
                                       reads=[PB[bg]], writes=[B_Gx])
                            op("dve", lambda bv=bv, t0=t0, n=n, vi=vi: DVE.tensor_copy(out=Vx[vi][:, t0:t0 + n], in_=psb[bv][:, 0:n]),
                               reads=[PB[bv]], writes=[B_Vx[vi]])
                        for (c0, ncs, kind, sidx) in segs:
                            o_ = goff[(kind, sidx)]
                            n = ncs * 64
                            t0 = c0 * 64
                            op("dve", lambda fb=fb, o_=o_, n=n, t0=t0: DVE.tensor_scalar(
                                out=Cx[:, t0:t0 + n], in0=Gx[:, o_ + 2:o_ + 2 + n], scalar1=cp[:, fb, 2:3], scalar2=cp[:, fb, 3:4],
                                op0=ALU.mult, op1=ALU.add), reads=[B_Gx, B_cp], writes=[B_Cx])
                            op("dve", lambda fb=fb, o_=o_, n=n, t0=t0: DVE.scalar_tensor_tensor(
                                out=Cx[:, t0:t0 + n], in0=Gx[:, o_ + 1:o_ + 1 + n], scalar=cp[:, fb, 1:2], in1=Cx[:, t0:t0 + n],
                                op0=ALU.mult, op1=ALU.add), reads=[B_Gx, B_cp, B_Cx], writes=[B_Cx])
                            op("dve", lambda fb=fb, o_=o_, n=n, t0=t0: DVE.scalar_tensor_tensor(
                                out=Cx[:, t0:t0 + n], in0=Gx[:, o_:o_ + n], scalar=cp[:, fb, 0:1], in1=Cx[:, t0:t0 + n],
                                op0=ALU.mult, op1=ALU.add), reads=[B_Gx, B_cp, B_Cx], writes=[B_Cx])
                            j0 = 0 if kind == "p" else 2 + 2 * sidx
                            op("pool", lambda fb=fb, o_=o_, n=n, j0=j0: POOL.tensor_copy(out=GT[:, fb, j0:j0 + 2], in_=Gx[:, o_ + n:o_ + n + 2]),
                               reads=[B_Gx], writes=[B_GT])
                            if kind == "p":
                                op("pool", lambda fb=fb, o_=o_, n=n: POOL.tensor_copy(out=CG[:, l, fb, :], in_=Gx[:, o_ + n:o_ + n + 2]),
                                   reads=[B_Gx], writes=[B_CG])
                        op("act", lambda: ACT.activation(out=Cx, in_=Cx, func=AF.Silu), reads=[B_Cx], writes=[B_Cx])
                        op("dve", lambda vi=vi: DVE.tensor_tensor(out=At[vi], in0=Cx, in1=Vx[vi], op=ALU.mult),
                           reads=[B_Cx, B_Vx[vi]], writes=[B_At[vi]])
                        dma(ATs[fb][:, 0:T], At[vi], reads=[B_At[vi]], writes=[B_ATs])
                kb.fence()
                orow = cvf(SOFF, 2048)[0:6, :]
                B_orow = Buf()
                for r0 in range(0, FC, 16):
                    nf = min(16, FC - r0)
                    for f in range(nf):
                        bank = f // 4
                        op("pe", lambda f=f, r0=r0, bank=bank: PE.transpose(
                            out=psb[bank][0:6, (f % 4) * 128:(f % 4 + 1) * 128], in_=GT[:, r0 + f, :], identity=identf[:]),
                            reads=[B_GT, B_const], writes=[PB[bank]], mark=(f % 4 == 3 or f == nf - 1))
                    for bank in range((nf + 3) // 4):
                        op("dve", lambda bank=bank: DVE.tensor_copy(out=orow[:, bank * 512:(bank + 1) * 512], in_=psb[bank][0:6, :]),
                           reads=[PB[bank]], writes=[B_orow])
                    cs = slice(r0 * 128, (r0 + nf) * 128)
                    if last:
                        dma(o_conv_p[l][:, cs], orow[0:2, 0:nf * 128], reads=[B_orow], is_out=True)
                    for i in range(NS):
                        dma(o_conv_s[l, i][:, cs], orow[2 + 2 * i:4 + 2 * i, 0:nf * 128], reads=[B_orow], is_out=True)

            def phaseF(l):
                kb.fence()
                wbs = [cvb(i * 11264, 22528).rearrange("p (k n) -> p k n", k=FC) for i in range(2)]
                aT = [cvb(22528 + i * 5632, 11264).rearrange("p (k t) -> p k t", k=FC) for i in range(2)]
                otf = [cvf(33792 + i * 512, 512) for i in range(4)]
                B_wb = [Buf(), Buf()]
                B_aT = [Buf(), Buf()]
                B_ot = [Buf() for _ in range(4)]
                Wd = ffn_w_down[l]
                load_wblock(wbs[0], B_wb[0], Wd, FC, 0, 512, None)
                rr = 0
                ai = 0
                tb2 = [(t, min(2, NT - t)) for t in range(0, NT, 2)]
                for nb in range(4):
                    cur = nb % 2
                    if nb + 1 < 4:
                        load_wblock(wbs[1 - cur], B_wb[1 - cur], Wd, FC, (nb + 1) * 512, 512, None)
                    for (tt0, ntl) in tb2:
                        a = ai % 2
                        ai += 1
                        dma(aT[a][:, :, 0:ntl * 128], ATs[:, :, tt0 * 128:(tt0 + ntl) * 128].rearrange("k p t -> p k t"),
                            reads=[B_ATs], writes=[B_aT[a]])
                        for j in range(ntl):
                            tt = tt0 + j
                            bank = rr % 4
                            oi = rr % 4
                            rr += 1
                            for kc in range(FC):
                                op("pe", lambda kc=kc, a=a, bank=bank, cur=cur, j=j: PE.matmul(
                                    psb[bank][:, :], lhsT=aT[a][:, kc, j * 128:(j + 1) * 128], rhs=wbs[cur][:, kc, :],
                                    start=(kc == 0), stop=(kc == FC - 1)),
                                    reads=[B_aT[a], B_wb[cur]], writes=[PB[bank]], mark=(kc == FC - 1))
                            op("act", lambda oi=oi, bank=bank: ACT.activation(out=otf[oi], in_=psb[bank][:, :], func=AF.Copy),
                               reads=[PB[bank]], writes=[B_ot[oi]])
                            dma(Ms[tt * 128:(tt + 1) * 128, nb * 512:(nb + 1) * 512], otf[oi], reads=[B_ot[oi]], writes=[B_Ms[tt]], q="act")

            def phaseKV():
                wbs, B_wb, otf, B_ot = gemm_setup()
                rr = [0, 0]

                def next_ot():
                    i = rr[1]
                    rr[1] = (i + 1) % 4
                    return i, otf[i], B_ot[i]

                load_wblock(wbs[0], B_wb[0], att_w_kv, KC, 0, 512, KVN)
                load_wblock(wbs[1], B_wb[1], att_w_kv, KC, 512, 512, KVN)

                def out_rows(tt):
                    if tt >= np_tok // 128:
                        return ("s", 0)
                    g0 = st * np_tok + tt * 128
                    if g0 >= NPT - 512:
                        return ("p", g0 - (NPT - 512))
                    return None

                if first:
                    i, o_, Bo = next_ot()
                    zt = o_.bitcast(BF16)[:, 0:512]
                    op("pool", lambda zt=zt: POOL.memset(zt, 0.0), writes=[Bo])
                    for h in range(4):
                        dma(KTs[h][:, 0:512], zt, reads=[Bo], writes=[B_KTs])
                        dma(VSH[h * 128:(h + 1) * 128, :], zt, reads=[Bo], writes=[B_VSH])
                for fb in range(4):
                    for (t0, n) in tblocks:
                        bank = rr[0] % 4
                        rr[0] += 1
                        mm_feat(bank, wbs[0], B_wb[0], fb * 128, 128, t0, n)
                        i, o_, Bo = next_ot()
                        o = o_.bitcast(BF16)
                        op("act", lambda o=o, bank=bank, n=n: ACT.activation(out=o[:, 0:n], in_=psb[bank][:, 0:n], func=AF.Copy),
                           reads=[PB[bank]], writes=[Bo])
                        if t0 < np_tok:
                            g0 = 512 + st * np_tok + t0
                            dma(KTs[fb][:, g0:g0 + n], o[:, 0:n], reads=[Bo], writes=[B_KTs], q="act")
                        else:
                            for si in range(NS):
                                dma(KTss[si, fb][:, 512:576], o[:, si * 64:(si + 1) * 64], reads=[Bo], writes=[B_KTs])
                import os
                if os.environ.get('DBG_KV') == '1':
                    return
                for tt in range(NT):
                    orow = out_rows(tt)
                    if orow is not None:
                        bank = rr[0] % 4
                        rr[0] += 1
                        for kc in range(KC):
                            op("pe", lambda kc=kc, tt=tt, bank=bank: PE.matmul(
                                psb[bank][:, :], lhsT=actT[:, kc, tt * 128:(tt + 1) * 128], rhs=wbs[0][:, kc, :],
                                start=(kc == 0), stop=(kc == KC - 1)),
                                reads=actbufs(tt) + [B_wb[0]], writes=[PB[bank]], mark=(kc == KC - 1))
                        i, o, Bo = next_ot()
                        op("act", lambda o=o, bank=bank: ACT.activation(out=o, in_=psb[bank][:, :], func=AF.Copy),
                           reads=[PB[bank]], writes=[Bo])
                        dst = o_k_p[orow[1]:orow[1] + 128, :] if orow[0] == "p" else o_k_s
                        dma(dst, o, reads=[Bo], is_out=True)
                    bank = rr[0] % 4
                    rr[0] += 1
                    for kc in range(KC):
                        op("pe", lambda kc=kc, tt=tt, bank=bank: PE.matmul(
                            psb[bank][:, :], lhsT=actT[:, kc, tt * 128:(tt + 1) * 128], rhs=wbs[1][:, kc, :],
                            start=(kc == 0), stop=(kc == KC - 1)),
                            reads=actbufs(tt) + [B_wb[1]], writes=[PB[bank]], mark=(kc == KC - 1))
                    vsrc, vB = psb[bank][:, :], PB[bank]
                    if orow is not None:
                        i, of, Bof = next_ot()
                        op("act", lambda of=of, bank=bank: ACT.activation(out=of, in_=psb[bank][:, :], func=AF.Copy),
                           reads=[PB[bank]], writes=[Bof])
                        dst = o_v_p[orow[1]:orow[1] + 128, :] if orow[0] == "p" else o_v_s
                        dma(dst, of, reads=[Bof], is_out=True)
                        vsrc, vB = of, Bof
                    i, o_, Bo = next_ot()
                    o = o_.bitcast(BF16)[:, 0:512]
                    op("dve", lambda o=o, vsrc=vsrc: DVE.tensor_copy(out=o, in_=vsrc), reads=[vB], writes=[Bo])
                    if tt < np_tok // 128:
                        r0 = 512 + st * np_tok + tt * 128
                        dma(VSH[r0:r0 + 128, :], o, reads=[Bo], writes=[B_VSH])
                    else:
                        for si in range(NS):
                            r0 = (8 + NPC + 9 * si) * 64 + 512
                            dma(VSH[r0:r0 + 64, :], o[si * 64:(si + 1) * 64, :], reads=[Bo], writes=[B_VSH])
                if os.environ.get('DBG_KV') == '2':
                    return
                if NS:
                    kb.fence()
                    cf = [cvf(SOFF + i * 512, 512) for i in range(2)]
                    cb_ = [cvb(SOFF + 1024 + i * 256, 512) for i in range(2)]
                    kt_ = [cvb(SOFF + 1536 + i * 256, 512).rearrange("p (h t) -> p h t", h=4) for i in range(2)]
                    B_cf, B_cb, B_kt = [Buf(), Buf()], [Buf(), Buf()], [Buf(), Buf()]
                    n_ = 0
                    for si in range(NS):
                        for rt in range(4):
                            for which in range(2):
                                b = n_ % 2
                                n_ += 1
                                src = (ck if which == 0 else cv)[si][rt * 128:(rt + 1) * 128, :]
                                dma(cf[b], src, writes=[B_cf[b]])
                                op("dve", lambda b=b: DVE.tensor_copy(out=cb_[b], in_=cf[b]), reads=[B_cf[b]], writes=[B_cb[b]])
                                if which == 1:
                                    r0 = (8 + NPC + 9 * si) * 64 + rt * 128
                                    dma(VSH[r0:r0 + 128, :], cb_[b], reads=[B_cb[b]], writes=[B_VSH])
                                else:
                                    pv = psbf(4 + b)
                                    for h in range(4):
                                        op("pe", lambda h=h, b=b, pv=pv: PE.transpose(out=pv[:, h * 128:(h + 1) * 128], in_=cb_[b][:, h * 128:(h + 1) * 128],
                                                                                      identity=ident[:]),
                                           reads=[B_cb[b], B_const], writes=[PB[4 + b]], mark=(h == 3))
                                    op("act", lambda b=b, pv=pv: ACT.activation(out=kt_[b], in_=pv[:, 0:512].rearrange("p (h t) -> p h t", h=4), func=AF.Copy),
                                       reads=[PB[4 + b]], writes=[B_kt[b]])
                                    dma(KTss[si][:, :, rt * 128:(rt + 1) * 128].rearrange("h p t -> p h t"), kt_[b], reads=[B_kt[b]], writes=[B_KTs])

            def phaseQ(l, j):
                wbs, B_wb, otf, B_ot = gemm_setup()
                gain = NG[:, (l * 4 + 0) * 16:(l * 4 + 0) * 16 + 16]
                Wq = att_w_q[j]
                load_wblock(wbs[0], B_wb[0], Wq, KC, 0, 512, gain)
                rr = 0
                for blk in range(4):
                    cur = blk % 2
                    if blk + 1 < 4:
                        load_wblock(wbs[1 - cur], B_wb[1 - cur], Wq, KC, (blk + 1) * 512, 512, gain)
                    for pl in range(2):
                        pair = blk * 2 + pl
                        for (t0, n) in tblocks:
                            oi = rr % 4
                            o = otf[oi].bitcast(BF16)[:, 0:2 * n].rearrange("p (c h t) -> p c h t", h=2, t=64)
                            for h2 in range(2):
                                bank = (2 * rr + h2) % 4
                                mm_feat(bank, wbs[cur], B_wb[cur], (pl * 2 + h2) * 128, 128, t0, n)
                                op("act", lambda o=o, bank=bank, n=n, h2=h2: ACT.activation(
                                    out=o[:, :, h2, :], in_=psb[bank][:, 0:n].rearrange("p (c t) -> p c t", t=64), func=AF.Copy,
                                    scale=float(128 ** -0.5)),
                                    reads=[PB[bank]], writes=[B_ot[oi]])
                            rr += 1
                            dma(QTs[pair][:, 2 * t0:2 * (t0 + n)], otf[oi].bitcast(BF16)[:, 0:2 * n], reads=[B_ot[oi]], writes=[B_QTs], q="act")

            def phaseATT(l, j):
                kb.fence()
                W = WOFF
                Bt = cvf(W + 0, 4608).rearrange("p (g k) -> p g k", g=8)
                qb = [cvb(W + 4608 + i * 2048, 4096).rearrange("p (g t) -> p g t", g=8) for i in range(2)]
                KTb = [cvb(W + 8704 + i * 1152, 2304).rearrange("p (h t) -> p h t", h=4) for i in range(2)]
                Vb = [cvb(W + 11008 + i * 2304, 4608)[0:64, :].rearrange("p (c n) -> p c n", c=9) for i in range(2)]
                sbf = [cvf(W + 15616 + i * 576, 576) for i in range(2)]
                pn = [cvb(W + 16768 + i * 288, 576) for i in range(2)]
                pTs = [cvb(W + 17344 + i * 576, 1152)[0:64, :].rearrange("p (c t) -> p c t", c=9) for i in range(2)]
                small = cvf(W + 18496, 64)
                rbt = cvf(SOFF, 513)[0:16, :]
                et = cvf(SOFF + 520, 640)[0:16, :]
                B_Bt, B_rbt, B_et = Buf(), Buf(), Buf()
                B_qb, B_KTb, B_Vb = [Buf(), Buf()], [Buf(), Buf()], [Buf(), Buf()]
                B_sbf, B_pn, B_pTs, B_sm = [Buf(), Buf()], [Buf(), Buf()], [Buf(), Buf()], [Buf(), Buf()]
                dma(rbt, att_rel_bias[j], writes=[B_rbt])
                op("dve", lambda: DVE.memset(et, 0.0), writes=[B_et])
                op("dve", lambda: DVE.tensor_copy(out=et[:, 0:319], in_=rbt[:, 512:513].broadcast_to([16, 319])),
                   reads=[B_rbt], writes=[B_et])
                op("dve", lambda: DVE.tensor_copy(out=et[:, 319:639], in_=rbt[:, 512:192:-1]), reads=[B_rbt], writes=[B_et])
                dma(EREL, et, reads=[B_et], writes=[B_EREL])
                for h in range(16):
                    dma(bass.AP(ZREL.tensor, h * 41024, [[641, 64], [1, 640]]), EREL[h].partition_broadcast(64),
                        reads=[B_EREL], writes=[B_ZREL])
                for h in range(16):
                    src = bass.AP(ZREL.tensor, h * 41024 + 63, [[640, 64], [1, 576]])
                    dma(Bt[(h % 2) * 64:(h % 2 + 1) * 64, h // 2, :], src, reads=[B_ZREL], writes=[B_Bt])
                cidx = 0
                pidx = 0
                qi = -1
                pendB = [None]
                qblk_of = {}
                for (c0, ncs, kind, sidx) in segs:
                    for c in range(c0, c0 + ncs):
                        qb0 = (c // 4) * 4
                        if kind == "s":
                            qb0 = segs[1][0]
                        if qb0 not in qblk_of:
                            qi += 1
                            nqc = min(4, (NPCH if kind == "p" else NPCH + NS) - qb0)
                            qblk_of[qb0] = qi % 2
                            dma(qb[qi % 2][:, :, 0:nqc * 128], QTs[:, :, qb0 * 128:(qb0 + nqc) * 128].rearrange("g p t -> p g t"),
                                reads=[B_QTs], writes=[B_qb[qi % 2]])
                        qsel = qblk_of[qb0]
                        cl = c - qb0
                        bi = cidx % 2
                        cidx += 1
                        if kind == "p":
                            gc = st * NPCH + c
                            j0 = max(0, 8 - gc) * 64
                            dma(KTb[bi], KTs[:, :, gc * 64:gc * 64 + 576].rearrange("h p t -> p h t"), reads=[B_KTs], writes=[B_KTb[bi]])
                            dma(Vb[bi], VSH[gc * 64:gc * 64 + 576, :].rearrange("(c p) n -> p c n", p=64), reads=[B_VSH], writes=[B_Vb[bi]])
                        else:
                            j0 = 0
                            r0 = (8 + NPC + 9 * sidx) * 64
                            dma(KTb[bi], KTss[sidx].rearrange("h p t -> p h t"), reads=[B_KTs], writes=[B_KTb[bi]])
                            dma(Vb[bi], VSH[r0:r0 + 576, :].rearrange("(c p) n -> p c n", p=64), reads=[B_VSH], writes=[B_Vb[bi]])
                        jb0 = j0 // 64
                        for p8 in range(8):
                            n_ = p8 // 2
                            pi = pidx % 2
                            pidx += 1
                            bA, bB = (0, 1) if pi == 0 else (2, 3)
                            bO = 6 + pi
                            lq = qb[qsel][:, p8, cl * 128:(cl + 1) * 128]
                            sm = small[:, pi * 8:(pi + 1) * 8]
                            if j0 < 512:
                                op("pe", lambda lq=lq, bA=bA, bi=bi, n_=n_, j0=j0: PE.matmul(psb[bA][:, j0:512], lhsT=lq, rhs=KTb[bi][:, n_, j0:512],
                                                                                            start=True, stop=True),
                                   reads=[B_qb[qsel], B_KTb[bi]], writes=[PB[bA]])
                            op("pe", lambda lq=lq, bB=bB, bi=bi, n_=n_: PE.matmul(psb[bB][:, 0:64], lhsT=lq, rhs=KTb[bi][:, n_, 512:576],
                                                                                 start=True, stop=True),
                               reads=[B_qb[qsel], B_KTb[bi]], writes=[PB[bB]])
                            if j0 < 512:
                                op("dve", lambda pi=pi, bA=bA, p8=p8, j0=j0: DVE.tensor_tensor(out=sbf[pi][:, j0:512], in0=psb[bA][:, j0:512],
                                                                                              in1=Bt[:, p8, j0:512], op=ALU.add),
                                   reads=[PB[bA], B_Bt], writes=[B_sbf[pi]])
                            op("dve", lambda pi=pi, bB=bB, p8=p8: DVE.tensor_tensor(out=sbf[pi][:, 512:576], in0=psb[bB][:, 0:64],
                                                                                   in1=Bt[:, p8, 512:576], op=ALU.add),
                               reads=[PB[bB], B_Bt], writes=[B_sbf[pi]])
                            op("dve", lambda pi=pi, sm=sm, j0=j0: DVE.reduce_max(out=sm[:, 0:1], in_=sbf[pi][:, j0:576], axis=AX.X),
                               reads=[B_sbf[pi]], writes=[B_sm[pi]])
                            op("dve", lambda sm=sm: DVE.tensor_scalar(out=sm[:, 1:2], in0=sm[:, 0:1], scalar1=-1.0, scalar2=None, op0=ALU.mult),
                               reads=[B_sm[pi]], writes=[B_sm[pi]])
                            op("act", lambda pi=pi, sm=sm, j0=j0: ACT.activation(out=sbf[pi][:, j0:576], in_=sbf[pi][:, j0:576], func=AF.Exp,
                                                                                bias=sm[:, 1:2], accum_out=sm[:, 2:3]),
                               reads=[B_sbf[pi], B_sm[pi]], writes=[B_sbf[pi], B_sm[pi]])
                            op("dve", lambda sm=sm: DVE.reciprocal(out=sm[:, 3:4], in_=sm[:, 2:3]), reads=[B_sm[pi]], writes=[B_sm[pi]])
                            op("dve", lambda pi=pi, sm=sm, j0=j0: DVE.tensor_scalar(out=pn[pi][:, j0:576], in0=sbf[pi][:, j0:576], scalar1=sm[:, 3:4],
                                                                                   scalar2=None, op0=ALU.mult),
                               reads=[B_sbf[pi], B_sm[pi]], writes=[B_pn[pi]])
                            def _stB(c=c, p8=p8, pi=pi, bi=bi, n_=n_, bO=bO, jb0=jb0):
                                pvA, pvB = psbf(4), psbf(5)
                                for jb in range(jb0, 9):
                                    dstp = pvA[0:64, jb * 128:(jb + 1) * 128] if jb < 8 else pvB[0:64, 0:128]
                                    bk = 4 if jb < 8 else 5
                                    op("pe", lambda pi=pi, jb=jb, dstp=dstp: PE.transpose(out=dstp, in_=pn[pi][:, jb * 64:(jb + 1) * 64], identity=ident[:]),
                                       reads=[B_pn[pi], B_const], writes=[PB[bk]], mark=(jb >= 7))
                                if jb0 < 8:
                                    op("act", lambda pi=pi, jb0=jb0: ACT.activation(out=pTs[pi][:, jb0:8, :],
                                                                                   in_=pvA[0:64, jb0 * 128:1024].rearrange("p (c t) -> p c t", t=128), func=AF.Copy),
                                       reads=[PB[4]], writes=[B_pTs[pi]])
                                op("act", lambda pi=pi: ACT.activation(out=pTs[pi][:, 8, :], in_=pvB[0:64, 0:128], func=AF.Copy),
                                   reads=[PB[5]], writes=[B_pTs[pi]])
                                for jb in range(jb0, 9):
                                    op("pe", lambda pi=pi, jb=jb, bi=bi, n_=n_, bO=bO: PE.matmul(psb[bO][:, 0:128], lhsT=Vb[bi][:, jb, n_ * 128:(n_ + 1) * 128],
                                                                                               rhs=pTs[pi][:, jb, :], start=(jb == jb0), stop=(jb == 8)),
                                       reads=[B_Vb[bi], B_pTs[pi]], writes=[PB[bO]], mark=(jb == 8))
                                op("act", lambda p8=p8, c=c, bO=bO: ACT.activation(out=actT[:, 2 * p8:2 * p8 + 2, c * 64:(c + 1) * 64],
                                                                                  in_=psb[bO][:, 0:128].rearrange("p (h t) -> p h t", h=2), func=AF.Copy),
                                   reads=[PB[bO]], writes=[B_actc[c]])
                            if pendB[0] is not None:
                                pendB[0]()
                            pendB[0] = _stB
                if pendB[0] is not None:
                    pendB[0]()

            Xrows = lambda tt: Xs[tt * 128:(tt + 1) * 128, :]
            BX = lambda tt: [B_Xs[tt]]
            resnorm(lambda tt: xin[xrow(tt):xrow(tt) + 128, :], lambda tt: [], None, None, Xrows, BX, True)
            if dbg is not None and dbg[0] == "h0":
                return "stop"
            for l in range(n_layers):
                if l < 2:
                    phaseA(l)
                    if dbg is not None and dbg[0] == "projA" and l == DL:
                        return "stop"
                    phaseB(l)
                    if dbg is not None and dbg[0] == "scan" and l == DL:
                        return "stop"
                    phaseC(gla_w_o[l], HNx[:, l, :, :].rearrange("p h v -> p (h v)"))
                else:
                    j = l - 2
                    if l == 2:
                        phaseKV()
                    if dbg is not None and dbg[0] == "kv":
                        return "stop"
                    phaseQ(l, j)
                    if dbg is not None and dbg[0] == "q":
                        return "stop"
                    phaseATT(l, j)
                    if dbg is not None and dbg[0] == "att":
                        return "stop"
                    phaseC(att_w_o[j], None)
                resnorm(Xrows, BX, lambda tt: Ms[tt * 128:(tt + 1) * 128, :], norm_gains[l, 1], Xrows, BX, True)
                if dbg is not None and dbg[0] == "mix" and l == DL:
                    return "stop"
                phaseE(l)
                phaseF(l)
                fin = (l == n_layers - 1)
                if fin:
                    resnorm(Xrows, BX, lambda tt: Ms[tt * 128:(tt + 1) * 128, :], norm_gains[l, 3],
                            lambda tt: yout[xrow(tt):xrow(tt) + 128, :], lambda tt: [], False, is_out=True)
                else:
                    resnorm(Xrows, BX, lambda tt: Ms[tt * 128:(tt + 1) * 128, :], norm_gains[l, 3], Xrows, BX, True)
            return None

        for st in range(n_st):
            r = run_supertile(st)
            if r == "stop":
                break
        if dbg is not None:
            kb.fence()
            Bt = Buf()
            nt_dbg = dbg[1][0] // 128
            if dbg[0] == "projA":
                tb = cvb(WOFF, 6144)
                for tt in range(nt_dbg):
                    dma(tb, PROJ[tt * 128:(tt + 1) * 128, :], reads=[B_PROJ[tt]], writes=[Bt])
                    dma(dbg_out[tt * 128:(tt + 1) * 128, :], tb, reads=[Bt], is_out=True)
            elif dbg[0] in ("h0", "scan", "att"):
                dma(dbg_out.rearrange("k p t -> p k t"), actT[:, :, 0:dbg[1][2]], reads=B_actc, is_out=True)
            elif dbg[0] == "mix":
                tb = cvf(WOFF, 2048)
                for tt in range(nt_dbg):
                    dma(tb, Xs[tt * 128:(tt + 1) * 128, :], reads=[B_Xs[tt]], writes=[Bt])
                    dma(dbg_out[tt * 128:(tt + 1) * 128, :], tb, reads=[Bt], is_out=True)
        kb.finish()
    return nc


WEIGHT_KEYS = ["norm_gains", "gla_w_in", "gla_w_gate", "gla_b_gate", "gla_head_norm", "gla_w_o", "kv_norm",
               "att_w_kv", "att_w_q", "att_rel_bias", "att_w_o", "ffn_w_up", "ffn_conv_w", "ffn_conv_b", "ffn_w_down"]


def make_in_maps(inp, npt):
    f = lambda a: np.ascontiguousarray(np.asarray(a, dtype=np.float32))
    w = {k: f(inp[k]) for k in WEIGHT_KEYS}
    xp, xs = f(inp["x_prompt"]), f(inp["x_sample"])
    sg, sc = f(inp["state_gla"]), f(inp["state_ffn_conv"])
    ck, cv = f(inp["cache_k"]), f(inp["cache_v"])
    maps = []
    for c in range(N_CORES):
        xin = np.zeros((npt + 128, D), np.float32)
        if c < xp.shape[0]:
            xin[:npt] = xp[c, :npt]
        xin[npt:] = xs[2 * c:2 * c + 2].reshape(128, D)
        m = dict(w)
        m["xin"] = xin
        m["gla_s0"] = np.ascontiguousarray(sg[:, 2 * c:2 * c + 2])
        m["conv_s0"] = np.ascontiguousarray(sc[:, 2 * c:2 * c + 2])
        m["ck"] = np.ascontiguousarray(ck[2 * c:2 * c + 2].reshape(2, 512, 512))
        m["cv"] = np.ascontiguousarray(cv[2 * c:2 * c + 2].reshape(2, 512, 512))
        maps.append(m)
    return maps


_NC_CACHE = {}


def kernel(**inputs):
    if "full" not in _NC_CACHE:
        _NC_CACHE["full"] = build_program()
    nc = _NC_CACHE["full"]
    maps = make_in_maps(inputs, SEQ)
    res = run_bass_kernel_spmd(nc, maps, core_ids=list(range(N_CORES))).results
    B = 4
    y_p = np.stack([res[c]["yout"][:SEQ] for c in range(B)])
    y_s = np.concatenate([res[c]["yout"][SEQ:].reshape(2, 64, D) for c in range(N_CORES)])
    gla_p = np.stack([res[c]["o_gla_p"] for c in range(B)], axis=1)
    gla_s = np.concatenate([res[c]["o_gla_s"] for c in range(N_CORES)], axis=1)
    conv_p = np.stack([res[c]["o_conv_p"] for c in range(B)], axis=1)
    conv_s = np.concatenate([res[c]["o_conv_s"] for c in range(N_CORES)], axis=1)
    k_p = np.stack([res[c]["o_k_p"].reshape(512, 4, 128) for c in range(B)])
    v_p = np.stack([res[c]["o_v_p"].reshape(512, 4, 128) for c in range(B)])
    k_s = np.concatenate([res[c]["o_k_s"].reshape(2, 64, 4, 128) for c in range(N_CORES)])
    v_s = np.concatenate([res[c]["o_v_s"].reshape(2, 64, 4, 128) for c in range(N_CORES)])
    outs = (y_p, y_s, gla_p, gla_s, conv_p, conv_s, k_p, v_p, k_s, v_s)
    return tuple(np.ascontiguousarray(o, dtype=np.float32) for o in outs)
```

```python
import numpy as np
from contextlib import ExitStack
import concourse.bass as bass
import concourse.mybir as mybir
from concourse.bass_utils import run_bass_kernel_spmd

F32 = mybir.dt.float32
BF16 = mybir.dt.bfloat16
AF = mybir.ActivationFunctionType
ALU = mybir.AluOpType
AX = mybir.AxisListType

D = 2048
KC = 16
DFF = 5632
FC = 44
GLA_IN = 6160
EPS = 1e-6
N_CORES = 8
SEQ = 4096
NPH = 2048
EPOCH = 16000


class Buf:
    __slots__ = ("w", "r", "name")

    def __init__(self, name=""):
        self.w = None
        self.r = {}
        self.name = name


class KB:
    def __init__(self, nc, es):
        self.nc = nc
        self.es = es
        self.eng = dict(pe=nc.tensor, act=nc.scalar, dve=nc.vector, pool=nc.gpsimd, sp=nc.sync)
        self.sems = {}
        self.cnt = {}
        self.epoch = {}
        for e in ("pe", "act", "dve", "pool"):
            self.epoch[e] = 0
            self._newsem(self._ekey(e))
        self.ndma = 24
        self.dkeys = []
        for j in range(self.ndma):
            k = "d%d_0" % j
            self._newsem(k)
            self.dkeys.append(k)
        self.dma_rr = 0
        self.waited = {e: {} for e in self.eng}
        self.out_events = []
        self.fence_evs = []

    def _ekey(self, e):
        return "%s_%d" % (e, self.epoch[e])

    def _newsem(self, key):
        self.sems[key] = self.es.enter_context(self.nc.semaphore("s_" + key))
        self.cnt[key] = 0

    def fence(self):
        self.fence_evs = [(k, c) for k, c in self.cnt.items() if c > 0]

    def _wait(self, e, evs):
        best = {}
        for (k, v) in list(evs) + self.fence_evs:
            if v > best.get(k, 0):
                best[k] = v
        for k, v in best.items():
            if e == "pe" and k.startswith("pe_"):
                continue
            if self.waited[e].get(k, 0) >= v:
                continue
            if e != "sp" and k == self._ekey(e) and v > self.cnt[k]:
                continue
            self.eng[e].wait_ge(self.sems[k], v)
            self.waited[e][k] = v

    @staticmethod
    def _deps(reads, writes):
        evs = []
        for b in reads:
            if b.w is not None:
                evs.append(b.w)
        for b in writes:
            if b.w is not None:
                evs.append(b.w)
            evs.extend(b.r.items())
        return evs

    @staticmethod
    def _upd(ev, reads, writes):
        for b in reads:
            if ev[1] > b.r.get(ev[0], 0):
                b.r[ev[0]] = ev[1]
        for b in writes:
            b.w = ev
            b.r = {}

    def op(self, e, fn, reads=(), writes=(), mark=True):
        self._wait(e, self._deps(reads, writes))
        ins = fn()
        key = self._ekey(e)
        ev = (key, self.cnt[key] + 1)
        if mark:
            ins.then_inc(self.sems[key], 1)
            self.cnt[key] += 1
            if self.cnt[key] >= EPOCH:
                self.epoch[e] += 1
                self._newsem(self._ekey(e))
        self._upd(ev, reads, writes)
        return ins

    def dma(self, out_ap, in_ap, reads=(), writes=(), q="sp", is_out=False, slow=False):
        j = self.dma_rr
        self.dma_rr = (j + 1) % self.ndma
        key = self.dkeys[j]
        if self.cnt[key] >= EPOCH:
            nk = "d%d_%d" % (j, int(key.split("_")[1]) + 1)
            self._wait(q, [(key, self.cnt[key])])
            self._newsem(nk)
            self.dkeys[j] = nk
            key = nk
        evs = self._deps(reads, writes)
        if self.cnt[key] > 0:
            evs.append((key, self.cnt[key]))
        self._wait(q, evs)
        if slow:
            self.eng[q].dma_start(out=out_ap, in_=in_ap, allow_slow_non_contiguous=True).then_inc(self.sems[key], 16)
        else:
            self.eng[q].dma_start(out=out_ap, in_=in_ap).then_inc(self.sems[key], 16)
        self.cnt[key] += 16
        ev = (key, self.cnt[key])
        self._upd(ev, reads, writes)
        if is_out:
            self.out_events.append(ev)
        return ev

    def finish(self):
        evs = list(self.out_events)
        for k in self.dkeys:
            if self.cnt[k] > 0:
                evs.append((k, self.cnt[k]))
        self._wait("sp", evs)


def build_program(n_layers=4, np_tok=NPH, n_st=2, dbg=None):
    nc = bass.Bass("TRN2", target_bir_lowering=False)
    NPT = np_tok * n_st
    XR = NPT + 128
    dt = nc.dram_tensor

    def din(name, shape, dtype=F32):
        return dt(name, list(shape), dtype, kind="ExternalInput").ap()

    def dout(name, shape, dtype=F32):
        return dt(name, list(shape), dtype, kind="ExternalOutput").ap()

    def dscr(name, shape, dtype):
        return dt(name, list(shape), dtype, kind="Internal").ap()

    xin = din("xin", [XR, D])
    gla_s0 = din("gla_s0", [2, 2, 4, 256, 512])
    conv_s0 = din("conv_s0", [4, 2, 2, DFF])
    ck = din("ck", [2, 512, 512])
    cv = din("cv", [2, 512, 512])
    norm_gains = din("norm_gains", [4, 4, D])
    gla_w_in = din("gla_w_in", [2, D, GLA_IN])
    gla_w_gate = din("gla_w_gate", [2, 16, 1024])
    gla_b_gate = din("gla_b_gate", [2, 1024])
    gla_head_norm = din("gla_head_norm", [2, 512])
    gla_w_o = din("gla_w_o", [2, D, D])
    kv_norm = din("kv_norm", [D])
    att_w_kv = din("att_w_kv", [D, 1024])
    att_w_q = din("att_w_q", [2, D, D])
    att_rel_bias = din("att_rel_bias", [2, 16, 513])
    att_w_o = din("att_w_o", [2, D, D])
    ffn_w_up = din("ffn_w_up", [4, D, 2 * DFF])
    ffn_conv_w = din("ffn_conv_w", [4, 3, DFF])
    ffn_conv_b = din("ffn_conv_b", [4, DFF])
    ffn_w_down = din("ffn_w_down", [4, DFF, D])
    yout = dout("yout", [XR, D])
    o_gla_p = dout("o_gla_p", [2, 4, 256, 512])
    o_gla_s = dout("o_gla_s", [2, 2, 4, 256, 512])
    o_conv_p = dout("o_conv_p", [4, 2, DFF])
    o_conv_s = dout("o_conv_s", [4, 2, 2, DFF])
    o_k_p = dout("o_k_p", [512, 512])
    o_v_p = dout("o_v_p", [512, 512])
    o_k_s = dout("o_k_s", [128, 512])
    o_v_s = dout("o_v_s", [128, 512])
    TMAX = np_tok + 128
    Xs = dscr("Xs", [TMAX, D], F32)
    Ms = dscr("Ms", [TMAX, D], F32)
    PROJ = dscr("PROJ", [TMAX, 6144], BF16)
    GLs = dscr("GLs", [16, TMAX], F32)
    ATs = dscr("ATs", [(TMAX + 255) // 256, 128, FC, 256], BF16)
    CARRY_S = dscr("CARRY_S", [2, 128, 8, 512], F32)
    NPC = NPT // 64
    VSH = dscr("VSH", [(8 + NPC + 18) * 64, 512], BF16)
    KTs = dscr("KTs", [4, 128, 512 + NPT], BF16)
    KTss = dscr("KTss", [2, 4, 128, 576], BF16)
    QTs = dscr("QTs", [8, 128, 2 * TMAX], BF16)
    EREL = dscr("EREL", [16, 640], F32)
    ZREL = dscr("ZREL", [16, 41024], F32)
    dbg_out = None
    DL = 0
    if dbg is not None:
        if ':' in dbg[0]:
            DL = int(dbg[0].split(':')[1])
            dbg = (dbg[0].split(':')[0],) + tuple(dbg[1:])
        dbg_out = dout("dbg", dbg[1], dbg[2])

    es = ExitStack()
    with es:
        kb = KB(nc, es)
        op, dma = kb.op, kb.dma
        PE, ACT, DVE, POOL = nc.tensor, nc.scalar, nc.vector, nc.gpsimd

        def sb(name, shape, dtype=F32):
            return es.enter_context(nc.sbuf_tensor(name, list(shape), dtype))

        psb = [es.enter_context(nc.psum_tensor("ps%d" % i, [128, 512], F32)) for i in range(8)]
        PB = [Buf("ps%d" % i) for i in range(8)]

        def psbf(i):
            return psb[i][:].bitcast(BF16)

        identf = sb("identf", [128, 128], F32)
        ident = sb("ident", [128, 128], BF16)
        c_one = sb("c_one", [128, 1], F32)
        c_eps = sb("c_eps", [128, 1], F32)
        c_ln16 = sb("c_ln16", [128, 1], F32)
        triI = sb("triI", [64, 64], F32)
        triU = sb("triU", [64, 64], F32)
        maskT = sb("maskT", [64, 4, 64], F32)
        B_const = Buf("const")

        def cst(fn):
            op("pool", fn, writes=[B_const])

        cst(lambda: POOL.memset(identf[:], 0.0))
        cst(lambda: POOL.affine_select(out=identf[:], in_=identf[:], pattern=[[-1, 128]], compare_op=ALU.not_equal,
                                       fill=1.0, base=0, channel_multiplier=1))
        cst(lambda: POOL.tensor_copy(out=ident[:], in_=identf[:]))
        cst(lambda: POOL.memset(c_one[:], 1.0))
        cst(lambda: POOL.memset(c_eps[:], EPS))
        cst(lambda: POOL.memset(c_ln16[:], float(np.log(1.0 / 16.0))))
        cst(lambda: POOL.memset(triI[:], -1.0 / 16.0))
        cst(lambda: POOL.affine_select(out=triI[:], in_=triI[:], pattern=[[1, 64]], compare_op=ALU.is_ge, fill=0.0,
                                       base=0, channel_multiplier=-1))
        cst(lambda: POOL.memset(triU[:], -1.0 / 16.0))
        cst(lambda: POOL.affine_select(out=triU[:], in_=triU[:], pattern=[[-1, 64]], compare_op=ALU.is_gt, fill=0.0,
                                       base=0, channel_multiplier=1))
        cst(lambda: POOL.memset(maskT[:], 1.0))
        cst(lambda: POOL.affine_select(out=maskT[:], in_=maskT[:], pattern=[[0, 4], [1, 64]], compare_op=ALU.is_ge,
                                       fill=0.0, base=0, channel_multiplier=-1))

        NG = sb("NG", [128, 256], F32)
        KVN = sb("KVN", [128, 16], F32)
        HNx = sb("HNx", [128, 2, 4, 4], F32)
        CG = sb("CG", [128, 4, FC, 2], F32)
        prm = sb("prm", [128, 2, 128], F32)
        prm2 = sb("prm2", [24, 128], F32)
        B_prm = Buf("prm")
        B_par = Buf("par")
        B_CG = Buf("cg")
        dma(prm[:], norm_gains.rearrange("l i (k p) -> (l i k) p", p=128).rearrange("(a r) p -> r a p", r=128),
            writes=[B_prm])
        dma(prm2[0:16, :], kv_norm.rearrange("(k p) -> k p", p=128), writes=[B_prm])
        dma(prm2[16:24, :], gla_head_norm.rearrange("l (k p) -> (l k) p", p=128), writes=[B_prm])
        for a in range(2):
            op("pe", lambda a=a: PE.transpose(out=psb[0][:, a * 128:(a + 1) * 128], in_=prm[:, a, :], identity=identf[:]),
               reads=[B_prm, B_const], writes=[PB[0]])
        op("pe", lambda: PE.transpose(out=psb[0][:, 256:280], in_=prm2[0:24, :], identity=identf[0:24, 0:24]),
           reads=[B_prm, B_const], writes=[PB[0]])
        op("dve", lambda: DVE.tensor_copy(out=NG[:], in_=psb[0][:, 0:256]), reads=[PB[0]], writes=[B_par])
        op("dve", lambda: DVE.tensor_copy(out=KVN[:], in_=psb[0][:, 256:272]), reads=[PB[0]], writes=[B_par])
        for l_ in range(2):
            op("dve", lambda l_=l_: DVE.tensor_copy(
                out=HNx[:, l_, :, :],
                in_=psb[0][:, 272 + l_ * 4:276 + l_ * 4].unsqueeze(1).broadcast_to([128, 4, 4])),
                reads=[PB[0]], writes=[B_par])
        op("pool", lambda: POOL.memset(CG[:], 0.0), writes=[B_CG])

        ACT_F = max(KC * TMAX // 2, 17408)
        WORK_F = 20224
        STG_F = 3 * 4096
        big = sb("big", [128, ACT_F + WORK_F + STG_F], F32)
        WOFF = ACT_F
        SOFF = ACT_F + WORK_F

        def cvf(off, n):
            return big[:, off:off + n]

        def cvb(off, n):
            return big[:, off:off + n // 2].bitcast(BF16)

        actT = cvb(0, KC * TMAX).rearrange("p (k t) -> p k t", k=KC)
        B_actc = [Buf("act%d" % c) for c in range(TMAX // 64)]
        stg = [cvf(SOFF + i * 4096, 4096).rearrange("p (k n) -> p k n", k=8) for i in range(3)]
        B_stg = [Buf("stg%d" % i) for i in range(3)]
        stg_rr = [0]
        cast_rr = [0]

        NTM = TMAX // 128
        B_Xs = [Buf() for _ in range(NTM)]
        B_Ms = [Buf() for _ in range(NTM)]
        B_PROJ = [Buf() for _ in range(NTM)]
        B_GLs = Buf()
        B_ATs = Buf()
        B_CS = [Buf(), Buf()]
        B_VSH = Buf()
        B_KTs = Buf()
        B_QTs = Buf()
        B_EREL = Buf()
        B_ZREL = Buf()

        def actbufs(tt):
            return [B_actc[2 * tt], B_actc[2 * tt + 1]]

        def load_wblock(wb_ap, B_wb, Wrows, kcs, c0, ncols, gain=None):
            for k0 in range(0, kcs, 8):
                nk = min(8, kcs - k0)
                i = stg_rr[0]
                stg_rr[0] = (i + 1) % 3
                src = Wrows[k0 * 128:(k0 + nk) * 128, c0:c0 + ncols].rearrange("(k p) n -> p k n", p=128)
                dma(stg[i][:, 0:nk, 0:ncols], src, writes=[B_stg[i]])
                ce = "act" if cast_rr[0] % 2 == 0 else "dve"
                cast_rr[0] += 1
                if gain is None:
                    if ce == "act":
                        op("act", lambda i=i, k0=k0, nk=nk: ACT.activation(out=wb_ap[:, k0:k0 + nk, 0:ncols],
                                                                          in_=stg[i][:, 0:nk, 0:ncols], func=AF.Copy),
                           reads=[B_stg[i]], writes=[B_wb])
                    else:
                        op("dve", lambda i=i, k0=k0, nk=nk: DVE.tensor_copy(out=wb_ap[:, k0:k0 + nk, 0:ncols],
                                                                           in_=stg[i][:, 0:nk, 0:ncols]),
                           reads=[B_stg[i]], writes=[B_wb])
                else:
                    for j in range(nk):
                        if ce == "act":
                            op("act", lambda i=i, j=j, k0=k0: ACT.activation(
                                out=wb_ap[:, k0 + j, 0:ncols], in_=stg[i][:, j, 0:ncols], func=AF.Copy,
                                scale=gain[:, k0 + j:k0 + j + 1]),
                                reads=[B_stg[i], B_par], writes=[B_wb], mark=(j == nk - 1))
                        else:
                            op("dve", lambda i=i, j=j, k0=k0: DVE.tensor_scalar(
                                out=wb_ap[:, k0 + j, 0:ncols], in0=stg[i][:, j, 0:ncols],
                                scalar1=gain[:, k0 + j:k0 + j + 1], scalar2=None, op0=ALU.mult),
                                reads=[B_stg[i], B_par], writes=[B_wb], mark=(j == nk - 1))

        def run_supertile(st):
            NS = 2 if st == 0 else 0
            T = np_tok + 64 * NS
            NT = T // 128
            NPCH = np_tok // 64
            first = (st == 0)
            last = (st == n_st - 1)
            tblocks = [(t0, min(512, np_tok - t0)) for t0 in range(0, np_tok, 512)]
            if NS:
                tblocks.append((np_tok, 128))
            segs = [(0, NPCH, "p", 0)] + [(NPCH + i, 1, "s", i) for i in range(NS)]

            def xrow(tt):
                if tt < np_tok // 128:
                    return st * np_tok + tt * 128
                return NPT

            def resnorm(x_src, Bx_src, m_src, gain_row, x_dst, Bx_dst, want_hT, is_out=False):
                kb.fence()
                W = WOFF
                xt = [cvf(W + 0, 2048), cvf(W + 2048, 2048)]
                mt = [cvf(W + 4096, 2048), cvf(W + 6144, 2048)]
                gB = cvf(W + 8192, 2048)
                xb = cvb(W + 10240, 2048)
                junk = cvb(W + 11264, 2048)
                small = cvf(W + 12288, 32)
                B_xt = [Buf(), Buf()]
                B_mt = [Buf(), Buf()]
                B_gb, B_xb, B_jk, B_sm = Buf(), Buf(), Buf(), [Buf(), Buf()]
                if m_src is not None:
                    dma(gB, gain_row.partition_broadcast(128), writes=[B_gb])
                for tt in range(NT):
                    p = tt % 2
                    sm = small[:, p * 8:(p + 1) * 8]
                    dma(xt[p], x_src(tt), reads=Bx_src(tt), writes=[B_xt[p]])
                    if m_src is not None:
                        dma(mt[p], m_src(tt), reads=[B_Ms[tt]], writes=[B_mt[p]])
                        op("act", lambda p=p, sm=sm: ACT.activation(out=junk, in_=mt[p], func=AF.Square,
                                                                     accum_out=sm[:, 0:1]),
                           reads=[B_mt[p]], writes=[B_jk, B_sm[p]])
                        op("act", lambda sm=sm: ACT.activation(out=sm[:, 1:2], in_=sm[:, 0:1], func=AF.Sqrt,
                                                               scale=1.0 / D, bias=c_eps[:]),
                           reads=[B_sm[p], B_const], writes=[B_sm[p]])
                        op("dve", lambda sm=sm: DVE.reciprocal(out=sm[:, 2:3], in_=sm[:, 1:2]),
                           reads=[B_sm[p]], writes=[B_sm[p]])
                        op("dve", lambda p=p, sm=sm: DVE.scalar_tensor_tensor(
                            out=mt[p], in0=mt[p], scalar=sm[:, 2:3], in1=gB, op0=ALU.mult, op1=ALU.mult),
                            reads=[B_sm[p], B_gb, B_mt[p]], writes=[B_mt[p]])
                        op("dve", lambda p=p: DVE.tensor_tensor(out=xt[p], in0=xt[p], in1=mt[p], op=ALU.add),
                           reads=[B_mt[p], B_xt[p]], writes=[B_xt[p]])
                    if x_dst is not None:
                        dma(x_dst(tt), xt[p], reads=[B_xt[p]], writes=Bx_dst(tt), is_out=is_out)
                    if want_hT:
                        op("act", lambda p=p, sm=sm: ACT.activation(out=junk, in_=xt[p], func=AF.Square,
                                                                     accum_out=sm[:, 3:4]),
                           reads=[B_xt[p]], writes=[B_jk, B_sm[p]])
                        op("act", lambda sm=sm: ACT.activation(out=sm[:, 4:5], in_=sm[:, 3:4], func=AF.Sqrt,
                                                               scale=1.0 / D, bias=c_eps[:]),
                           reads=[B_sm[p], B_const], writes=[B_sm[p]])
                        op("dve", lambda sm=sm: DVE.reciprocal(out=sm[:, 5:6], in_=sm[:, 4:5]),
                           reads=[B_sm[p]], writes=[B_sm[p]])
                        op("act", lambda p=p, sm=sm: ACT.activation(out=xb, in_=xt[p], func=AF.Copy, scale=sm[:, 5:6]),
                           reads=[B_sm[p], B_xt[p]], writes=[B_xb])
                        for g in range(2):
                            bank = 6 + g
                            pv = psbf(bank)
                            for j in range(8):
                                kc = g * 8 + j
                                op("pe", lambda kc=kc, j=j, pv=pv: PE.transpose(
                                    out=pv[:, j * 128:(j + 1) * 128], in_=xb[:, kc * 128:(kc + 1) * 128],
                                    identity=ident[:]),
                                    reads=[B_xb, B_const], writes=[PB[bank]], mark=(j == 7))
                            dst = actT[:, g * 8:(g + 1) * 8, tt * 128:(tt + 1) * 128]
                            srcv = pv.rearrange("p (k t) -> p k t", k=8)
                            if g == 0:
                                op("dve", lambda dst=dst, srcv=srcv: DVE.tensor_copy(out=dst, in_=srcv),
                                   reads=[PB[bank]], writes=actbufs(tt))
                            else:
                                op("act", lambda dst=dst, srcv=srcv: ACT.activation(out=dst, in_=srcv, func=AF.Copy),
                                   reads=[PB[bank]], writes=actbufs(tt))

            def gemm_setup():
                kb.fence()
                wbs = [cvb(WOFF + i * 4096, 8192).rearrange("p (k n) -> p k n", k=16) for i in range(2)]
                otf = [cvf(WOFF + 8192 + i * 512, 512) for i in range(4)]
                return wbs, [Buf(), Buf()], otf, [Buf() for _ in range(4)]

            def gemm_tok(Wrows, col_blocks, gain, evac, extra=None):
                wbs, B_wb, otf, B_ot = gemm_setup()
                rr = [0, 0]

                def next_ot():
                    i = rr[1]
                    rr[1] = (i + 1) % 4
                    return i, otf[i], B_ot[i]

                load_wblock(wbs[0], B_wb[0], Wrows, KC, col_blocks[0][0], col_blocks[0][1], gain)
                for bi, (c0, ncols) in enumerate(col_blocks):
                    cur = bi % 2
                    if bi + 1 < len(col_blocks):
                        load_wblock(wbs[1 - cur], B_wb[1 - cur], Wrows, KC, col_blocks[bi + 1][0],
                                    col_blocks[bi + 1][1], gain)
                    if extra is not None and extra(bi, c0, ncols, wbs[cur], B_wb[cur], next_ot):
                        continue
                    for tt in range(NT):
                        bank = rr[0] % 4
                        rr[0] += 1
                        for kc in range(KC):
                            op("pe", lambda kc=kc, tt=tt, bank=bank, cur=cur, ncols=ncols: PE.matmul(
                                psb[bank][:, 0:ncols], lhsT=actT[:, kc, tt * 128:(tt + 1) * 128],
                                rhs=wbs[cur][:, kc, 0:ncols], start=(kc == 0), stop=(kc == KC - 1)),
                                reads=actbufs(tt) + [B_wb[cur]], writes=[PB[bank]], mark=(kc == KC - 1))
                        evac(bi, c0, ncols, tt, bank, next_ot)

            def mm_feat(bank, wb, B_wb, f0, m, t0, n):
                c0_ = t0 // 64
                rb_ = [B_actc[c] for c in range(c0_, (t0 + n) // 64)]
                for kc in range(KC):
                    op("pe", lambda kc=kc: PE.matmul(psb[bank][0:m, 0:n], lhsT=wb[:, kc, f0:f0 + m],
                                                     rhs=actT[:, kc, t0:t0 + n], start=(kc == 0), stop=(kc == KC - 1)),
                       reads=rb_ + [B_wb], writes=[PB[bank]], mark=(kc == KC - 1))

            def evac_to_Ms(bi, c0, ncols, tt, bank, next_ot):
                i, o, Bo = next_ot()
                op("act", lambda o=o, bank=bank: ACT.activation(out=o, in_=psb[bank][:, 0:512], func=AF.Copy),
                   reads=[PB[bank]], writes=[Bo])
                dma(Ms[tt * 128:(tt + 1) * 128, c0:c0 + 512], o, reads=[Bo], writes=[B_Ms[tt]], q="act")

            def phaseA(l):
                gain = NG[:, (l * 4 + 0) * 16:(l * 4 + 0) * 16 + 16]
                import os
                blocks = [(c, 512) for c in range(0, 6144, 512)] + [(6144, 16)]
                if os.environ.get("DBG_NOG"): blocks = blocks[:int(os.environ["DBG_NOG"])]

                def evacA(bi, c0, ncols, tt, bank, next_ot):
                    i, o_, Bo = next_ot()
                    o = o_.bitcast(BF16)[:, 0:512]
                    fn = AF.Silu if c0 >= 4096 else AF.Copy
                    op("act", lambda o=o, bank=bank, fn=fn: ACT.activation(out=o, in_=psb[bank][:, 0:512], func=fn),
                       reads=[PB[bank]], writes=[Bo])
                    dma(PROJ[tt * 128:(tt + 1) * 128, c0:c0 + 512], o, reads=[Bo], writes=[B_PROJ[tt]], q="act")

                def extraA(bi, c0, ncols, wb, B_wb, next_ot):
                    if c0 != 6144:
                        return False
                    for (t0, n) in tblocks:
                        bank = 4
                        mm_feat(bank, wb, B_wb, 0, 16, t0, n)
                        i, o, Bo = next_ot()
                        op("act", lambda o=o, n=n: ACT.activation(out=o[0:16, 0:n], in_=psb[4][0:16, 0:n], func=AF.Copy),
                           reads=[PB[4]], writes=[Bo])
                        dma(GLs[:, t0:t0 + n], o[0:16, 0:n], reads=[Bo], writes=[B_GLs])
                    return True

                gemm_tok(gla_w_in[l], blocks, gain, evacA, extraA)

            def phaseB(l):
                kb.fence()
                W = WOFF
                Pb = [cvb(SOFF + i * 3072, 6144)[0:64, :] for i in range(3)]
                B_P = [Buf() for _ in range(3)]
                S = cvf(W + 0, 4096).rearrange("p (k v) -> p k v", k=8)
                Sb = cvb(W + 4096, 4096).rearrange("p (k v) -> p k v", k=8)
                spD = [cvf(W + 6144 + d * 4232, 1024)[0:64, :] for d in range(2)]
                XDD = [cvf(W + 7168 + d * 4232, 1024)[0:64, :] for d in range(2)]
                EQD = [cvf(W + 8192 + d * 4232, 512).rearrange("p (k t) -> p k t", k=8) for d in range(2)]
                EKD = [cvf(W + 8704 + d * 4232, 512).rearrange("p (k t) -> p k t", k=8) for d in range(2)]
                qtD = [cvb(W + 9216 + d * 4232, 512).rearrange("p (k t) -> p k t", k=8) for d in range(2)]
                ktD = [cvb(W + 9472 + d * 4232, 512).rearrange("p (k t) -> p k t", k=8) for d in range(2)]
                kpD = [cvb(W + 9728 + d * 4232, 1024)[0:64, :] for d in range(2)]
                amD = [cvb(W + 10240 + d * 4232, 256)[0:64, :].rearrange("p (h t) -> p h t", h=4) for d in range(2)]
                EendD = [cvf(W + 10368 + d * 4232, 8) for d in range(2)]
                osb = [cvf(W + 14608 + i * 512, 512)[0:64, :] for i in range(2)]
                on = cvb(W + 15632, 2048)[0:64, :]
                small = cvf(W + 16656, 32)
                junk = cvb(W + 16688, 512)[0:64, :]
                WG = cvf(W + 16944, 1024)[0:17, :]
                GLc = [cvf(W + 17968 + i * 64, 64)[0:17, :] for i in range(2)]
                B_S, B_Sb = Buf(), Buf()
                B_spD, B_XDD, B_EQD, B_EKD, B_qtD, B_ktD, B_kpD, B_amD, B_EendD = [[Buf(), Buf()] for _ in range(9)]
                B_osb = [Buf(), Buf()]
                B_on, B_small, B_junk, B_WG = Buf(), Buf(), Buf(), Buf()
                B_GLc = [Buf(), Buf()]
                dma(WG[0:16, :], gla_w_gate[l], writes=[B_WG])
                dma(WG[16:17, :], gla_b_gate[l].unsqueeze(0), writes=[B_WG])
                for i in range(2):
                    op("pool", lambda i=i: POOL.memset(GLc[i], 1.0), writes=[B_GLc[i]])
                def sel(d):
                    return (spD[d], XDD[d], EQD[d], EKD[d], qtD[d], ktD[d], kpD[d], amD[d], EendD[d],
                            B_spD[d], B_XDD[d], B_EQD[d], B_EKD[d], B_qtD[d], B_ktD[d], B_kpD[d], B_amD[d], B_EendD[d])

                def stage1(c, pi, gi, d):
                    sp, XD, EQ, EK, qt, kt, kp, am, Eend, B_sp, B_XD, B_EQ, B_EK, B_qt, B_kt, B_kp, B_am, B_Eend = sel(d)
                    P = Pb[pi]
                    dma(P, PROJ[c * 64:(c + 1) * 64, :], reads=[B_PROJ[c // 2]], writes=[B_P[pi]])
                    dma(GLc[gi][0:16, :], GLs[:, c * 64:(c + 1) * 64], reads=[B_GLs], writes=[B_GLc[gi]])
                    for hf in range(2):
                        op("pe", lambda hf=hf, gi=gi: PE.matmul(psb[hf][0:64, :], lhsT=GLc[gi], rhs=WG[:, hf * 512:(hf + 1) * 512],
                                                                start=True, stop=True),
                           reads=[B_GLc[gi], B_WG], writes=[PB[hf]])
                        op("act", lambda hf=hf: ACT.activation(out=sp[:, hf * 512:(hf + 1) * 512], in_=psb[hf][0:64, :],
                                                               func=AF.Exp, scale=-1.0),
                           reads=[PB[hf]], writes=[B_sp])
                    op("act", lambda: ACT.activation(out=sp, in_=sp, func=AF.Ln, bias=c_one[0:64, :]),
                       reads=[B_sp, B_const], writes=[B_sp])
                    bps = psb[2][:].rearrange("p (k t) -> p k t", k=8)
                    for kc in range(8):
                        op("pe", lambda kc=kc: PE.matmul(bps[:, kc, :], lhsT=sp[:, kc * 128:(kc + 1) * 128], rhs=triI[:],
                                                         start=True, stop=True),
                           reads=[B_sp, B_const], writes=[PB[2]], mark=(kc == 7))
                    for hf in range(2):
                        op("pe", lambda hf=hf: PE.matmul(psb[hf][0:64, :], lhsT=triU[:], rhs=sp[:, hf * 512:(hf + 1) * 512],
                                                         start=True, stop=True),
                           reads=[B_sp, B_const], writes=[PB[hf]])
                        op("act", lambda hf=hf: ACT.activation(out=XD[:, hf * 512:(hf + 1) * 512], in_=psb[hf][0:64, :],
                                                               func=AF.Exp),
                           reads=[PB[hf]], writes=[B_XD])
                    op("act", lambda: ACT.activation(out=EQ, in_=bps, func=AF.Exp, bias=c_ln16[:]),
                       reads=[PB[2], B_const], writes=[B_EQ])
                    op("act", lambda: ACT.activation(out=EK, in_=bps, func=AF.Exp, scale=-1.0),
                       reads=[PB[2]], writes=[B_EK])
                    op("act", lambda: ACT.activation(out=Eend, in_=bps[:, :, 63], func=AF.Exp),
                       reads=[PB[2]], writes=[B_Eend])
                    qv = psbf(3)[:, 0:512].rearrange("p (k t) -> p k t", k=8)
                    kv_ = psbf(3)[:, 512:1024].rearrange("p (k t) -> p k t", k=8)
                    for kc in range(8):
                        op("pe", lambda kc=kc: PE.transpose(out=qv[:, kc, :], in_=P[:, kc * 128:(kc + 1) * 128],
                                                            identity=ident[0:64, 0:64]),
                           reads=[B_P[pi], B_const], writes=[PB[3]], mark=False)
                    for kc in range(8):
                        op("pe", lambda kc=kc: PE.transpose(out=kv_[:, kc, :], in_=P[:, 1024 + kc * 128:1024 + (kc + 1) * 128],
                                                            identity=ident[0:64, 0:64]),
                           reads=[B_P[pi], B_const], writes=[PB[3]], mark=(kc == 7))
                    op("dve", lambda: DVE.tensor_tensor(out=qt, in0=qv, in1=EQ, op=ALU.mult),
                       reads=[PB[3], B_EQ], writes=[B_qt])
                    op("dve", lambda: DVE.tensor_tensor(out=kt, in0=kv_, in1=EK, op=ALU.mult),
                       reads=[PB[3], B_EK], writes=[B_kt])
                    op("dve", lambda: DVE.tensor_tensor(out=kp, in0=P[:, 1024:2048], in1=XD, op=ALU.mult),
                       reads=[B_P[pi], B_XD], writes=[B_kp])
                    aps = psb[4][0:64, 0:256].rearrange("p (h t) -> p h t", h=4)
                    for h in range(4):
                        for j in range(2):
                            op("pe", lambda h=h, j=j: PE.matmul(aps[:, h, :], lhsT=kt[:, 2 * h + j, :], rhs=qt[:, 2 * h + j, :],
                                                                start=(j == 0), stop=(j == 1)),
                               reads=[B_kt, B_qt], writes=[PB[4]], mark=(h == 3 and j == 1))
                    op("dve", lambda: DVE.tensor_tensor(out=am, in0=aps, in1=maskT[:], op=ALU.mult),
                       reads=[PB[4], B_const], writes=[B_am])

                def stage2(c, pi, d):
                    sp, XD, EQ, EK, qt, kt, kp, am, Eend, B_sp, B_XD, B_EQ, B_EK, B_qt, B_kt, B_kp, B_am, B_Eend = sel(d)
                    P = Pb[pi]
                    for h in range(4):
                        oi = h % 2
                        vh = P[:, 2048 + h * 512:2048 + (h + 1) * 512]
                        op("pe", lambda h=h, vh=vh: PE.matmul(psb[5][0:64, :], lhsT=am[:, h, :], rhs=vh, start=True, stop=False),
                           reads=[B_am, B_P[pi]], writes=[PB[5]], mark=False)
                        for j in range(2):
                            op("pe", lambda h=h, j=j: PE.matmul(psb[5][0:64, :], lhsT=qt[:, 2 * h + j, :], rhs=Sb[:, 2 * h + j, :],
                                                                start=False, stop=(j == 1)),
                               reads=[B_qt, B_Sb], writes=[PB[5]], mark=(j == 1))
                        op("act", lambda oi=oi: ACT.activation(out=osb[oi], in_=psb[5][0:64, :], func=AF.Copy),
                           reads=[PB[5]], writes=[B_osb[oi]])
                        op("act", lambda oi=oi, h=h: ACT.activation(out=junk, in_=osb[oi], func=AF.Square,
                                                                    accum_out=small[0:64, h:h + 1]),
                           reads=[B_osb[oi]], writes=[B_junk, B_small])
                        op("act", lambda h=h: ACT.activation(out=small[0:64, 4 + h:5 + h], in_=small[0:64, h:h + 1],
                                                             func=AF.Sqrt, scale=1.0 / 512, bias=c_eps[0:64, :]),
                           reads=[B_small, B_const], writes=[B_small])
                        op("dve", lambda h=h: DVE.reciprocal(out=small[0:64, 8 + h:9 + h], in_=small[0:64, 4 + h:5 + h]),
                           reads=[B_small], writes=[B_small])
                        op("dve", lambda h=h, oi=oi: DVE.scalar_tensor_tensor(
                            out=on[:, h * 512:(h + 1) * 512], in0=osb[oi], scalar=small[0:64, 8 + h:9 + h],
                            in1=P[:, 4096 + h * 512:4096 + (h + 1) * 512], op0=ALU.mult, op1=ALU.mult),
                            reads=[B_small, B_osb[oi], B_P[pi]], writes=[B_on])
                        for j in range(2):
                            kc = 2 * h + j
                            op("pe", lambda kc=kc, vh=vh: PE.matmul(psb[6 + kc % 2][:, :], lhsT=kp[:, kc * 128:(kc + 1) * 128], rhs=vh,
                                                                    start=True, stop=True),
                               reads=[B_kp, B_P[pi]], writes=[PB[6 + kc % 2]])
                            op("dve", lambda kc=kc: DVE.scalar_tensor_tensor(
                                out=S[:, kc, :], in0=S[:, kc, :], scalar=Eend[:, kc:kc + 1], in1=psb[6 + kc % 2][:, :],
                                op0=ALU.mult, op1=ALU.add),
                                reads=[B_Eend, PB[6 + kc % 2], B_S, B_Sb], writes=[B_S])
                            op("act", lambda kc=kc: ACT.activation(out=Sb[:, kc, :], in_=S[:, kc, :], func=AF.Copy),
                               reads=[B_S], writes=[B_Sb])
                    tv = psbf(7).rearrange("p (k t) -> p k t", k=16)
                    for kc in range(16):
                        op("pe", lambda kc=kc: PE.transpose(out=tv[:, kc, :], in_=on[:, kc * 128:(kc + 1) * 128],
                                                            identity=ident[0:64, 0:64]),
                           reads=[B_on, B_const], writes=[PB[7]], mark=(kc == 15))
                    op("dve", lambda c=c: DVE.tensor_copy(out=actT[:, :, c * 64:(c + 1) * 64], in_=tv),
                       reads=[PB[7]], writes=[B_actc[c]])

                items = []
                for (c0, ncs, kind, sidx) in segs:
                    for c in range(c0, c0 + ncs):
                        items.append((c, kind, sidx, c == c0, c == c0 + ncs - 1))

                def seg_init(kind, sidx):

                    if kind == "p":
                        if first:
                            op("pool", lambda: POOL.memset(S, 0.0), writes=[B_S])
                        else:
                            dma(S, CARRY_S[l], reads=[B_CS[l]], writes=[B_S])
                    else:
                        dma(S, gla_s0[l, sidx].rearrange("h (j p) v -> p (h j) v", p=128), writes=[B_S])
                    op("act", lambda: ACT.activation(out=Sb, in_=S, func=AF.Copy), reads=[B_S], writes=[B_Sb])

                def seg_final(kind, sidx):

                    sview = lambda ap: ap.rearrange("h (j p) v -> p (h j) v", p=128)
                    if kind == "p":
                        if last:
                            dma(sview(o_gla_p[l]), S, reads=[B_S], is_out=True)
                        else:
                            dma(CARRY_S[l], S, reads=[B_S], writes=[B_CS[l]])
                    else:
                        dma(sview(o_gla_s[l, sidx]), S, reads=[B_S], is_out=True)


                for i, (c, kind, sidx, sfirst, slast) in enumerate(items):
                    if i == 0:
                        stage1(c, i % 3, i % 2, i % 2)
                    if i + 1 < len(items):
                        stage1(items[i + 1][0], (i + 1) % 3, (i + 1) % 2, (i + 1) % 2)
                    if sfirst:
                        seg_init(kind, sidx)
                    stage2(c, i % 3, i % 2)
                    if slast:
                        seg_final(kind, sidx)

            def phaseC(Wrows, gain):
                gemm_tok(Wrows, [(c, 512) for c in range(0, D, 512)], gain, evac_to_Ms)

            def phaseE(l):
                kb.fence()
                W = WOFF
                prw = cvf(SOFF, DFF)[0:8, :]
                B_prw = Buf()
                cp = cvf(W + 19100, FC * 8).rearrange("p (f j) -> p f j", f=FC)
                GT = cvf(W + 19452, FC * 6).rearrange("p (f j) -> p f j", f=FC)
                B_cp, B_GT = Buf(), Buf()
                op("pool", lambda: POOL.memset(prw, 0.0), writes=[B_prw])
                dma(prw[0:3, :], ffn_conv_w[l], writes=[B_prw])
                dma(prw[3:4, :], ffn_conv_b[l].unsqueeze(0), writes=[B_prw])
                if NS:
                    dma(prw[4:8, :], conv_s0[l].rearrange("s j f -> (s j) f"), writes=[B_prw])
                for f in range(FC):
                    op("pe", lambda f=f: PE.transpose(out=psb[0][:, f * 8:(f + 1) * 8], in_=prw[:, f * 128:(f + 1) * 128],
                                                      identity=identf[0:8, 0:8]),
                       reads=[B_prw, B_const], writes=[PB[0]], mark=(f == FC - 1))
                op("dve", lambda: DVE.tensor_copy(out=cp, in_=psb[0][:, 0:FC * 8].rearrange("p (f j) -> p f j", f=FC)),
                   reads=[PB[0]], writes=[B_cp])
                kb.fence()
                wbs = [cvb(W + i * 4096, 8192).rearrange("p (g k n) -> p g k n", g=2, k=16) for i in range(2)]
                B_wb = [Buf(), Buf()]
                TG = T + 6
                GxD = [cvf(W + 8192 + i * 2184, TG) for i in range(2)]
                Vx = [cvb(W + 8192 + 4368 + i * 1088, 2176)[:, 0:T] for i in range(2)]
                Cx = cvf(W + 8192 + 4368 + 2176, 2176)[:, 0:T]
                At = [cvb(W + 8192 + 4368 + 2176 + 2176 + i * 1088, 2176)[:, 0:T] for i in range(2)]
                B_GxD, B_Cx = [Buf(), Buf()], Buf()
                B_Vx = [Buf(), Buf()]
                B_At = [Buf(), Buf()]
                goff = {}
                for (c0, ncs, kind, sidx) in segs:
                    goff[(kind, sidx)] = 0 if kind == "p" else np_tok + 2 + 66 * sidx
                gain = NG[:, (l * 4 + 2) * 16:(l * 4 + 2) * 16 + 16]
                Wup = ffn_w_up[l]
                NG2 = DFF // 256

                def load_grp(g, slot):
                    load_wblock(wbs[slot][:, 0, :, :], B_wb[slot], Wup, KC, g * 256, 256, gain)
                    load_wblock(wbs[slot][:, 1, :, :], B_wb[slot], Wup, KC, DFF + g * 256, 256, gain)

                load_grp(0, 0)
                fbi = 0
                for g in range(NG2):
                    cur = g % 2
                    if g + 1 < NG2:
                        load_grp(g + 1, 1 - cur)
                    for fl in range(2):
                        fb = g * 2 + fl
                        vi = fbi % 2
                        fbi += 1
                        Gx, B_Gx = GxD[vi], B_GxD[vi]
                        if first:
                            op("pool", lambda: POOL.memset(Gx[:, 0:2], 0.0), writes=[B_Gx])
                        else:
                            op("pool", lambda fb=fb: POOL.tensor_copy(out=Gx[:, 0:2], in_=CG[:, l, fb, :]),
                               reads=[B_CG], writes=[B_Gx])
                        for i in range(NS):
                            o_ = goff[("s", i)]
                            op("pool", lambda fb=fb, i=i, o_=o_: POOL.tensor_copy(out=Gx[:, o_:o_ + 2], in_=cp[:, fb, 4 + 2 * i:6 + 2 * i]),
                               reads=[B_cp], writes=[B_Gx])
                        for bi, (t0, n) in enumerate(tblocks):
                            bg = bi % 3
                            bv = 3 + bi % 3
                            mm_feat(bg, wbs[cur][:, 0, :, :], B_wb[cur], fl * 128, 128, t0, n)
                            mm_feat(bv, wbs[cur][:, 1, :, :], B_wb[cur], fl * 128, 128, t0, n)
                            if t0 < np_tok:
                                op("act", lambda bg=bg, t0=t0, n=n: ACT.activation(out=Gx[:, 2 + t0:2 + t0 + n], in_=psb[bg][:, 0:n], func=AF.Copy),
                                   reads=[PB[bg]], writes=[B_Gx])
                            else:
                                for i in range(NS):
                                    o_ = goff[("s", i)] + 2
                                    op("act", lambda bg=bg, i=i, o_=o_: ACT.activation(out=Gx[:, o_:o_ + 64], in_=psb[bg][:, i * 64:(i + 1) * 64], func=AF.Copy),
                                       reads=[PB[bg]], writes=[B_Gx])
                            op("dve", lambda bv=bv, t0=t0, n=n, vi=vi: DVE.tensor_copy(out=Vx[vi][:, t0:t0 + n], in_=psb[bv][:, 0:n]),
                               reads=[PB[bv]], writes=[B_Vx[vi]])
                        for (c0, ncs, kind, sidx) in segs:
                            o_ = goff[(kind, sidx)]
                            n = ncs * 64
                            t0 = c0 * 64
                            op("dve", lambda fb=fb, o_=o_, n=n, t0=t0: DVE.tensor_scalar(
                                out=Cx[:, t0:t0 + n], in0=Gx[:, o_ + 2:o_ + 2 + n], scalar1=cp[:, fb, 2:3], scalar2=cp[:, fb, 3:4],
                                op0=ALU.mult, op1=ALU.add), reads=[B_Gx, B_cp], writes=[B_Cx])
                            op("dve", lambda fb=fb, o_=o_, n=n, t0=t0: DVE.scalar_tensor_tensor(
                                out=Cx[:, t0:t0 + n], in0=Gx[:, o_ + 1:o_ + 1 + n], scalar=cp[:, fb, 1:2], in1=Cx[:, t0:t0 + n],
                                op0=ALU.mult, op1=ALU.add), reads=[B_Gx, B_cp, B_Cx], writes=[B_Cx])
                            op("dve", lambda fb=fb, o_=o_, n=n, t0=t0: DVE.scalar_tensor_tensor(
                                out=Cx[:, t0:t0 + n], in0=Gx[:, o_:o_ + n], scalar=cp[:, fb, 0:1], in1=Cx[:, t0:t0 + n],
                                op0=ALU.mult, op1=ALU.add), reads=[B_Gx, B_cp, B_Cx], writes=[B_Cx])
                            j0 = 0 if kind == "p" else 2 + 2 * sidx
                            op("pool", lambda fb=fb, o_=o_, n=n, j0=j0: POOL.tensor_copy(out=GT[:, fb, j0:j0 + 2], in_=Gx[:, o_ + n:o_ + n + 2]),
                               reads=[B_Gx], writes=[B_GT])
                            if kind == "p":
                                op("pool", lambda fb=fb, o_=o_, n=n: POOL.tensor_copy(out=CG[:, l, fb, :], in_=Gx[:, o_ + n:o_ + n + 2]),
                                   reads=[B_Gx], writes=[B_CG])
                        op("act", lambda: ACT.activation(out=Cx, in_=Cx, func=AF.Silu), reads=[B_Cx], writes=[B_Cx])
                        op("dve", lambda vi=vi: DVE.tensor_tensor(out=At[vi], in0=Cx, in1=Vx[vi], op=ALU.mult),
                           reads=[B_Cx, B_Vx[vi]], writes=[B_At[vi]])
                        nfb = T // 256
                        dma(ATs[0:nfb, :, fb, :].rearrange("b p t -> p b t"), At[vi][:, 0:nfb * 256].rearrange("p (b t) -> p b t", t=256),
                            reads=[B_At[vi]], writes=[B_ATs])
                        if T % 256:
                            dma(ATs[nfb][:, fb, 0:T % 256], At[vi][:, nfb * 256:T], reads=[B_At[vi]], writes=[B_ATs])
                kb.fence()
                orow = cvf(SOFF, 2048)[0:6, :]
                B_orow = Buf()
                for r0 in range(0, FC, 16):
                    nf = min(16, FC - r0)
                    for f in range(nf):
                        bank = f // 4
                        op("pe", lambda f=f, r0=r0, bank=bank: PE.transpose(
                            out=psb[bank][0:6, (f % 4) * 128:(f % 4 + 1) * 128], in_=GT[:, r0 + f, :], identity=identf[:]),
                            reads=[B_GT, B_const], writes=[PB[bank]], mark=(f % 4 == 3 or f == nf - 1))
                    for bank in range((nf + 3) // 4):
                        op("dve", lambda bank=bank: DVE.tensor_copy(out=orow[:, bank * 512:(bank + 1) * 512], in_=psb[bank][0:6, :]),
                           reads=[PB[bank]], writes=[B_orow])
                    cs = slice(r0 * 128, (r0 + nf) * 128)
                    if last:
                        dma(o_conv_p[l][:, cs], orow[0:2, 0:nf * 128], reads=[B_orow], is_out=True)
                    for i in range(NS):
                        dma(o_conv_s[l, i][:, cs], orow[2 + 2 * i:4 + 2 * i, 0:nf * 128], reads=[B_orow], is_out=True)

            def phaseF(l):
                kb.fence()
                wbs = [cvb(i * 11264, 22528).rearrange("p (k n) -> p k n", k=FC) for i in range(2)]
                aT = [cvb(22528 + i * 5632, 11264).rearrange("p (k t) -> p k t", k=FC) for i in range(2)]
                otf = [cvf(33792 + i * 512, 512) for i in range(4)]
                B_wb = [Buf(), Buf()]
                B_aT = [Buf(), Buf()]
                B_ot = [Buf() for _ in range(4)]
                Wd = ffn_w_down[l]
                load_wblock(wbs[0], B_wb[0], Wd, FC, 0, 512, None)
                rr = 0
                ai = 0
                tb2 = [(t, min(2, NT - t)) for t in range(0, NT, 2)]
                for nb in range(4):
                    cur = nb % 2
                    if nb + 1 < 4:
                        load_wblock(wbs[1 - cur], B_wb[1 - cur], Wd, FC, (nb + 1) * 512, 512, None)
                    for (tt0, ntl) in tb2:
                        a = ai % 2
                        ai += 1
                        dma(aT[a][:, :, 0:ntl * 128], ATs[tt0 // 2][:, :, 0:ntl * 128], reads=[B_ATs], writes=[B_aT[a]])
                        for j in range(ntl):
                            tt = tt0 + j
                            bank = rr % 4
                            oi = rr % 4
                            rr += 1
                            for kc in range(FC):
                                op("pe", lambda kc=kc, a=a, bank=bank, cur=cur, j=j: PE.matmul(
                                    psb[bank][:, :], lhsT=aT[a][:, kc, j * 128:(j + 1) * 128], rhs=wbs[cur][:, kc, :],
                                    start=(kc == 0), stop=(kc == FC - 1)),
                                    reads=[B_aT[a], B_wb[cur]], writes=[PB[bank]], mark=(kc == FC - 1))
                            op("act", lambda oi=oi, bank=bank: ACT.activation(out=otf[oi], in_=psb[bank][:, :], func=AF.Copy),
                               reads=[PB[bank]], writes=[B_ot[oi]])
                            dma(Ms[tt * 128:(tt + 1) * 128, nb * 512:(nb + 1) * 512], otf[oi], reads=[B_ot[oi]], writes=[B_Ms[tt]], q="act")

            def phaseKV():
                wbs, B_wb, otf, B_ot = gemm_setup()
                rr = [0, 0]

                def next_ot():
                    i = rr[1]
                    rr[1] = (i + 1) % 4
                    return i, otf[i], B_ot[i]

                load_wblock(wbs[0], B_wb[0], att_w_kv, KC, 0, 512, KVN)
                load_wblock(wbs[1], B_wb[1], att_w_kv, KC, 512, 512, KVN)

                def out_rows(tt):
                    if tt >= np_tok // 128:
                        return ("s", 0)
                    g0 = st * np_tok + tt * 128
                    if g0 >= NPT - 512:
                        return ("p", g0 - (NPT - 512))
                    return None

                if first:
                    i, o_, Bo = next_ot()
                    zt = o_.bitcast(BF16)[:, 0:512]
                    op("pool", lambda zt=zt: POOL.memset(zt, 0.0), writes=[Bo])
                    for h in range(4):
                        dma(KTs[h][:, 0:512], zt, reads=[Bo], writes=[B_KTs])
                        dma(VSH[h * 128:(h + 1) * 128, :], zt, reads=[Bo], writes=[B_VSH])
                for fb in range(4):
                    for (t0, n) in tblocks:
                        bank = rr[0] % 4
                        rr[0] += 1
                        mm_feat(bank, wbs[0], B_wb[0], fb * 128, 128, t0, n)
                        i, o_, Bo = next_ot()
                        o = o_.bitcast(BF16)
                        op("act", lambda o=o, bank=bank, n=n: ACT.activation(out=o[:, 0:n], in_=psb[bank][:, 0:n], func=AF.Copy),
                           reads=[PB[bank]], writes=[Bo])
                        if t0 < np_tok:
                            g0 = 512 + st * np_tok + t0
                            dma(KTs[fb][:, g0:g0 + n], o[:, 0:n], reads=[Bo], writes=[B_KTs], q="act")
                        else:
                            for si in range(NS):
                                dma(KTss[si, fb][:, 512:576], o[:, si * 64:(si + 1) * 64], reads=[Bo], writes=[B_KTs])
                import os
                if os.environ.get('DBG_KV') == '1':
                    return
                for tt in range(NT):
                    orow = out_rows(tt)
                    if orow is not None:
                        bank = rr[0] % 4
                        rr[0] += 1
                        for kc in range(KC):
                            op("pe", lambda kc=kc, tt=tt, bank=bank: PE.matmul(
                                psb[bank][:, :], lhsT=actT[:, kc, tt * 128:(tt + 1) * 128], rhs=wbs[0][:, kc, :],
                                start=(kc == 0), stop=(kc == KC - 1)),
                                reads=actbufs(tt) + [B_wb[0]], writes=[PB[bank]], mark=(kc == KC - 1))
                        i, o, Bo = next_ot()
                        op("act", lambda o=o, bank=bank: ACT.activation(out=o, in_=psb[bank][:, :], func=AF.Copy),
                           reads=[PB[bank]], writes=[Bo])
                        dst = o_k_p[orow[1]:orow[1] + 128, :] if orow[0] == "p" else o_k_s
                        dma(dst, o, reads=[Bo], is_out=True)
                    bank = rr[0] % 4
                    rr[0] += 1
                    for kc in range(KC):
                        op("pe", lambda kc=kc, tt=tt, bank=bank: PE.matmul(
                            psb[bank][:, :], lhsT=actT[:, kc, tt * 128:(tt + 1) * 128], rhs=wbs[1][:, kc, :],
                            start=(kc == 0), stop=(kc == KC - 1)),
                            reads=actbufs(tt) + [B_wb[1]], writes=[PB[bank]], mark=(kc == KC - 1))
                    vsrc, vB = psb[bank][:, :], PB[bank]
                    if orow is not None:
                        i, of, Bof = next_ot()
                        op("act", lambda of=of, bank=bank: ACT.activation(out=of, in_=psb[bank][:, :], func=AF.Copy),
                           reads=[PB[bank]], writes=[Bof])
                        dst = o_v_p[orow[1]:orow[1] + 128, :] if orow[0] == "p" else o_v_s
                        dma(dst, of, reads=[Bof], is_out=True)
                        vsrc, vB = of, Bof
                    i, o_, Bo = next_ot()
                    o = o_.bitcast(BF16)[:, 0:512]
                    op("dve", lambda o=o, vsrc=vsrc: DVE.tensor_copy(out=o, in_=vsrc), reads=[vB], writes=[Bo])
                    if tt < np_tok // 128:
                        r0 = 512 + st * np_tok + tt * 128
                        dma(VSH[r0:r0 + 128, :], o, reads=[Bo], writes=[B_VSH])
                    else:
                        for si in range(NS):
                            r0 = (8 + NPC + 9 * si) * 64 + 512
                            dma(VSH[r0:r0 + 64, :], o[si * 64:(si + 1) * 64, :], reads=[Bo], writes=[B_VSH])
                if os.environ.get('DBG_KV') == '2':
                    return
                if NS:
                    kb.fence()
                    cf = [cvf(SOFF + i * 512, 512) for i in range(2)]
                    cb_ = [cvb(SOFF + 1024 + i * 256, 512) for i in range(2)]
                    kt_ = [cvb(SOFF + 1536 + i * 256, 512).rearrange("p (h t) -> p h t", h=4) for i in range(2)]
                    B_cf, B_cb, B_kt = [Buf(), Buf()], [Buf(), Buf()], [Buf(), Buf()]
                    n_ = 0
                    for si in range(NS):
                        for rt in range(4):
                            for which in range(2):
                                b = n_ % 2
                                n_ += 1
                                src = (ck if which == 0 else cv)[si][rt * 128:(rt + 1) * 128, :]
                                dma(cf[b], src, writes=[B_cf[b]])
                                op("dve", lambda b=b: DVE.tensor_copy(out=cb_[b], in_=cf[b]), reads=[B_cf[b]], writes=[B_cb[b]])
                                if which == 1:
                                    r0 = (8 + NPC + 9 * si) * 64 + rt * 128
                                    dma(VSH[r0:r0 + 128, :], cb_[b], reads=[B_cb[b]], writes=[B_VSH])
                                else:
                                    pv = psbf(4 + b)
                                    for h in range(4):
                                        op("pe", lambda h=h, b=b, pv=pv: PE.transpose(out=pv[:, h * 128:(h + 1) * 128], in_=cb_[b][:, h * 128:(h + 1) * 128],
                                                                                      identity=ident[:]),
                                           reads=[B_cb[b], B_const], writes=[PB[4 + b]], mark=(h == 3))
                                    op("act", lambda b=b, pv=pv: ACT.activation(out=kt_[b], in_=pv[:, 0:512].rearrange("p (h t) -> p h t", h=4), func=AF.Copy),
                                       reads=[PB[4 + b]], writes=[B_kt[b]])
                                    dma(KTss[si][:, :, rt * 128:(rt + 1) * 128].rearrange("h p t -> p h t"), kt_[b], reads=[B_kt[b]], writes=[B_KTs])

            def phaseQ(l, j):
                wbs, B_wb, otf, B_ot = gemm_setup()
                gain = NG[:, (l * 4 + 0) * 16:(l * 4 + 0) * 16 + 16]
                Wq = att_w_q[j]
                load_wblock(wbs[0], B_wb[0], Wq, KC, 0, 512, gain)
                rr = 0
                for blk in range(4):
                    cur = blk % 2
                    if blk + 1 < 4:
                        load_wblock(wbs[1 - cur], B_wb[1 - cur], Wq, KC, (blk + 1) * 512, 512, gain)
                    for pl in range(2):
                        pair = blk * 2 + pl
                        for (t0, n) in tblocks:
                            oi = rr % 4
                            o = otf[oi].bitcast(BF16)[:, 0:2 * n].rearrange("p (c h t) -> p c h t", h=2, t=64)
                            for h2 in range(2):
                                bank = (2 * rr + h2) % 4
                                mm_feat(bank, wbs[cur], B_wb[cur], (pl * 2 + h2) * 128, 128, t0, n)
                                op("act", lambda o=o, bank=bank, n=n, h2=h2: ACT.activation(
                                    out=o[:, :, h2, :], in_=psb[bank][:, 0:n].rearrange("p (c t) -> p c t", t=64), func=AF.Copy,
                                    scale=float(128 ** -0.5)),
                                    reads=[PB[bank]], writes=[B_ot[oi]])
                            rr += 1
                            dma(QTs[pair][:, 2 * t0:2 * (t0 + n)], otf[oi].bitcast(BF16)[:, 0:2 * n], reads=[B_ot[oi]], writes=[B_QTs], q="act")

            def phaseATT(l, j):
                kb.fence()
                W = WOFF
                Bt = cvf(W + 0, 4608).rearrange("p (g k) -> p g k", g=8)
                qb = [cvb(W + 4608 + i * 2048, 4096).rearrange("p (g t) -> p g t", g=8) for i in range(2)]
                KTb = [cvb(W + 8704 + i * 1152, 2304).rearrange("p (h t) -> p h t", h=4) for i in range(2)]
                Vb = [cvb(W + 11008 + i * 2304, 4608)[0:64, :].rearrange("p (c n) -> p c n", c=9) for i in range(2)]
                sbf = [cvf(W + 15616 + i * 576, 576) for i in range(2)]
                pn = [cvb(W + 16768 + i * 288, 576) for i in range(2)]
                pTs = [cvb(W + 17344 + i * 576, 1152)[0:64, :].rearrange("p (c t) -> p c t", c=9) for i in range(2)]
                small = cvf(W + 18496, 64)
                rbt = cvf(SOFF, 513)[0:16, :]
                et = cvf(SOFF + 520, 640)[0:16, :]
                B_Bt, B_rbt, B_et = Buf(), Buf(), Buf()
                B_qb, B_KTb, B_Vb = [Buf(), Buf()], [Buf(), Buf()], [Buf(), Buf()]
                B_sbf, B_pn, B_pTs, B_sm = [Buf(), Buf()], [Buf(), Buf()], [Buf(), Buf()], [Buf(), Buf()]
                dma(rbt, att_rel_bias[j], writes=[B_rbt])
                op("dve", lambda: DVE.memset(et, 0.0), writes=[B_et])
                op("dve", lambda: DVE.tensor_copy(out=et[:, 0:319], in_=rbt[:, 512:513].broadcast_to([16, 319])),
                   reads=[B_rbt], writes=[B_et])
                op("dve", lambda: DVE.tensor_copy(out=et[:, 319:639], in_=rbt[:, 512:192:-1]), reads=[B_rbt], writes=[B_et])
                dma(EREL, et, reads=[B_et], writes=[B_EREL])
                for h in range(16):
                    dma(bass.AP(ZREL.tensor, h * 41024, [[641, 64], [1, 640]]), EREL[h].partition_broadcast(64),
                        reads=[B_EREL], writes=[B_ZREL])
                for h in range(16):
                    src = bass.AP(ZREL.tensor, h * 41024 + 63, [[640, 64], [1, 576]])
                    dma(Bt[(h % 2) * 64:(h % 2 + 1) * 64, h // 2, :], src, reads=[B_ZREL], writes=[B_Bt])
                cidx = 0
                pidx = 0
                qi = -1
                pendB = [None]
                qblk_of = {}
                for (c0, ncs, kind, sidx) in segs:
                    for c in range(c0, c0 + ncs):
                        qb0 = (c // 4) * 4
                        if kind == "s":
                            qb0 = segs[1][0]
                        if qb0 not in qblk_of:
                            qi += 1
                            nqc = min(4, (NPCH if kind == "p" else NPCH + NS) - qb0)
                            qblk_of[qb0] = qi % 2
                            dma(qb[qi % 2][:, :, 0:nqc * 128], QTs[:, :, qb0 * 128:(qb0 + nqc) * 128].rearrange("g p t -> p g t"),
                                reads=[B_QTs], writes=[B_qb[qi % 2]])
                        qsel = qblk_of[qb0]
                        cl = c - qb0
                        bi = cidx % 2
                        cidx += 1
                        if kind == "p":
                            gc = st * NPCH + c
                            j0 = max(0, 8 - gc) * 64
                            dma(KTb[bi], KTs[:, :, gc * 64:gc * 64 + 576].rearrange("h p t -> p h t"), reads=[B_KTs], writes=[B_KTb[bi]])
                            dma(Vb[bi], VSH[gc * 64:gc * 64 + 576, :].rearrange("(c p) n -> p c n", p=64), reads=[B_VSH], writes=[B_Vb[bi]])
                        else:
                            j0 = 0
                            r0 = (8 + NPC + 9 * sidx) * 64
                            dma(KTb[bi], KTss[sidx].rearrange("h p t -> p h t"), reads=[B_KTs], writes=[B_KTb[bi]])
                            dma(Vb[bi], VSH[r0:r0 + 576, :].rearrange("(c p) n -> p c n", p=64), reads=[B_VSH], writes=[B_Vb[bi]])
                        jb0 = j0 // 64
                        for p8 in range(8):
                            n_ = p8 // 2
                            pi = pidx % 2
                            pidx += 1
                            bA, bB = (0, 1) if pi == 0 else (2, 3)
                            bO = 6 + pi
                            lq = qb[qsel][:, p8, cl * 128:(cl + 1) * 128]
                            sm = small[:, pi * 8:(pi + 1) * 8]
                            if j0 < 512:
                                op("pe", lambda lq=lq, bA=bA, bi=bi, n_=n_, j0=j0: PE.matmul(psb[bA][:, j0:512], lhsT=lq, rhs=KTb[bi][:, n_, j0:512],
                                                                                            start=True, stop=True),
                                   reads=[B_qb[qsel], B_KTb[bi]], writes=[PB[bA]])
                            op("pe", lambda lq=lq, bB=bB, bi=bi, n_=n_: PE.matmul(psb[bB][:, 0:64], lhsT=lq, rhs=KTb[bi][:, n_, 512:576],
                                                                                 start=True, stop=True),
                               reads=[B_qb[qsel], B_KTb[bi]], writes=[PB[bB]])
                            if j0 < 512:
                                op("dve", lambda pi=pi, bA=bA, p8=p8, j0=j0: DVE.tensor_tensor(out=sbf[pi][:, j0:512], in0=psb[bA][:, j0:512],
                                                                                              in1=Bt[:, p8, j0:512], op=ALU.add),
                                   reads=[PB[bA], B_Bt], writes=[B_sbf[pi]])
                            op("dve", lambda pi=pi, bB=bB, p8=p8: DVE.tensor_tensor(out=sbf[pi][:, 512:576], in0=psb[bB][:, 0:64],
                                                                                   in1=Bt[:, p8, 512:576], op=ALU.add),
                               reads=[PB[bB], B_Bt], writes=[B_sbf[pi]])
                            op("dve", lambda pi=pi, sm=sm, j0=j0: DVE.reduce_max(out=sm[:, 0:1], in_=sbf[pi][:, j0:576], axis=AX.X),
                               reads=[B_sbf[pi]], writes=[B_sm[pi]])
                            op("dve", lambda sm=sm: DVE.tensor_scalar(out=sm[:, 1:2], in0=sm[:, 0:1], scalar1=-1.0, scalar2=None, op0=ALU.mult),
                               reads=[B_sm[pi]], writes=[B_sm[pi]])
                            op("act", lambda pi=pi, sm=sm, j0=j0: ACT.activation(out=sbf[pi][:, j0:576], in_=sbf[pi][:, j0:576], func=AF.Exp,
                                                                                bias=sm[:, 1:2], accum_out=sm[:, 2:3]),
                               reads=[B_sbf[pi], B_sm[pi]], writes=[B_sbf[pi], B_sm[pi]])
                            op("dve", lambda sm=sm: DVE.reciprocal(out=sm[:, 3:4], in_=sm[:, 2:3]), reads=[B_sm[pi]], writes=[B_sm[pi]])
                            op("dve", lambda pi=pi, sm=sm, j0=j0: DVE.tensor_scalar(out=pn[pi][:, j0:576], in0=sbf[pi][:, j0:576], scalar1=sm[:, 3:4],
                                                                                   scalar2=None, op0=ALU.mult),
                               reads=[B_sbf[pi], B_sm[pi]], writes=[B_pn[pi]])
                            def _stB(c=c, p8=p8, pi=pi, bi=bi, n_=n_, bO=bO, jb0=jb0):
                                pvA, pvB = psbf(4), psbf(5)
                                for jb in range(jb0, 9):
                                    dstp = pvA[0:64, jb * 128:(jb + 1) * 128] if jb < 8 else pvB[0:64, 0:128]
                                    bk = 4 if jb < 8 else 5
                                    op("pe", lambda pi=pi, jb=jb, dstp=dstp: PE.transpose(out=dstp, in_=pn[pi][:, jb * 64:(jb + 1) * 64], identity=ident[:]),
                                       reads=[B_pn[pi], B_const], writes=[PB[bk]], mark=(jb >= 7))
                                if jb0 < 8:
                                    op("act", lambda pi=pi, jb0=jb0: ACT.activation(out=pTs[pi][:, jb0:8, :],
                                                                                   in_=pvA[0:64, jb0 * 128:1024].rearrange("p (c t) -> p c t", t=128), func=AF.Copy),
                                       reads=[PB[4]], writes=[B_pTs[pi]])
                                op("act", lambda pi=pi: ACT.activation(out=pTs[pi][:, 8, :], in_=pvB[0:64, 0:128], func=AF.Copy),
                                   reads=[PB[5]], writes=[B_pTs[pi]])
                                for jb in range(jb0, 9):
                                    op("pe", lambda pi=pi, jb=jb, bi=bi, n_=n_, bO=bO: PE.matmul(psb[bO][:, 0:128], lhsT=Vb[bi][:, jb, n_ * 128:(n_ + 1) * 128],
                                                                                               rhs=pTs[pi][:, jb, :], start=(jb == jb0), stop=(jb == 8)),
                                       reads=[B_Vb[bi], B_pTs[pi]], writes=[PB[bO]], mark=(jb == 8))
                                op("act", lambda p8=p8, c=c, bO=bO: ACT.activation(out=actT[:, 2 * p8:2 * p8 + 2, c * 64:(c + 1) * 64],
                                                                                  in_=psb[bO][:, 0:128].rearrange("p (h t) -> p h t", h=2), func=AF.Copy),
                                   reads=[PB[bO]], writes=[B_actc[c]])
                            if pendB[0] is not None:
                                pendB[0]()
                            pendB[0] = _stB
                if pendB[0] is not None:
                    pendB[0]()

            Xrows = lambda tt: Xs[tt * 128:(tt + 1) * 128, :]
            BX = lambda tt: [B_Xs[tt]]
            resnorm(lambda tt: xin[xrow(tt):xrow(tt) + 128, :], lambda tt: [], None, None, Xrows, BX, True)
            if dbg is not None and dbg[0] == "h0":
                return "stop"
            for l in range(n_layers):
                if l < 2:
                    phaseA(l)
                    if dbg is not None and dbg[0] == "projA" and l == DL:
                        return "stop"
                    phaseB(l)
                    if dbg is not None and dbg[0] == "scan" and l == DL:
                        return "stop"
                    phaseC(gla_w_o[l], HNx[:, l, :, :].rearrange("p h v -> p (h v)"))
                else:
                    j = l - 2
                    if l == 2:
                        phaseKV()
                    if dbg is not None and dbg[0] == "kv":
                        return "stop"
                    phaseQ(l, j)
                    if dbg is not None and dbg[0] == "q":
                        return "stop"
                    phaseATT(l, j)
                    if dbg is not None and dbg[0] == "att":
                        return "stop"
                    phaseC(att_w_o[j], None)
                resnorm(Xrows, BX, lambda tt: Ms[tt * 128:(tt + 1) * 128, :], norm_gains[l, 1], Xrows, BX, True)
                if dbg is not None and dbg[0] == "mix" and l == DL:
                    return "stop"
                phaseE(l)
                phaseF(l)
                fin = (l == n_layers - 1)
                if fin:
                    resnorm(Xrows, BX, lambda tt: Ms[tt * 128:(tt + 1) * 128, :], norm_gains[l, 3],
                            lambda tt: yout[xrow(tt):xrow(tt) + 128, :], lambda tt: [], False, is_out=True)
                else:
                    resnorm(Xrows, BX, lambda tt: Ms[tt * 128:(tt + 1) * 128, :], norm_gains[l, 3], Xrows, BX, True)
            return None

        for st in range(n_st):
            r = run_supertile(st)
            if r == "stop":
                break
        if dbg is not None:
            kb.fence()
            Bt = Buf()
            nt_dbg = dbg[1][0] // 128
            if dbg[0] == "projA":
                tb = cvb(WOFF, 6144)
                for tt in range(nt_dbg):
                    dma(tb, PROJ[tt * 128:(tt + 1) * 128, :], reads=[B_PROJ[tt]], writes=[Bt])
                    dma(dbg_out[tt * 128:(tt + 1) * 128, :], tb, reads=[Bt], is_out=True)
            elif dbg[0] in ("h0", "scan", "att"):
                dma(dbg_out.rearrange("k p t -> p k t"), actT[:, :, 0:dbg[1][2]], reads=B_actc, is_out=True)
            elif dbg[0] == "mix":
                tb = cvf(WOFF, 2048)
                for tt in range(nt_dbg):
                    dma(tb, Xs[tt * 128:(tt + 1) * 128, :], reads=[B_Xs[tt]], writes=[Bt])
                    dma(dbg_out[tt * 128:(tt + 1) * 128, :], tb, reads=[Bt], is_out=True)
        kb.finish()
    return nc


WEIGHT_KEYS = ["norm_gains", "gla_w_in", "gla_w_gate", "gla_b_gate", "gla_head_norm", "gla_w_o", "kv_norm",
               "att_w_kv", "att_w_q", "att_rel_bias", "att_w_o", "ffn_w_up", "ffn_conv_w", "ffn_conv_b", "ffn_w_down"]


def make_in_maps(inp, npt):
    f = lambda a: np.ascontiguousarray(np.asarray(a, dtype=np.float32))
    w = {k: f(inp[k]) for k in WEIGHT_KEYS}
    xp, xs = f(inp["x_prompt"]), f(inp["x_sample"])
    sg, sc = f(inp["state_gla"]), f(inp["state_ffn_conv"])
    ck, cv = f(inp["cache_k"]), f(inp["cache_v"])
    maps = []
    for c in range(N_CORES):
        xin = np.zeros((npt + 128, D), np.float32)
        if c < xp.shape[0]:
            xin[:npt] = xp[c, :npt]
        xin[npt:] = xs[2 * c:2 * c + 2].reshape(128, D)
        m = dict(w)
        m["xin"] = xin
        m["gla_s0"] = np.ascontiguousarray(sg[:, 2 * c:2 * c + 2])
        m["conv_s0"] = np.ascontiguousarray(sc[:, 2 * c:2 * c + 2])
        m["ck"] = np.ascontiguousarray(ck[2 * c:2 * c + 2].reshape(2, 512, 512))
        m["cv"] = np.ascontiguousarray(cv[2 * c:2 * c + 2].reshape(2, 512, 512))
        maps.append(m)
    return maps


_NC_CACHE = {}


def kernel(**inputs):
    if "full" not in _NC_CACHE:
        _NC_CACHE["full"] = build_program()
    nc = _NC_CACHE["full"]
    maps = make_in_maps(inputs, SEQ)
    res = run_bass_kernel_spmd(nc, maps, core_ids=list(range(N_CORES))).results
    B = 4
    y_p = np.stack([res[c]["yout"][:SEQ] for c in range(B)])
    y_s = np.concatenate([res[c]["yout"][SEQ:].reshape(2, 64, D) for c in range(N_CORES)])
    gla_p = np.stack([res[c]["o_gla_p"] for c in range(B)], axis=1)
    gla_s = np.concatenate([res[c]["o_gla_s"] for c in range(N_CORES)], axis=1)
    conv_p = np.stack([res[c]["o_conv_p"] for c in range(B)], axis=1)
    conv_s = np.concatenate([res[c]["o_conv_s"] for c in range(N_CORES)], axis=1)
    k_p = np.stack([res[c]["o_k_p"].reshape(512, 4, 128) for c in range(B)])
    v_p = np.stack([res[c]["o_v_p"].reshape(512, 4, 128) for c in range(B)])
    k_s = np.concatenate([res[c]["o_k_s"].reshape(2, 64, 4, 128) for c in range(N_CORES)])
    v_s = np.concatenate([res[c]["o_v_s"].reshape(2, 64, 4, 128) for c in range(N_CORES)])
    outs = (y_p, y_s, gla_p, gla_s, conv_p, conv_s, k_p, v_p, k_s, v_s)
    return tuple(np.ascontiguousarray(o, dtype=np.float32) for o in outs)
```
